# Optimizing a Trainium2 kernel written in Bass

```python
import math
import jax, jax.numpy as jnp
from jax import lax
import numpy as np

D_MODEL = 1024
BATCH = 4
SEQ = 4096
DEPTH = 2
DEC_BATCH = 32
DEC_SEQ = 4
PAST_LEN = 16384
PAGE_SIZE = 128

N_HEADS_A = 8
HEAD_DIM_A = 64
D_A = N_HEADS_A * HEAD_DIM_A
ROT_DIM = HEAD_DIM_A // 4
ROPE_THETA = 500000.0
DILATED_PATTERNS = ((128, 1), (512, 4), (2048, 16))
MAX_WINDOW = 2048
ATTN_BLOCK = 32
N_HEADS_B = 4
DV_B = D_MODEL // 2
DK_B = DV_B // 2
DV_HEAD = DV_B // N_HEADS_B
DK_HEAD = DK_B // N_HEADS_B
GATE_RANK = 16
GATE_TAU = 16.0
GLA_CHUNK = 64
D_MIX = D_A + DV_B
SPLITS = (D_A, 2 * D_A, 3 * D_A, 3 * D_A + DK_B, 3 * D_A + 2 * DK_B,
          3 * D_A + 2 * DK_B + DV_B, 3 * D_A + 2 * DK_B + 2 * DV_B)
IN_COLS = 3 * D_A + 2 * DK_B + 2 * DV_B + GATE_RANK
D_FF = 2816
CONV_W = 3
EPS = 1e-6

kernel_name = "hymba_longnet_gla_convffn_step"


def rms_norm(x, g):
    xf = x.astype(jnp.float32)
    y = xf * lax.rsqrt(jnp.mean(xf * xf, axis=-1, keepdims=True) + EPS)
    return (y * g.astype(jnp.float32)).astype(x.dtype)


def partial_rope(x, pos):
    half = ROT_DIM // 2
    inv_freq = ROPE_THETA ** (-jnp.arange(half, dtype=jnp.float32) / half)
    ang = pos.astype(jnp.float32)[:, None] * inv_freq[None, :]
    cos = jnp.cos(ang)[None, :, None, :]
    sin = jnp.sin(ang)[None, :, None, :]
    x1 = x[..., :half].astype(jnp.float32)
    x2 = x[..., half:ROT_DIM].astype(jnp.float32)
    rot = jnp.concatenate([x1 * cos - x2 * sin, x2 * cos + x1 * sin], axis=-1).astype(x.dtype)
    return jnp.concatenate([rot, x[..., ROT_DIM:]], axis=-1)


def dilated_window_attention(q, k_all, v_all, offset):
    B, T, H, hd = q.shape
    blk = math.gcd(T, ATTN_BLOCK)
    nblk = T // blk
    scale = hd ** -0.5
    q_blocks = q.reshape(B, nblk, blk, H, hd).swapaxes(0, 1)

    def one_block(args):
        q_blk, b = args
        rows = offset + b * blk + jnp.arange(blk, dtype=jnp.int32)
        outs, lses = [], []
        for window, dil in DILATED_PATTERNS:
            idx = rows[:, None] - dil * jnp.arange(window // dil + 1, dtype=jnp.int32)[None, :]
            valid = idx >= 0
            idx = jnp.maximum(idx, 0)
            k_g = k_all[:, idx]
            v_g = v_all[:, idx]
            s = jnp.einsum('bqhd,bqnhd->bhqn', q_blk, k_g,
                           preferred_element_type=jnp.float32) * scale
            s = jnp.where(valid[None, None], s, -jnp.inf)
            lse = jax.nn.logsumexp(s, axis=-1)
            p = jnp.exp(s - lse[..., None])
            outs.append(jnp.einsum('bhqn,bqnhd->bqhd', p, v_g.astype(jnp.float32)))
            lses.append(lse)
        w = jax.nn.softmax(jnp.stack(lses), axis=0)
        w = w.transpose(0, 1, 3, 2)[..., None]
        return jnp.sum(w * jnp.stack(outs), axis=0).astype(q.dtype)

    out = lax.map(one_block, (q_blocks, jnp.arange(nblk, dtype=jnp.int32)))
    return out.swapaxes(0, 1).reshape(B, T, H, hd)


def gla_chunked(q, k, v, log_a, s0):
    B, T, H, dk = q.shape
    dv = v.shape[-1]
    c = math.gcd(T, GLA_CHUNK)
    n = T // c

    def to_chunks(x):
        return x.reshape(B, n, c, H, x.shape[-1]).transpose(1, 0, 3, 2, 4)

    causal = jnp.tril(jnp.ones((c, c), dtype=bool))[None, None, :, :, None]

    def step(S, inp):
        qi, ki, vi, ai = inp
        qf = qi.astype(jnp.float32)
        kf = ki.astype(jnp.float32)
        vf = vi.astype(jnp.float32)
        b = jnp.cumsum(ai.astype(jnp.float32), axis=2)
        diff = b[:, :, :, None, :] - b[:, :, None, :, :]
        decay = jnp.exp(jnp.where(causal, diff, -jnp.inf))
        attn = jnp.einsum('bhtk,bhsk,bhtsk->bhts', qf, kf, decay)
        o = (jnp.einsum('bhts,bhsv->bhtv', attn, vf)
             + jnp.einsum('bhtk,bhkv->bhtv', qf * jnp.exp(b), S))
        b_last = b[:, :, -1:, :]
        S_new = (jnp.exp(b_last[:, :, 0, :])[..., None] * S
                 + jnp.einsum('bhsk,bhsv->bhkv', kf * jnp.exp(b_last - b), vf))
        return S_new, o

    S_fin, o = lax.scan(step, s0.astype(jnp.float32),
                        (to_chunks(q), to_chunks(k), to_chunks(v), to_chunks(log_a)))
    o = o.transpose(1, 0, 3, 2, 4).reshape(B, T, H, dv)
    return o.astype(v.dtype), S_fin.astype(v.dtype)


def token_mixers(h, pos, k_buf, v_buf, s0, w_in, w_gate2, b_gate, g_gla, w_out):
    B, T, _ = h.shape
    proj = h @ w_in
    q_a, k_a, v_a, q_b, k_b, v_b, r_b, gate_lr = jnp.split(proj, SPLITS, axis=-1)
    q_a = partial_rope(q_a.reshape(B, T, N_HEADS_A, HEAD_DIM_A), pos)
    k_a = partial_rope(k_a.reshape(B, T, N_HEADS_A, HEAD_DIM_A), pos)
    v_a = v_a.reshape(B, T, N_HEADS_A, HEAD_DIM_A)
    k_all = jnp.concatenate([k_buf, k_a], axis=1)
    v_all = jnp.concatenate([v_buf, v_a], axis=1)
    o_a = dilated_window_attention(q_a, k_all, v_all, k_buf.shape[1]).reshape(B, T, D_A)
    log_a = jax.nn.log_sigmoid((gate_lr @ w_gate2 + b_gate).astype(jnp.float32)) / GATE_TAU
    q_b = q_b.reshape(B, T, N_HEADS_B, DK_HEAD) * (DK_HEAD ** -0.5)
    k_b = k_b.reshape(B, T, N_HEADS_B, DK_HEAD)
    v_b = v_b.reshape(B, T, N_HEADS_B, DV_HEAD)
    o_b, s_new = gla_chunked(q_b, k_b, v_b, log_a.reshape(B, T, N_HEADS_B, DK_HEAD), s0)
    o_b = rms_norm(o_b, g_gla) * jax.nn.silu(r_b.reshape(B, T, N_HEADS_B, DV_HEAD))
    o = jnp.concatenate([o_a, o_b.reshape(B, T, DV_B)], axis=-1)
    return o @ w_out, k_a, v_a, s_new


def conv_ffn(h, conv_buf, w_up, conv_w, conv_b, w_down):
    T = h.shape[1]
    u = h @ w_up
    ext = jnp.concatenate([conv_buf, u], axis=1)
    c = conv_b + conv_w[0] * ext[:, 0:T]
    for i in range(1, CONV_W):
        c = c + conv_w[i] * ext[:, i:i + T]
    gate, up = jnp.split(c, 2, axis=-1)
    y = jax.nn.gelu(gate, approximate=True) * up
    return y @ w_down, ext[:, T:]


def run_trunk(x, pos, k_bufs, v_bufs, gla_states, conv_bufs, params):
    (g_mix_pre, g_mix_post, g_ffn_pre, g_ffn_post, w_in, w_gate2, b_gate, g_gla, w_out,
     w_up, conv_w, conv_b, w_down) = params
    new_k, new_v, new_s, new_c = [], [], [], []
    for l in range(DEPTH):
        h = rms_norm(x, g_mix_pre[l])
        m, k_rows, v_rows, s_l = token_mixers(h, pos, k_bufs[l], v_bufs[l], gla_states[l],
                                              w_in[l], w_gate2[l], b_gate[l], g_gla[l], w_out[l])
        x = x + rms_norm(m, g_mix_post[l])
        h = rms_norm(x, g_ffn_pre[l])
        f, c_l = conv_ffn(h, conv_bufs[l], w_up[l], conv_w[l], conv_b[l], w_down[l])
        x = x + rms_norm(f, g_ffn_post[l])
        new_k.append(k_rows)
        new_v.append(v_rows)
        new_s.append(s_l)
        new_c.append(c_l)
    return x, jnp.stack(new_k), jnp.stack(new_v), jnp.stack(new_s), jnp.stack(new_c)


def setup_inputs(seed: int = 0) -> dict:
    key = jax.random.key(seed)
    ks = jax.random.split(key, 19)
    win_past = min(MAX_WINDOW, PAST_LEN)

    def nrm(k, shape, scale):
        return jax.random.normal(k, shape, jnp.float32) * scale

    return {
        "x_prompt": nrm(ks[0], (BATCH, SEQ, D_MODEL), 1.0),
        "x_sample": nrm(ks[1], (DEC_BATCH, DEC_SEQ, D_MODEL), 1.0),
        "cache_k_win": nrm(ks[2], (DEPTH, DEC_BATCH, win_past, N_HEADS_A, HEAD_DIM_A), 1.0),
        "cache_v_win": nrm(ks[3], (DEPTH, DEC_BATCH, win_past, N_HEADS_A, HEAD_DIM_A), 1.0),
        "state_gla": nrm(ks[4], (DEPTH, DEC_BATCH, N_HEADS_B, DK_HEAD, DV_HEAD), 1.0),
        "state_ffn_conv": nrm(ks[5], (DEPTH, DEC_BATCH, CONV_W - 1, 2 * D_FF), 1.0),
        "g_mix_pre": 1.0 + nrm(ks[6], (DEPTH, D_MODEL), 0.02),
        "g_mix_post": 1.0 + nrm(ks[7], (DEPTH, D_MODEL), 0.02),
        "g_ffn_pre": 1.0 + nrm(ks[8], (DEPTH, D_MODEL), 0.02),
        "g_ffn_post": 1.0 + nrm(ks[9], (DEPTH, D_MODEL), 0.02),
        "w_in": nrm(ks[10], (DEPTH, D_MODEL, IN_COLS), D_MODEL ** -0.5),
        "w_gate2": nrm(ks[11], (DEPTH, GATE_RANK, DK_B), GATE_RANK ** -0.5),
        "b_gate": nrm(ks[12], (DEPTH, DK_B), 0.01),
        "g_gla": 1.0 + nrm(ks[13], (DEPTH, DV_HEAD), 0.02),
        "w_out": nrm(ks[14], (DEPTH, D_MIX, D_MODEL), D_MIX ** -0.5),
        "w_up": nrm(ks[15], (DEPTH, D_MODEL, 2 * D_FF), D_MODEL ** -0.5),
        "conv_w": nrm(ks[16], (DEPTH, CONV_W, 2 * D_FF), CONV_W ** -0.5),
        "conv_b": nrm(ks[17], (DEPTH, 2 * D_FF), 0.01),
        "w_down": nrm(ks[18], (DEPTH, D_FF, D_MODEL), D_FF ** -0.5),
    }


def reference(x_prompt, x_sample, cache_k_win, cache_v_win, state_gla, state_ffn_conv,
              g_mix_pre, g_mix_post, g_ffn_pre, g_ffn_post, w_in, w_gate2, b_gate, g_gla,
              w_out, w_up, conv_w, conv_b, w_down):
    params = (g_mix_pre, g_mix_post, g_ffn_pre, g_ffn_post, w_in, w_gate2, b_gate, g_gla,
              w_out, w_up, conv_w, conv_b, w_down)
    B, T, _ = x_prompt.shape
    dt = x_prompt.dtype
    pos_p = jnp.arange(T, dtype=jnp.int32)
    z_kv = jnp.zeros((DEPTH, B, 0, N_HEADS_A, HEAD_DIM_A), dt)
    z_s = jnp.zeros((DEPTH, B, N_HEADS_B, DK_HEAD, DV_HEAD), dt)
    z_c = jnp.zeros((DEPTH, B, CONV_W - 1, 2 * D_FF), dt)
    y_prompt, k_p, v_p, s_p, c_p = run_trunk(x_prompt, pos_p, z_kv, z_kv, z_s, z_c, params)
    win_p = min(MAX_WINDOW, T)
    new_k_win_prompt = k_p[:, :, T - win_p:]
    new_v_win_prompt = v_p[:, :, T - win_p:]
    pos_s = PAST_LEN + jnp.arange(x_sample.shape[1], dtype=jnp.int32)
    y_sample, k_s, v_s, s_s, c_s = run_trunk(x_sample, pos_s, cache_k_win, cache_v_win,
                                             state_gla, state_ffn_conv, params)
    return (y_prompt, y_sample, new_k_win_prompt, new_v_win_prompt, s_p, c_p, k_s, v_s, s_s, c_s)
```

```python
import numpy as np
import ml_dtypes
import concourse.bass as bass
import concourse.mybir as mybir
from concourse.bass_utils import run_bass_kernel_spmd

F32, BF16 = mybir.dt.float32, mybir.dt.bfloat16
AF = mybir.ActivationFunctionType
ALU = mybir.AluOpType
AX = mybir.AxisListType

D = 1024
TL = 2048
NT = TL // 128
NS = 16
DEPTH = 2
DFF = 2816
NBLK = DFF // 128
INC = 3088
EPS = 1e-6
PAST = 16384
NEG = -30000.0
DEBUG_LAYERS = None


class Prog:
    ENGS = ("pe", "act", "dve", "pool", "sp")

    def __init__(self, nc):
        self.nc = nc
        self.q = {e: [] for e in self.ENGS}
        self.cnt = {e: 0 for e in self.ENGS}
        self.sem = {e: nc.alloc_semaphore("c_" + e) for e in ("pe", "act", "dve", "pool")}
        self.dsem = {}
        self.seen = {e: {} for e in self.ENGS}
        self.lastw = {}
        self.readers = {}
        self.n_wait = 0
        self.log = None
        self.ever = set()
        self.unknown = set()
        self.rec = None
        self.pe_last = None
        self.pend = []
        self.background = set()
        self.needed = {e: set() for e in ("pe", "act", "dve", "pool")}
        self.remap = {}

    def _dsem(self, name):
        if name not in self.dsem:
            self.dsem[name] = [self.nc.alloc_semaphore("d_" + name), 0]
        return self.dsem[name]

    def _handle(self, tok):
        return self.sem[tok[1]] if tok[0] == "e" else self.dsem[tok[1]][0]

    def _wait(self, eng, tok):
        key = (tok[0], tok[1])
        if self.seen[eng].get(key, 0) >= tok[2]:
            return
        self.seen[eng][key] = tok[2]
        if self.log is not None:
            self.log.append(f"    {eng} WAIT {tok}")
        if tok[0] == "e":
            self.needed[tok[1]].add(tok[2])
        self.pend.append(tok)
        self.n_wait += 1

    def _flush_waits(self, eng, keep_last):
        pend, self.pend = self.pend, []
        last = pend.pop() if (keep_last and pend) else None
        for tok in pend:
            self.q[eng].append(lambda e, tok=tok: e.wait_ge(*self._resolve(tok)))
        return last

    def _resolve(self, tok):
        if tok[0] == "e":
            return self.sem[tok[1]], self.remap[tok[1]][tok[2]]
        return self.dsem[tok[1]][0], tok[2]

    def _deps(self, eng, reads, writes, is_dma):
        toks = []
        for r in reads:
            if r not in self.ever:
                self.unknown.add(r)
            t = self.lastw.get(r)
            if t is not None:
                toks.append(t)
        for w in writes:
            t = self.lastw.get(w)
            if t is not None and (is_dma or not (t[0] == "e" and t[1] == eng)):
                toks.append(t)
            for t in self.readers.get(w, ()):
                if is_dma or not (t[0] == "e" and t[1] == eng):
                    toks.append(t)
        best = {}
        for t in toks:
            k = (t[0], t[1])
            if best.get(k, 0) < t[2]:
                best[k] = t[2]
        for k, v in best.items():
            self._wait(eng, (k[0], k[1], v))

    def _commit(self, tok, reads, writes):
        self.ever.update(writes)
        for w in writes:
            self.lastw[w] = tok
            self.readers[w] = []
        for r in reads:
            if r not in writes:
                lst = self.readers.setdefault(r, [])
                for i_, t in enumerate(lst):
                    if t[0] == tok[0] and t[1] == tok[1]:
                        if t[2] < tok[2]:
                            lst[i_] = tok
                        break
                else:
                    lst.append(tok)

    def begin_rec(self):
        self.rec = []

    def end_rec(self):
        r, self.rec = self.rec, None
        return r

    def play(self, *lists):
        lists = [l for l in lists if l]
        idx = [0] * len(lists)
        total = sum(len(l) for l in lists)
        for _ in range(total):
            best, bi = None, -1
            for li, l in enumerate(lists):
                if idx[li] < len(l):
                    frac = (idx[li] + 0.5) / len(l)
                    if best is None or frac < best:
                        best, bi = frac, li
            kind, args, kw = lists[bi][idx[bi]]
            idx[bi] += 1
            getattr(self, kind)(*args, **kw)

    def op(self, eng, fn, reads=(), writes=(), pe=None, simple=False):
        if self.rec is not None:
            self.rec.append(("op", (eng, fn, reads, writes), {"pe": pe, "simple": simple}))
            return
        if pe is not None:
            g = set(range(pe[0] // 32, (pe[0] + pe[1] - 1) // 32 + 1))
            b = set(k for k in writes if k.startswith("ps"))
            if self.pe_last is not None and not (g & self.pe_last[0]) and (b & self.pe_last[1]):
                raise RuntimeError(f"PE row-group hazard: disjoint row groups {g}/{self.pe_last[0]} share PSUM bank {b}")
            self.pe_last = (g, b)
        self._deps(eng, reads, writes, False)
        last = self._flush_waits(eng, ATTACH_WAITS and simple)
        self.cnt[eng] += 1
        if self.log is not None:
            self.log.append(f"{eng} #{self.cnt[eng]} r={list(reads)} w={list(writes)}")
        sem = self.sem[eng]
        idx = self.cnt[eng]

        def emit_(e, fn=fn, sem=sem, last=last, eng=eng, idx=idx):
            ins = fn(e)
            if last is not None:
                h, v = self._resolve(last)
                ins.wait_op(h, v, "sem-ge")
            if idx in self.remap[eng]:
                ins.then_inc(sem, 1)
        self.q[eng].append(emit_)
        self._commit(("e", eng, self.cnt[eng]), reads, writes)

    def dma(self, eng, out, in_, reads=(), writes=(), sem="misc", **kw):
        if self.rec is not None:
            self.rec.append(("dma", (eng, out, in_, reads, writes, sem), kw))
            return
        self._deps(eng, reads, writes, True)
        self._flush_waits(eng, False)
        ds = self._dsem(sem)
        ds[1] += 16
        h = ds[0]
        self.q[eng].append(lambda e, out=out, in_=in_, h=h, kw=kw: e.dma_start(out=out, in_=in_, **kw).then_inc(h, 16))
        self._commit(("d", sem, ds[1]), reads, writes)

    def collective(self, ins, outs, groups, reads=(), writes=()):
        self._deps("pool", reads, writes, True)
        self._flush_waits("pool", False)
        ds = self._dsem("cc")
        ds[1] += 1
        h = ds[0]
        self.q["pool"].append(lambda e, ins=ins, outs=outs, h=h: e.collective_compute(
            "AllGather", ALU.bypass, replica_groups=groups, ins=ins, outs=outs).then_inc(h, 1))
        self._commit(("d", "cc", ds[1]), reads, writes)

    def barrier(self, final=False):
        toks = [("e", e, self.cnt[e]) for e in self.sem if self.cnt[e] > 0]
        toks += [("d", n, v[1]) for n, v in self.dsem.items() if v[1] > 0 and (final or n not in self.background)]
        for eng in self.ENGS:
            for t in toks:
                if not (t[0] == "e" and t[1] == eng):
                    self._wait(eng, t)
            self._flush_waits(eng, False)
        keep = {k: t for k, t in self.lastw.items() if t[0] == "d" and t[1] in self.background} if not final else {}
        self.lastw.clear()
        self.lastw.update(keep)
        self.readers.clear()

    def emit(self):
        nc = self.nc
        self.remap = {e: {v: r + 1 for r, v in enumerate(sorted(ix))} for e, ix in self.needed.items()}
        with nc.Block() as block:
            @block.tensor
            def _(e):
                for f in self.q["pe"]:
                    f(e)

            @block.scalar
            def _(e):
                for f in self.q["act"]:
                    f(e)

            @block.vector
            def _(e):
                for f in self.q["dve"]:
                    f(e)

            @block.gpsimd
            def _(e):
                for f in self.q["pool"]:
                    f(e)

            @block.sync
            def _(e):
                for f in self.q["sp"]:
                    f(e)


DBG_T = {}
ATTACH_WAITS = True
PLOG = []
LOG_ON = False


class Arena:
    def __init__(self, nc, base, limit):
        self.nc, self.base, self.limit, self.off, self.n = nc, base, limit, base, 0
        self.offs = {}

    def reset(self, off=None):
        self.off = self.base if off is None else off

    def alloc(self, name, shape, dtype):
        esz = 4 if dtype == F32 else 2
        size = esz
        for s in shape[1:]:
            size *= s
        size = (size + 63) // 64 * 64
        assert self.off + size <= self.limit, (name, self.off, size, self.limit)
        self.n += 1
        t = self.nc.alloc_sbuf_tensor_at(f"{name}_{self.n}", list(shape), dtype, offset=self.off)
        self.off += size
        DBG_T[name] = t.name
        self.offs[t.name] = self.off - size
        return t

    def alias(self, name, shape, dtype, base, byte_off=0):
        self.n += 1
        return self.nc.alloc_sbuf_tensor_at(f"{name}_{self.n}", list(shape), dtype, offset=self.offs[base.name] + byte_off)


def _rope_tables(pos):
    half = 8
    inv = (np.float32(500000.0) ** (-(np.arange(half, dtype=np.float32) / np.float32(half)))).astype(np.float32)
    ang = pos.astype(np.float32)[:, None] * inv[None, :]
    c, s = np.cos(ang).astype(np.float32), np.sin(ang).astype(np.float32)
    k = np.concatenate([c, c, s, s], axis=1)
    return (k * np.float32(0.125)).astype(np.float32), k.astype(np.float32)


def _consts(half):
    bf = ml_dtypes.bfloat16
    c = {}
    c["identb"] = np.eye(128, dtype=np.float32).astype(bf)
    c["identf"] = np.eye(128, dtype=np.float32)
    k = np.arange(128)[:, None]
    q = np.arange(128)[None, :]
    c["ucs"] = (k <= q).astype(np.float32)
    c["onesf"] = np.ones((128, 128), np.float32)
    c["onesb"] = np.ones((128, 64), np.float32).astype(bf)
    mdiag = np.where(k <= q, 0.0, NEG).astype(np.float32)
    mprev = np.where(k >= q, 0.0, NEG).astype(np.float32)
    mpre = mprev if half == 1 else np.full((128, 128), NEG, np.float32)
    c["ma"] = (np.concatenate([mprev, mdiag], axis=1) == 0.0).astype(np.float32).astype(bf)
    c["mb"] = (np.concatenate([mpre, mdiag], axis=1) == 0.0).astype(np.float32).astype(bf)
    pos = half * TL + np.arange(TL)
    rq, rk = _rope_tables(pos)
    c["ropeq"] = np.ascontiguousarray(rq.reshape(NT, 128, 32).transpose(1, 0, 2))
    c["ropek"] = np.ascontiguousarray(rk.reshape(NT, 128, 32).transpose(1, 0, 2))
    sq, sk = _rope_tables(PAST + (np.arange(NS) % 4))
    c["ropesq"], c["ropesk"] = sq, sk
    c["flag"] = np.full((128, 1), float(half), np.float32)
    sm = np.full((128, 4, 9, 16), NEG, np.float32)
    m = np.arange(128)
    for j in range(4):
        for i in range(4):
            qq = 4 * j + i
            sm[m >= i, j, 0, qq] = 0.0
            sm[:, j, 1 + i, qq] = 0.0
            sm[:, j, 5 + i, qq] = 0.0
    c["smask"] = sm.reshape(128, 4 * 9 * 16).astype(bf)
    t = np.arange(16)
    same = (t[:, None] // 4) == (t[None, :] // 4)
    c["mnew"] = (same * ((t[:, None] < t[None, :]) * 1.0 + (t[:, None] == t[None, :]) * 3.0)).astype(np.float32)
    c["ucs_s"] = (same & (t[:, None] <= t[None, :])).astype(np.float32)
    c["seqsel"] = ((t[:, None] // 4) == np.arange(4)[None, :]).astype(np.float32)
    sc = np.zeros((128, 4, 16), np.float32)
    for j in range(4):
        sc[:, j, 4 * j:4 * j + 4] = 1.0
    c["seqcol"] = sc.reshape(128, 64).astype(bf)
    return c


CONST_SPECS = [
    ("identb", [128, 128], BF16), ("identf", [128, 128], F32), ("ucs", [128, 128], F32),
    ("onesf", [128, 128], F32), ("onesb", [128, 64], BF16), ("ma", [128, 256], BF16), ("mb", [128, 256], BF16),
    ("ropeq", [128, NT, 32], F32), ("ropek", [128, NT, 32], F32), ("ropesq", [NS, 32], F32), ("ropesk", [NS, 32], F32),
    ("flag", [128, 1], F32), ("smask", [128, 576], BF16), ("mnew", [16, 16], F32), ("ucs_s", [16, 16], F32),
    ("seqsel", [16, 4], F32), ("seqcol", [128, 64], BF16),
]


def build_program(n_layers=DEPTH, dbg=False, ncores=8, skip=()):
    nc = bass.Bass("TRN2", target_bir_lowering=False)
    P = Prog(nc)
    if dbg and LOG_ON:
        P.log = PLOG

    def din(name, shape, dt=F32):
        return nc.dram_tensor(name, list(shape), dt, kind="ExternalInput").ap()

    def dout(name, shape, dt=F32):
        return nc.dram_tensor(name, list(shape), dt, kind="ExternalOutput").ap()

    xp = din("xp", [TL, D]); xs = din("xs", [NS, D])
    ck = din("ck", [DEPTH, 4, 2048, 512]); cv = din("cv", [DEPTH, 4, 2048, 512])
    sg_in = din("sg", [DEPTH, 4, 4, 64, 128]); sc_in = din("sc", [DEPTH, 4, 2, 2 * DFF])
    g_mix_pre = din("g_mix_pre", [DEPTH, D]); g_mix_post = din("g_mix_post", [DEPTH, D])
    g_ffn_pre = din("g_ffn_pre", [DEPTH, D]); g_ffn_post = din("g_ffn_post", [DEPTH, D])
    w_in = din("w_in", [DEPTH, D, INC]); w_gate2 = din("w_gate2", [DEPTH, 16, 256]); b_gate = din("b_gate", [DEPTH, 256])
    g_gla = din("g_gla", [DEPTH, 128]); w_out = din("w_out", [DEPTH, D, D]); w_up = din("w_up", [DEPTH, D, 2 * DFF])
    conv_w = din("conv_w", [DEPTH, 3, 2 * DFF]); conv_b = din("conv_b", [DEPTH, 2 * DFF]); w_down = din("w_down", [DEPTH, DFF, D])
    cin = {n: din("c_" + n, s, d) for n, s, d in CONST_SPECS}

    yp = dout("yp", [TL, D]); ys = dout("ys", [NS, D])
    kp = dout("kp", [DEPTH, TL, 512]); vp = dout("vp", [DEPTH, TL, 512])
    spo = dout("spo", [DEPTH, 4, 64, 128]); cpo = dout("cpo", [DEPTH, 2, 2 * DFF])
    kso = dout("kso", [DEPTH, NS, 512]); vso = dout("vso", [DEPTH, NS, 512])
    sso = dout("sso", [DEPTH, 4, 4, 64, 128]); cso = dout("cso", [DEPTH, 4, 2, 2 * DFF])
    if dbg:
        dbg_ot = dout("dbg_ot", [512, TL], BF16)
        dbg_x = dout("dbg_x", [TL, D])
        dbg_x2 = dout("dbg_x2", [NS, D])

    k_src = nc.dram_tensor("k_src", [512, TL], BF16)
    v_src = nc.dram_tensor("v_src", [512, TL], BF16)
    k_all = nc.dram_tensor("k_all", [1024, TL], BF16)
    v_all = nc.dram_tensor("v_all", [1024, TL], BF16)
    s_src = nc.dram_tensor("s_src", [128, 256], F32)
    s_all = nc.dram_tensor("s_all", [2 * 128, 256], F32)
    h_src = nc.dram_tensor("h_src", [128, 16], BF16)
    h_all = nc.dram_tensor("h_all", [2 * 128, 16], BF16)
    wup_bf = [nc.dram_tensor(f"wup_bf{l_}", [NBLK * 128, 8 * 256], BF16) for l_ in range(DEPTH)]
    wa_bf = [nc.dram_tensor(f"wa_bf{l_}", [128, 8 * 1536], BF16) for l_ in range(DEPTH)]
    wb_bf = [nc.dram_tensor(f"wb_bf{l_}", [128, 8 * 1040], BF16) for l_ in range(DEPTH)]
    wr_bf = [nc.dram_tensor(f"wr_bf{l_}", [128, 8 * 512], BF16) for l_ in range(DEPTH)]
    wo_bf = [nc.dram_tensor(f"wo_bf{l_}", [128, 8 * D], BF16) for l_ in range(DEPTH)]
    wd_bf = [nc.dram_tensor(f"wd_bf{l_}", [128, NBLK * D], BF16) for l_ in range(DEPTH)]
    GROUPS = [[2 * g_, 2 * g_ + 1] for g_ in range(ncores // 2)]

    B0 = (nc.sbuf_base + 63) // 64 * 64
    TOP = nc.sbuf_top // 64 * 64
    pers = Arena(nc, B0, TOP)
    X = pers.alloc("X", [128, NT + 1, D], F32)
    C = {n: pers.alloc("c_" + n, s, d) for n, s, d in CONST_SPECS}
    RSTD = pers.alloc("rstd", [128, NT + 1], F32)
    SSQ = pers.alloc("ssq", [128, NT + 1], F32)
    EPST = pers.alloc("epst", [128, 1], F32)
    GA = pers.alloc("GA", [128, D], F32)
    GGLA = pers.alloc("ggla", [128, 128], F32)
    BGATE = pers.alloc("bgate", [128, 256], F32)
    WG2 = pers.alloc("wg2", [16, 256], BF16)
    CONVP = pers.alloc("convp", [128, 2 * NBLK, 4], F32)
    CPRE = pers.alloc("cpre", [128, 2 * NBLK, 8], F32)
    ph = Arena(nc, pers.off, TOP)

    PS = [nc.alloc_psum_tensor(f"ps{i}", [128, 512], F32) for i in range(8)]

    def psb(i):
        return PS[i][:, :].bitcast(BF16)

    def act(out, in_, func, r, w, **kw):
        P.op("act", lambda e: e.activation(out=out, in_=in_, func=func, **kw), r, w, simple=("accum_out" not in kw))

    def tt(out, in0, in1, op, r, w, eng="dve"):
        P.op(eng, lambda e: e.tensor_tensor(out=out, in0=in0, in1=in1, op=op), r, w, simple=True)

    def stt(out, in0, scalar, in1, op0, op1, r, w, accum_out=None):
        P.op("dve", lambda e: e.scalar_tensor_tensor(out=out, in0=in0, scalar=scalar, in1=in1, op0=op0, op1=op1,
                                                      accum_out=accum_out), r, w, simple=(accum_out is None))

    def ts(out, in0, s1, s2, op0, op1, r, w):
        P.op("dve", lambda e: e.tensor_scalar(out=out, in0=in0, scalar1=s1, scalar2=s2, op0=op0, op1=op1), r, w, simple=True)

    def vcopy(out, in_, r, w):
        P.op("dve", lambda e: e.tensor_copy(out=out, in_=in_), r, w, simple=True)

    def acopy(out, in_, r, w):
        P.op("act", lambda e: e.copy(out=out, in_=in_), r, w, simple=True)

    def mm(out, lhsT, rhs, start, stop, r, w):
        P.op("pe", lambda e: e.matmul(out, lhsT, rhs, start=start, stop=stop, skip_group_check=True), r, w,
             pe=(lhsT.start_partition(), lhsT.partition_size()), simple=True)

    def tr(out, in_, ident, r, w):
        P.op("pe", lambda e: e.transpose(out, in_, ident), r, w, pe=(in_.start_partition(), in_.partition_size()), simple=True)

    def rsqrt_cols(dst, src, n, scale, r, w, tmp):
        act(tmp, src, AF.Ln, r, [w + "_t"], scale=scale, bias=EPST[0:n, 0:1])
        act(dst, tmp, AF.Exp, [w + "_t"], [w], scale=-0.5)

    for n, s, d in CONST_SPECS:
        P.dma("sp", C[n][:], cin[n], [], ["c_" + n], sem="const")
    P.op("dve", lambda e: e.memset(EPST[:], EPS), [], ["epst"])
    P.op("dve", lambda e: e.memset(X[:, NT, :], 0.0), [], ["x16"])
    for i in range(NT):
        P.dma("sp", X[:, i, :], xp[i * 128:(i + 1) * 128, :], [], [f"x{i}"], sem="xin")
    P.dma("sp", X[0:NS, NT, :], xs, [], ["x16"], sem="xin")
    ALLC = ["c_" + n for n, _, _ in CONST_SPECS] + ["epst"]

    def convert_weights(l_, which):
        win_ = w_in[l_].rearrange("(k p) c -> p k c", p=128)
        def bsem(nm):
            P.background.add(f"bg{nm}{l_}")
            return f"bg{nm}{l_}"
        if which == "a":
            P.dma("pool", wa_bf[l_].ap().rearrange("p (k c) -> p k c", k=8), win_[:, :, 0:1536], [], [f"wa_bf{l_}"], sem=bsem("wa"))
            return
        wbv = wb_bf[l_].ap().rearrange("p (k c) -> p k c", k=8)
        P.dma("pool", wbv[:, :, 0:1024], win_[:, :, 1536:2560], [], [f"wb_bf{l_}"], sem=bsem("wb"))
        P.dma("pool", wbv[:, :, 1024:1040], win_[:, :, 3072:3088], [], [f"wb_bf{l_}"], sem=bsem("wb"))
        P.dma("pool", wo_bf[l_].ap().rearrange("p (k c) -> p k c", k=8), w_out[l_].rearrange("(k p) c -> p k c", p=128), [], [f"wo_bf{l_}"], sem=bsem("wo"))
        P.dma("pool", wr_bf[l_].ap().rearrange("p (k c) -> p k c", k=8), win_[:, :, 2560:3072], [], [f"wr_bf{l_}"], sem=bsem("wr"))
        P.dma("pool", wd_bf[l_].ap().rearrange("p (b c) -> p b c", b=NBLK), w_down[l_].rearrange("(b p) c -> p b c", p=128), [], [f"wd_bf{l_}"], sem=bsem("wd"))
        wsrc_ = w_up[l_].rearrange("(k p) c -> p k c", p=128)
        for blk in range(NBLK):
            dst_ = wup_bf[l_].ap()[blk * 128:(blk + 1) * 128, :].rearrange("p (k c) -> p k c", k=8)
            P.dma("pool", dst_[:, :, 0:128], wsrc_[:, :, blk * 128:(blk + 1) * 128], [], [f"wupbf{l_}"], sem=bsem("wu"))
            P.dma("pool", dst_[:, :, 128:256], wsrc_[:, :, DFF + blk * 128:DFF + (blk + 1) * 128], [], [f"wupbf{l_}"], sem=bsem("wu"))

    def tile_np(i):
        return 128 if i < NT else NS

    for l in range(n_layers):
        last = (l == n_layers - 1)
        P.barrier()
        ph.reset()
        OT = ph.alloc("OT", [128, 4, TL], BF16)
        SQT = ph.alloc("sqT", [128, 4, NS], BF16)
        SKT = ph.alloc("skT", [128, 4, NS], BF16)
        SVB = ph.alloc("svb", [NS, 512], BF16)
        OMS = ph.alloc("oms", [NS, D], BF16)
        ot_end = ph.off
        QT = ph.alloc("QT", [128, 4, TL], BF16)
        KT = ph.alloc("KT", [128, 4, TL], BF16)
        VT = ph.alloc("VT", [128, 4, TL], BF16)
        keep_off = ph.off
        WA = ph.alloc("WA", [128, 8, 1536], BF16)
        HB = [ph.alloc(f"hb{b}", [128, D], BF16) for b in range(2)]
        HT = [ph.alloc(f"hT{b}", [128, 8, 128], BF16) for b in range(2)]
        JUNK = ph.alloc("junk", [128, D], BF16)
        KF = ph.alloc("kf", [128, 512], F32)
        VF = ph.alloc("vf", [128, 512], F32)
        QB = ph.alloc("qb", [128, 512], BF16)
        KB = ph.alloc("kb", [128, 512], BF16)
        VB = ph.alloc("vb", [128, 512], BF16)
        T1 = ph.alloc("t1", [128, 8, 16], F32)
        T2 = ph.alloc("t2", [128, 8, 16], F32)
        LNT = ph.alloc("lnt", [128, NT + 1], F32)

        if skip:
            for t_ in (OT, QT, KT, VT):
                P.op("dve", lambda e, t_=t_: e.memset(t_[:, :, :], 0.0), [], [t_.name] + [f"{t_.name}{i}" for i in range(NT)])
        if l == 0:
            for cg in range(3):
                P.dma("pool", WA[:, :, cg * 512:(cg + 1) * 512], w_in[l].rearrange("(k p) c -> p k c", p=128)[:, :, cg * 512:(cg + 1) * 512],
                      [], [f"WA{cg}"], sem=f"win{cg}")
        else:
            P.dma("sp", WA[:, :, :].rearrange("p k c -> p (k c)"), wa_bf[l].ap(), [f"wa_bf{l}"], ["WA0", "WA1", "WA2"], sem="win")
        P.dma("sp", GA[:], g_mix_pre[l].partition_broadcast(128), [], ["GA"], sem="lp")

        for r_ in range(3):
            P.dma("sp", CONVP[:, :, r_], conv_w[l, r_].rearrange("(b p) -> p b", p=128), [], ["convp"], sem="lp3",
                  allow_slow_non_contiguous=True)
        P.dma("sp", CONVP[:, :, 3], conv_b[l].rearrange("(b p) -> p b", p=128), [], ["convp"], sem="lp3",
              allow_slow_non_contiguous=True)
        for s_ in range(4):
            for r_ in range(2):
                P.dma("sp", CPRE[:, :, s_ * 2 + r_], sc_in[l, s_, r_].rearrange("(b p) -> p b", p=128), [], ["cpre"], sem="lp3",
                      allow_slow_non_contiguous=True)

        P.op("dve", lambda e: e.memset(SSQ[:, NT:NT + 1], 1.0), [], [f"ssq{NT}"])
        for i in range(NT + 1):
            n_ = tile_np(i)
            stt(JUNK[0:n_, :], X[0:n_, i, :], 1.0, X[0:n_, i, :], ALU.mult, ALU.mult, [f"x{i}", f"ssq{i}"], ["junk", f"ssq{i}"],
                accum_out=SSQ[0:n_, i:i + 1])
        if True:
            allss = [f"ssq{i}" for i in range(NT + 1)]
            act(LNT[:, :], SSQ[:, :], AF.Ln, allss + ["epst"], ["lnt"], scale=1.0 / D, bias=EPST[:, 0:1])
            act(RSTD[:, :], LNT[:, :], AF.Exp, ["lnt"], ["rstd"], scale=-0.5)

        def norm_and_transpose(i, hb, hT, pbank, gkey="GA"):
            n_ = tile_np(i)
            stt(hb[0:n_, :], X[0:n_, i, :], RSTD[0:n_, i:i + 1], GA[0:n_, :], ALU.mult, ALU.mult,
                [f"x{i}", "rstd", gkey], [hb.name])
            pv = psb(pbank)
            for k in range(8):
                tr(pv[:, k * 128:k * 128 + n_], hb[0:n_, k * 128:(k + 1) * 128], C["identb"][0:n_, 0:n_],
                   [hb.name, "c_identb"], [f"ps{pbank}"])
            src = pv.rearrange("p (k t) -> p k t", k=8)[:, :, 0:n_]
            acopy(hT[:, :, 0:n_], src, [], [f"ps{pbank}", hT.name])

        def rope(ps_ap, tab, out_ap, n_, rkeys, wkeys, tabkey):
            pv = ps_ap.rearrange("p (h d) -> p h d", h=8)[:, :, 0:16]
            ov = out_ap.rearrange("p (h d) -> p h d", h=8)
            cc = tab[:, 0:16].unsqueeze(1).broadcast_to([n_, 8, 16])
            ss = tab[:, 16:32].unsqueeze(1).broadcast_to([n_, 8, 16])
            tt(T1[0:n_], pv, cc, ALU.mult, rkeys + [tabkey], ["t1"] + [k for k in wkeys if k.startswith("ps")])
            tt(T2[0:n_], pv, ss, ALU.mult, rkeys + [tabkey], ["t2"] + [k for k in wkeys if k.startswith("ps")])
            tt(ov[:, :, 0:8], T1[0:n_, :, 0:8], T2[0:n_, :, 8:16], ALU.subtract, ["t1", "t2"], wkeys)
            tt(ov[:, :, 8:16], T1[0:n_, :, 8:16], T2[0:n_, :, 0:8], ALU.add, ["t1", "t2"], wkeys)

        PBANKS = ((2, 3, 4), (1, 6, 7))

        def m1a_front(i):
            n_ = tile_np(i)
            b = i % 2
            hb, hT = HB[b], HT[b]
            norm_and_transpose(i, hb, hT, 0)
            for cg in range(3):
                pb_ = PBANKS[b][cg]
                for k in range(8):
                    mm(PS[pb_][0:n_, :], hT[:, k, 0:n_], WA[:, k, cg * 512:(cg + 1) * 512], k == 0, k == 7,
                       [hT.name, f"WA{cg}"], [f"ps{pb_}"])

        def m1a_back(i):
            n_ = tile_np(i)
            bq, bk, bv = PBANKS[i % 2]
            tq = C["ropeq"][:, i, :] if i < NT else C["ropesq"][:, :]
            tk = C["ropek"][:, i, :] if i < NT else C["ropesk"][:, :]
            tqk = "c_ropeq" if i < NT else "c_ropesq"
            tkk = "c_ropek" if i < NT else "c_ropesk"
            act(QB[0:n_, :], PS[bq][0:n_, :], AF.Copy, [], [f"ps{bq}", QB.name], scale=0.125)
            rope(PS[bq][0:n_, :], tq[0:n_], QB[0:n_, :], n_, [], [f"ps{bq}", QB.name], tqk)
            acopy(KF[0:n_, :], PS[bk][0:n_, :], [], [f"ps{bk}", "kf"])
            rope(PS[bk][0:n_, :], tk[0:n_], KF[0:n_, :], n_, [], [f"ps{bk}", "kf"], tkk)
            acopy(KB[0:n_, :], KF[0:n_, :], ["kf"], [KB.name])
            acopy(VF[0:n_, :], PS[bv][0:n_, :], [], [f"ps{bv}", "vf"])
            vdst = VB if i < NT else SVB
            vcopy(vdst[0:n_, :], VF[0:n_, :], ["vf"], [vdst.name])
            if i < NT:
                P.dma("sp", kp[l, i * 128:(i + 1) * 128, :], KF[:, :], ["kf"], [], sem="st_kf")
                P.dma("sp", vp[l, i * 128:(i + 1) * 128, :], VF[:, :], ["vf"], [], sem="st_vf")
            else:
                P.dma("sp", kso[l], KF[0:NS, :], ["kf"], [], sem="st_kf")
                P.dma("sp", vso[l], VF[0:NS, :], ["vf"], [], sem="st_vf")
            for (src, half_, dstT, sdst) in ((QB, 0, QT, SQT), (KB, 1, KT, SKT), (VB, 0, VT, None)):
                if i == NT and sdst is None:
                    continue
                pv = psb(5)[:, half_ * 512:(half_ + 1) * 512]
                for j in range(4):
                    tr(pv[:, j * 128:j * 128 + n_], src[0:n_, j * 128:(j + 1) * 128], C["identb"][0:n_, 0:n_],
                       [src.name, "c_identb"], ["ps5"])
                srcv = pv.rearrange("p (j t) -> p j t", j=4)[:, :, 0:n_]
                if i < NT:
                    vcopy(dstT[:, :, i * 128:(i + 1) * 128], srcv, [], ["ps5", f"{dstT.name}{i}"])
                else:
                    vcopy(sdst[:, :, :], srcv, [], ["ps5", sdst.name])

        prev_back = None
        for i in ([] if "m1a" in skip else list(range(NT + 1)) + [None]):
            front = None
            if i is not None:
                P.begin_rec()
                m1a_front(i)
                front = P.end_rec()
            P.play(front, prev_back)
            prev_back = None
            if i is not None:
                P.begin_rec()
                m1a_back(i)
                prev_back = P.end_rec()

        ktk = [f"{KT.name}{i}" for i in range(NT)]
        vtk = [f"{VT.name}{i}" for i in range(NT)]
        for j in range(4):
            P.dma("sp", k_src.ap()[j * 128:(j + 1) * 128, :], KT[:, j, :], ktk, ["k_src"], sem="kvx")
            P.dma("sp", v_src.ap()[j * 128:(j + 1) * 128, :], VT[:, j, :], vtk, ["v_src"], sem="kvx")
        P.collective([k_src.ap()], [k_all.ap()], GROUPS, ["k_src"], ["k_all"])
        P.collective([v_src.ap()], [v_all.ap()], GROUPS, ["v_src"], ["v_all"])
        if dbg == "m1a":
            break

        P.barrier()
        ph.reset(keep_off)
        KTP = [ph.alloc(f"ktp{b}", [128, TL], BF16) for b in range(2)]
        VTP = [ph.alloc(f"vtp{b}", [128, TL], BF16) for b in range(2)]
        ACC = [ph.alloc(f"acc{b}", [128, TL], F32) for b in range(2)]
        RDEN = ph.alloc("rden", [128, TL], F32)
        PT = [ph.alloc(f"pt{b}", [128, 256], BF16) for b in range(4)]
        NVA = 12
        VA = [[ph.alloc(f"va{e}_{s}", [128, 128], BF16) for s in range(NVA)] for e in range(2)]
        for e_ in range(2):
            for s_ in range(NVA):
                t_ = VA[e_][s_]
                P.op("dve", lambda e, t_=t_: e.memset(t_[:, :], 1.0), [], [t_.name])

        def load_prefix(j):
            b = j % 2
            P.dma("sp", KTP[b][:, :], k_all.ap()[j * 128:(j + 1) * 128, :], ["k_all"], [KTP[b].name], sem=f"pre{b}")
            P.dma("sp", VTP[b][:, :], v_all.ap()[j * 128:(j + 1) * 128, :], ["v_all"], [VTP[b].name], sem=f"pre{b}")

        if l == 0:
            convert_weights(0, "rest")
        load_prefix(0)
        cnt = {"sb": 0, "vb": 0, "pt": 0, "va": [0, 0], "tb": 0, "cp": 0}
        for h in ([] if "m2" in skip else range(8)):
            j, e_ = h // 2, h % 2
            if e_ == 0 and j + 1 < 4:
                load_prefix(j + 1)
            rs = slice(e_ * 64, (e_ + 1) * 64)
            ab = h % 2
            acc = ACC[ab]
            ktp, vtp = KTP[j % 2], VTP[j % 2]
            vcols = slice(0, 64) if e_ == 0 else slice(64, 128)

            def build_v(src, srckeys, cols):
                s_ = cnt["va"][e_] % NVA
                cnt["va"][e_] += 1
                va = VA[e_][s_]
                tb = 6 + (cnt["tb"] % 2)
                sl = (cnt["tb"] // 2) % 8
                cnt["tb"] += 1
                pv = psb(tb)[:, sl * 64:(sl + 1) * 64]
                tr(pv, src[rs, cols], C["identb"][rs, rs], srckeys + ["c_identb"], [f"ps{tb}"])
                if cnt["cp"] % 2 == 0:
                    vcopy(va[:, vcols], pv, [], [f"ps{tb}", va.name])
                else:
                    acopy(va[:, vcols], pv, [], [f"ps{tb}", va.name])
                cnt["cp"] += 1
                return va

            jobs = []
            for d_ in (1, 4, 16):
                for r in range(d_):
                    for i in range(16 // d_):
                        jobs.append({"d": d_, "r": r, "i": i})
            chain = {}

            def s1(jb):
                d_, r, i = jb["d"], jb["r"], jb["i"]
                span = d_ * 127 + 1
                p0 = TL - 128 * d_ + r
                if i == 0:
                    chain[(d_, r)] = build_v(vtp, [vtp.name], slice(p0, p0 + span, d_))
                vprev = chain[(d_, r)]
                c0 = d_ * 128 * i + r
                cols = slice(c0, c0 + span, d_)
                blks = list(range((d_ * 128 * i) // 128, (d_ * 128 * (i + 1)) // 128))
                vdiag = build_v(VT[:, j, :], [f"{VT.name}{b_}" for b_ in blks], cols)
                chain[(d_, r)] = vdiag
                sb = cnt["sb"] % 4; cnt["sb"] += 1
                qk_r = [f"{QT.name}{b_}" for b_ in blks]
                if i == 0:
                    mm(PS[sb][:, 0:128], ktp[rs, slice(p0, p0 + span, d_)], QT[rs, j, cols], True, False,
                       [ktp.name] + qk_r, [f"ps{sb}"])
                else:
                    pc0 = d_ * 128 * (i - 1) + r
                    pblks = list(range((d_ * 128 * (i - 1)) // 128, (d_ * 128 * i) // 128))
                    mm(PS[sb][:, 0:128], KT[rs, j, slice(pc0, pc0 + span, d_)], QT[rs, j, cols], True, False,
                       [f"{KT.name}{b_}" for b_ in pblks] + qk_r, [f"ps{sb}"])
                mm(PS[sb][:, 128:256], KT[rs, j, cols], QT[rs, j, cols], False, True,
                   [f"{KT.name}{b_}" for b_ in blks] + qk_r, [f"ps{sb}"])
                jb.update(vprev=vprev, vdiag=vdiag, sb=sb, cols=cols, blks=blks)

            def s2a(jb):
                sb = jb["sb"]
                pt = PT[cnt["pt"] % 4]; cnt["pt"] += 1
                act(pt[:, :], PS[sb][:, 0:256], AF.Exp, [], [f"ps{sb}", pt.name])
                msk = C["mb"] if jb["i"] == 0 else C["ma"]
                tt(pt[:, :], pt[:, :], msk[:, :], ALU.mult, [pt.name, "c_ma", "c_mb"], [pt.name])
                jb["pt"] = pt

            def s2(jb):
                sb, vprev, vdiag, pt = jb["sb"], jb["vprev"], jb["vdiag"], jb["pt"]
                vb = 4 + cnt["vb"] % 2; cnt["vb"] += 1
                mm(PS[vb][:, 0:128], vprev[:, :], pt[:, 0:128], True, False, [vprev.name, pt.name], [f"ps{vb}"])
                mm(PS[vb][:, 0:128], vdiag[:, :], pt[:, 128:256], False, True, [vdiag.name, pt.name], [f"ps{vb}"])
                jb["vb"] = vb

            def s3(jb):
                d_, r, i, vb, cols, blks = jb["d"], jb["r"], jb["i"], jb["vb"], jb["cols"], jb["blks"]
                if d_ == 1:
                    acopy(acc[:, cols], PS[vb][:, 0:128], [], [f"ps{vb}", f"A1_{ab}_{i}"])
                elif d_ == 4:
                    tt(acc[:, cols], PS[vb][:, 0:128], acc[:, cols], ALU.add,
                       [f"A1_{ab}_{b_}" for b_ in blks], [f"ps{vb}", f"A4_{ab}_{r}_{i}"])
                else:
                    tt(acc[:, cols], PS[vb][:, 0:128], acc[:, cols], ALU.add,
                       [f"A1_{ab}_{b_}" for b_ in range(16)] + [f"A4_{ab}_{r % 4}_{i_}" for i_ in range(4)],
                       [f"ps{vb}", f"A16_{ab}_{r}"])

            nj = len(jobs)
            for k_ in range(nj + 3):
                if k_ < nj:
                    s1(jobs[k_])
                if 0 <= k_ - 1 < nj:
                    s2a(jobs[k_ - 1])
                if 0 <= k_ - 2 < nj:
                    s2(jobs[k_ - 2])
                if 0 <= k_ - 3 < nj:
                    s3(jobs[k_ - 3])
            allacc = ([f"A1_{ab}_{b_}" for b_ in range(16)] + [f"A4_{ab}_{r}_{i_}" for r in range(4) for i_ in range(4)]
                      + [f"A16_{ab}_{r}" for r in range(16)])
            urs = rs
            drs = slice(64, 128) if e_ == 0 else slice(0, 64)
            P.op("dve", lambda e, urs=urs, drs=drs, acc=acc: e.reciprocal(out=RDEN[urs, :], in_=acc[drs, :]), allacc, ["rden"])
            tt(OT[urs, j, :], acc[urs, :], RDEN[urs, :], ALU.mult, allacc + ["rden"], [f"OT{h}"] + allacc)
        P.barrier()
        ph.reset(ot_end)
        KC = [ph.alloc(f"kc{b}", [128, 9, 512], BF16) for b in range(2)]
        VC = [ph.alloc(f"vc{b}", [128, 9, 512], BF16) for b in range(2)]
        KCT = ph.alloc("kct", [128, 4, 9, 128], BF16)
        PTS = [ph.alloc(f"pts{b}", [128, 144], BF16) for b in range(2)]
        PNF = ph.alloc("pnf", [16, 2, 4, 16], F32)
        PNB = ph.alloc("pnb", [16, 2, 4, 16], BF16)
        RDS = ph.alloc("rds", [16, 8], F32)

        def load_cache(j):
            b = j % 2
            for (dst, src) in ((KC[b], ck), (VC[b], cv)):
                P.dma("pool", dst[:, 0, :], src[l, j, 1920:2048, :], [], [dst.name], sem=dst.name)
                P.dma("pool", dst[:, 1:5, :], src[l, j, 1536:2048, :].rearrange("(m i) c -> m i c", i=4), [], [dst.name], sem=dst.name)
                P.dma("pool", dst[:, 5:9, :], src[l, j].rearrange("(m r) c -> m r c", r=16)[:, 0:4, :], [], [dst.name], sem=dst.name)

        P.op("dve", lambda e: e.memset(PS[4][0:16, :], 0.0), [], ["ps4"])
        P.op("dve", lambda e: e.memset(PS[5][0:16, 0:8], 0.0), [], ["ps5"])
        load_cache(0)
        scn = 0
        for j in range(4):
            if j + 1 < 4:
                load_cache(j + 1)
            kc, vc = KC[j % 2], VC[j % 2]
            for p_ in range(4):
                for tau in range(9):
                    bank, col = (0, tau * 128) if tau < 8 else (1, 0)
                    tr(psb(bank)[:, col:col + 128], kc[:, tau, p_ * 128:(p_ + 1) * 128], C["identb"][:, :], [kc.name, "c_identb"], [f"ps{bank}"])
                acopy(KCT[:, p_, 0:8, :], psb(0)[:, :].rearrange("p (t k) -> p t k", t=8), [], ["ps0", f"kct{p_}"])
                vcopy(KCT[:, p_, 8, :], psb(1)[:, 0:128], [], ["ps1", f"kct{p_}"])
            for h in range(8):
                p_, e_ = h // 2, h % 2
                rs = slice(e_ * 64, (e_ + 1) * 64)
                sb = 2 + scn % 2
                pts = PTS[scn % 2]; scn += 1
                mm(PS[sb][:, 0:144], C["identb"][:, :], C["smask"][:, j * 144:(j + 1) * 144], True, False, ["c_identb", "c_smask"], [f"ps{sb}"])
                for tau in range(9):
                    mm(PS[sb][:, tau * 16:(tau + 1) * 16], KCT[rs, p_, tau, :], SQT[rs, p_, :], False, tau == 8, [f"kct{p_}", SQT.name], [f"ps{sb}"])
                act(pts[:, :], PS[sb][:, 0:144], AF.Exp, [], [f"ps{sb}", pts.name])
                for tau in range(9):
                    mm(PS[4][0:16, h * 64:(h + 1) * 64], pts[:, tau * 16:(tau + 1) * 16], vc[:, tau, h * 64:(h + 1) * 64], False, False,
                       [pts.name, vc.name], ["ps4"])
                    mm(PS[5][0:16, h:h + 1], pts[:, tau * 16:(tau + 1) * 16], C["onesb"][:, 0:1], False, False, [pts.name, "c_onesb"], ["ps5"])
        for e_ in range(2):
            rs = slice(e_ * 64, (e_ + 1) * 64)
            nb = 6 + e_
            for p_ in range(4):
                mm(PS[nb][0:16, p_ * 16:(p_ + 1) * 16], SKT[rs, p_, :], SQT[rs, p_, :], True, True, [SKT.name, SQT.name], [f"ps{nb}"])
            act(PNF[:, e_, :, :], PS[nb][0:16, 0:64].rearrange("p (a t) -> p a t", a=4), AF.Exp, [], [f"ps{nb}", f"pnf{e_}"])
            tt(PNB[:, e_, :, :], PNF[:, e_, :, :], C["mnew"][:, :].unsqueeze(1).broadcast_to([16, 4, 16]), ALU.mult,
               [f"pnf{e_}", "c_mnew"], [f"pnb{e_}"])
        for h in range(8):
            p_, e_ = h // 2, h % 2
            mm(PS[4][0:16, h * 64:(h + 1) * 64], PNB[:, e_, p_, :], SVB[0:16, h * 64:(h + 1) * 64], False, False, [f"pnb{e_}", SVB.name], ["ps4"])
            mm(PS[5][0:16, h:h + 1], PNB[:, e_, p_, :], C["onesb"][0:16, 0:1], False, False, [f"pnb{e_}", "c_onesb"], ["ps5"])
        P.op("dve", lambda e: e.reciprocal(out=RDS[:, :], in_=PS[5][0:16, 0:8]), [], ["ps5", "rds"])
        tt(OMS[0:16, 0:512].rearrange("p (h d) -> p h d", h=8), PS[4][0:16, :].rearrange("p (h d) -> p h d", h=8),
           RDS[:, :].unsqueeze(2).broadcast_to([16, 8, 64]), ALU.mult, ["rds"], ["ps4", "oms_a"])
        if dbg == "m2s":
            P.dma("pool", dbg_x[0:16, 0:512], OMS[0:16, 0:512], ["oms_a"], [], sem="dbg")
            break
        if dbg == "m2":
            for j in range(4):
                P.dma("sp", dbg_ot[j * 128:(j + 1) * 128, :], OT[:, j, :], [f"OT{h_}" for h_ in range(8)], [], sem="dbg")
            break

        P.barrier()
        ph.reset(ot_end)
        OB = ph.alloc("OB", [128, NT + 1, 512], F32)
        QH = ph.alloc("QH", [128, 2, TL], BF16)
        SFIN = ph.alloc("sfin", [128, 2, 128], F32)
        SA_F = ph.alloc("sa_f", [128, 2, 128], F32)
        SA_B = ph.alloc("sa_b", [128, 2, 128], BF16)
        keep2_off = ph.off
        WB = ph.alloc("WB", [128, 8, 1040], BF16)
        par = lambda nm, shp, dt: [ph.alloc(f"{nm}{b}", shp, dt) for b in range(2)]
        HB1s, HT1s = par("hb", [128, D], BF16), par("hT", [128, 8, 128], BF16)
        LAs, EBs, ENBs, EGs = par("la", [128, 256], F32), par("eb", [128, 256], F32), par("enb", [128, 256], F32), par("eg", [128, 256], F32)
        QTLs, KTLs, QHLs = par("qtl", [128, 256], BF16), par("ktl", [128, 256], BF16), par("qhl", [128, 256], BF16)
        VBFs = par("vbf", [128, 512], BF16)
        QTTs, KTTs = par("qtt", [128, 2, 128], BF16), par("ktt", [128, 2, 128], BF16)
        AMs_ = par("am", [128, 4, 128], BF16)
        GLs, GLTs = par("gl", [128, 16], BF16), par("glt", [16, 128], BF16)
        ETOTs = par("etot", [128, 2], F32)
        GACC = ph.alloc("gacc", [128, 256], F32)
        S_F = ph.alloc("s_f", [128, 2, 128], F32)
        S_B = ph.alloc("s_b", [128, 2, 128], BF16)
        TOTA = ph.alloc("tota", [128, 2], F32)

        P.dma("sp", WB[:, :, :].rearrange("p k c -> p (k c)"), wb_bf[l].ap(), [f"wb_bf{l}"], ["WB"], sem="win")
        P.dma("sp", GGLA[:], g_gla[l].partition_broadcast(128), [], ["ggla"], sem="lp")
        P.dma("sp", BGATE[:], b_gate[l].partition_broadcast(128), [], ["bgate"], sem="lp")
        P.dma("pool", WG2[:], w_gate2[l], [], ["wg2"], sem="lp2")
        if l == 0 and n_layers > 1:
            convert_weights(1, "a")
            convert_weights(1, "rest")
        P.op("dve", lambda e: e.memset(S_F[:], 0.0), [], ["s_f"])
        P.op("dve", lambda e: e.memset(S_B[:], 0.0), [], ["s_b"])
        P.op("dve", lambda e: e.memset(GACC[:], 0.0), [], ["gacc"])
        P.op("dve", lambda e: e.memset(TOTA[:], 0.0), [], ["tota"])
        ONE1 = C["onesf"][:, 0:1]

        def gla_front(i, n_, b, sample=False):
            HB1, HT1, LA, EB, ENB, EG = HB1s[b], HT1s[b], LAs[b], EBs[b], ENBs[b], EGs[b]
            QTL, KTL, QHL, VBF, GL, GLT, ETOT = QTLs[b], KTLs[b], QHLs[b], VBFs[b], GLs[b], GLTs[b], ETOTs[b]
            norm_and_transpose(i, HB1, HT1, 0)
            for (bank, c0, cw) in ((1, 0, 512), (2, 512, 512), (4, 1024, 16)):
                for k in range(8):
                    mm(PS[bank][0:n_, 0:cw], HT1[:, k, 0:n_], WB[:, k, c0:c0 + cw], k == 0, k == 7, [HT1.name, "WB"], [f"ps{bank}"])
            acopy(GL[0:n_, :], PS[4][0:n_, 0:16], [], ["ps4", GL.name])
            tr(psb(3)[0:16, 0:n_], GL[0:n_, 0:16], C["identb"][0:n_, 0:n_], [GL.name, "c_identb"], ["ps3"])
            vcopy(GLT[0:16, 0:n_], psb(3)[0:16, 0:n_], [], ["ps3", GLT.name])
            mm(PS[4][0:n_, 256:512], GLT[0:16, 0:n_], WG2[0:16, :], True, True, [GLT.name, "wg2"], ["ps4"])
            tt(LA[0:n_, :], PS[4][0:n_, 256:512], BGATE[0:n_, :], ALU.add, ["bgate"], ["ps4", LA.name])
            act(LA[0:n_, :], LA[0:n_, :], AF.Exp, [LA.name], [LA.name], scale=-1.0)
            act(LA[0:n_, :], LA[0:n_, :], AF.Ln, [LA.name, "c_onesf"], [LA.name], bias=ONE1[0:n_, :])
            cs = C["ucs_s"] if sample else C["ucs"]
            mm(PS[5][0:n_, 0:256], cs[0:n_, 0:n_], LA[0:n_, :], True, True, [LA.name, "c_ucs", "c_ucs_s"], ["ps5"])
            act(EB[0:n_, :], PS[5][0:n_, 0:256], AF.Exp, [], ["ps5", EB.name], scale=-1.0 / 16)
            act(ENB[0:n_, :], PS[5][0:n_, 0:256], AF.Exp, [], ["ps5", ENB.name], scale=1.0 / 16)
            stt(QTL[0:n_, :], PS[1][0:n_, 0:256], 0.125, EB[0:n_, :], ALU.mult, ALU.mult, [EB.name], ["ps1", QTL.name])
            tt(KTL[0:n_, :], PS[1][0:n_, 256:512], ENB[0:n_, :], ALU.mult, [ENB.name], ["ps1", KTL.name])
            acopy(VBF[0:n_, :], PS[2][0:n_, :], [], ["ps2", VBF.name])
            if sample:
                return
            mm(PS[5][0:n_, 256:512], C["onesf"][0:n_, 0:n_], LA[0:n_, :], True, True, [LA.name, "c_onesf"], ["ps5"])
            for p_ in range(2):
                mm(PS[3][:, 128 + p_:129 + p_], LA[0:n_, p_ * 128:(p_ + 1) * 128], C["onesf"][0:n_, 0:1], True, True,
                   [LA.name, "c_onesf"], ["ps3"])
            act(EG[0:n_, :], GACC[0:n_, :], AF.Exp, ["gacc"], [EG.name], scale=-1.0 / 16)
            tt(GACC[0:n_, :], PS[5][0:n_, 256:512], GACC[0:n_, :], ALU.add, [], ["ps5", "gacc"])
            act(ETOT[:, :], PS[3][:, 128:130], AF.Exp, [], ["ps3", ETOT.name], scale=-1.0 / 16)
            tt(TOTA[:, :], PS[3][:, 128:130], TOTA[:, :], ALU.add, [], ["ps3", "tota"])
            tt(QHL[0:n_, :], QTL[0:n_, :], EG[0:n_, :], ALU.mult, [QTL.name, EG.name], [QHL.name])
            for p_ in range(2):
                tr(psb(3)[:, 512 + p_ * 128:512 + p_ * 128 + n_], QHL[0:n_, p_ * 128:(p_ + 1) * 128],
                   C["identb"][0:n_, 0:n_], [QHL.name, "c_identb"], ["ps3"])
            acopy(QH[:, :, i * 128:(i + 1) * 128], psb(3)[:, 512:768].rearrange("p (a t) -> p a t", a=2), [], ["ps3", f"QH{i}"])

        def gla_back(i, b):
            n_ = 128
            QTL, KTL, VBF, QTT, KTT, AM, ETOT = QTLs[b], KTLs[b], VBFs[b], QTTs[b], KTTs[b], AMs_[b], ETOTs[b]
            for (src, c0) in ((QTL, 0), (KTL, 256)):
                for p_ in range(2):
                    tr(psb(6)[:, c0 + p_ * 128:c0 + p_ * 128 + n_], src[0:n_, p_ * 128:(p_ + 1) * 128],
                       C["identb"][0:n_, 0:n_], [src.name, "c_identb"], ["ps6"])
            v6 = psb(6)
            acopy(QTT[:, :, :], v6[:, 0:256].rearrange("p (a t) -> p a t", a=2), [], ["ps6", QTT.name])
            vcopy(KTT[:, :, :], v6[:, 256:512].rearrange("p (a t) -> p a t", a=2), [], ["ps6", KTT.name])
            for h in range(4):
                p_, e_ = h // 2, h % 2
                rs = slice(e_ * 64, (e_ + 1) * 64)
                if e_ == 0:
                    mm(PS[7][:, p_ * 128:(p_ + 1) * 128], KTT[rs, p_, :], QTT[rs, p_, :], True, True, [KTT.name, QTT.name], ["ps7"])
                else:
                    mm(PS[6][:, 256 + p_ * 128:256 + (p_ + 1) * 128], KTT[rs, p_, :], QTT[rs, p_, :], True, True, [KTT.name, QTT.name], ["ps6"])
            ucb = C["ucs"][:, :].unsqueeze(1).broadcast_to([128, 2, 128])
            tt(AM[:, 0::2, :], PS[7][:, 0:256].rearrange("p (h t) -> p h t", h=2), ucb, ALU.mult, ["c_ucs"], ["ps7", AM.name + "0"])
            tt(AM[:, 1::2, :], PS[6][:, 256:512].rearrange("p (h t) -> p h t", h=2), ucb, ALU.mult, ["c_ucs"], ["ps6", AM.name + "1"])
            for h in range(4):
                p_, e_ = h // 2, h % 2
                rs = slice(e_ * 64, (e_ + 1) * 64)
                mm(PS[7][:, h * 128:(h + 1) * 128], AM[:, h, :], VBF[:, h * 128:(h + 1) * 128], True, False, [AM.name + str(e_), VBF.name], ["ps7"])
                mm(PS[7][:, h * 128:(h + 1) * 128], QTT[rs, p_, :], S_B[rs, p_, :], False, True, [QTT.name, "s_b"], ["ps7"])
            acopy(OB[:, i, :], PS[7][:, :], [], ["ps7", f"ob{i}"])
            for p_ in range(2):
                mm(PS[6][:, p_ * 256:(p_ + 1) * 256], KTL[:, p_ * 128:(p_ + 1) * 128], VBF[:, p_ * 256:(p_ + 1) * 256], True, True,
                   [KTL.name, VBF.name], ["ps6"])
            tt(S_F[:, :, :], S_F[:, :, :], ETOT[:, :].unsqueeze(2).broadcast_to([128, 2, 128]), ALU.mult, ["s_f", ETOT.name], ["s_f"])
            for p_ in range(2):
                for e_ in range(2):
                    rs = slice(e_ * 64, (e_ + 1) * 64)
                    stt(S_F[rs, p_, :], PS[6][rs, p_ * 256 + e_ * 128:p_ * 256 + (e_ + 1) * 128], ETOT[rs, p_:p_ + 1], S_F[rs, p_, :],
                        ALU.mult, ALU.add, ["s_f", ETOT.name], ["ps6", "s_f"])
            vcopy(S_B[:, :, :], S_F[:, :, :], ["s_f"], ["s_b"])

        prev_back = None
        for i in range(NT + 1):
            front = None
            if i < NT:
                P.begin_rec()
                gla_front(i, 128, i % 2)
                front = P.end_rec()
            P.play(front, prev_back)
            prev_back = None
            if i < NT:
                P.begin_rec()
                gla_back(i, i % 2)
                prev_back = P.end_rec()
        if dbg and dbg.startswith("m1b") and dbg != "m1bx":
            break

        if "sgla" not in skip:
            QTL, KTL, VBF, QTT, KTT, LA = QTLs[0], KTLs[0], VBFs[0], QTTs[0], KTTs[0], LAs[0]
            S0F = [ph.alias("s0f0", [128, 2, 128], F32, EGs[1]), ph.alias("s0f1", [128, 2, 128], F32, EBs[1])]
            S0K = [[EGs[1].name], [EBs[1].name]]
            S0B = [ph.alloc(f"s0b{b}", [128, 2, 128], BF16) for b in range(2)]
            ETS = ph.alloc("ets", [128, 2, 4], F32)
            QTM = [ph.alloc(f"qtm{b}", [128, 2, 16], BF16) for b in range(2)]
            AMS = ph.alias("ams", [128, 4, 16], BF16, AMs_[1])
            KTM = [ph.alias("ktm0", [16, 256], BF16, AMs_[1], 256), ph.alias("ktm1", [16, 256], BF16, QHLs[1])]
            KTK = [[AMs_[1].name + "0", AMs_[1].name + "1"], [QHLs[1].name]]
            i = NT
            n_ = NS
            P.op("dve", lambda e: e.memset(AMS[:, :, :], 0.0), [], ["ams0", "ams1", AMs_[1].name + "0", AMs_[1].name + "1"])
            gla_front(i, n_, 0, sample=True)
            for p_ in range(2):
                mm(PS[3][:, 128 + p_ * 4:128 + (p_ + 1) * 4], LA[0:n_, p_ * 128:(p_ + 1) * 128], C["seqsel"][0:n_, 0:4], True, True,
                   [LA.name, "c_seqsel"], ["ps3"])
            act(ETS[:, :, :], PS[3][:, 128:136].rearrange("p (a s) -> p a s", a=2), AF.Exp, [], ["ps3", "ets"], scale=-1.0 / 16)
            for (src, c0) in ((QTL, 0), (KTL, 256)):
                for p_ in range(2):
                    tr(psb(6)[:, c0 + p_ * 128:c0 + p_ * 128 + n_], src[0:n_, p_ * 128:(p_ + 1) * 128],
                       C["identb"][0:n_, 0:n_], [src.name, "c_identb"], ["ps6"])
            v6 = psb(6)
            acopy(QTT[:, :, 0:n_], v6[:, 0:256].rearrange("p (a t) -> p a t", a=2)[:, :, 0:n_], [], ["ps6", QTT.name])
            vcopy(KTT[:, :, 0:n_], v6[:, 256:512].rearrange("p (a t) -> p a t", a=2)[:, :, 0:n_], [], ["ps6", KTT.name])
            for h in range(4):
                p_, e_ = h // 2, h % 2
                rs = slice(e_ * 64, (e_ + 1) * 64)
                ab_ = 7 if e_ == 0 else 0
                mm(PS[ab_][0:n_, p_ * 16:(p_ + 1) * 16], KTT[rs, p_, 0:n_], QTT[rs, p_, 0:n_], True, True, [KTT.name, QTT.name], [f"ps{ab_}"])
            for e_ in range(2):
                ab_ = 7 if e_ == 0 else 0
                tt(AMS[0:n_, e_::2, :], PS[ab_][0:n_, 0:32].rearrange("p (h t) -> p h t", h=2),
                   C["ucs_s"][:, :].unsqueeze(1).broadcast_to([16, 2, 16]), ALU.mult, ["c_ucs_s"], [f"ps{ab_}", f"ams{e_}"])
            P.op("dve", lambda e: e.memset(PS[1][0:16, :], 0.0), [], ["ps1"])
            P.op("dve", lambda e: e.memset(PS[3][0:16, :], 0.0), [], ["ps3"])
            for h in range(4):
                e_ = h % 2
                mm(PS[1][0:n_, h * 128:(h + 1) * 128], AMS[:, h, :], VBF[:, h * 128:(h + 1) * 128], False, False, [f"ams{e_}", VBF.name], ["ps1"])
            for j in range(4):
                b = j % 2
                s0f, s0b, qtm, ktm = S0F[b], S0B[b], QTM[b], KTM[b]
                for e_ in range(2):
                    P.dma("sp", s0f[e_ * 64:(e_ + 1) * 64, :, :], sg_in[l, j].rearrange("(a e) k v -> e k a v", e=2)[e_], [],
                          [s0f.name] + S0K[b], sem=f"s0ld{b}")
                vcopy(s0b[:, :, :], s0f[:, :, :], [s0f.name], [s0b.name])
                tt(qtm[:, :, :], QTT[:, :, 0:n_], C["seqcol"][:, j * 16:(j + 1) * 16].unsqueeze(1).broadcast_to([128, 2, 16]), ALU.mult,
                   [QTT.name, "c_seqcol"], [qtm.name])
                ts(ktm[0:n_, :], KTL[0:n_, :], C["seqsel"][0:n_, j:j + 1], None, ALU.mult, ALU.bypass, [KTL.name, "c_seqsel"], [ktm.name] + KTK[b])
                for e_ in range(2):
                    rs = slice(e_ * 64, (e_ + 1) * 64)
                    ob_ = 1 if e_ == 0 else 3
                    for p_ in range(2):
                        h = 2 * p_ + e_
                        mm(PS[ob_][0:n_, h * 128:(h + 1) * 128], qtm[rs, p_, :], s0b[rs, p_, :], False, False, [qtm.name, s0b.name], [f"ps{ob_}"])
                for p_ in range(2):
                    mm(PS[2][:, p_ * 256:(p_ + 1) * 256], ktm[0:n_, p_ * 128:(p_ + 1) * 128], VBF[0:n_, p_ * 256:(p_ + 1) * 256], True, True,
                       [ktm.name, VBF.name], ["ps2"])
                tt(s0f[:, :, :], s0f[:, :, :], ETS[:, :, j:j + 1].broadcast_to([128, 2, 128]), ALU.mult, [s0f.name, "ets", s0b.name], [s0f.name])
                for p_ in range(2):
                    for e_ in range(2):
                        rs = slice(e_ * 64, (e_ + 1) * 64)
                        stt(s0f[rs, p_, :], PS[2][rs, p_ * 256 + e_ * 128:p_ * 256 + (e_ + 1) * 128], ETS[rs, p_, j:j + 1], s0f[rs, p_, :],
                            ALU.mult, ALU.add, [s0f.name, "ets"], ["ps2", s0f.name])
                for e_ in range(2):
                    P.dma("sp", sso[l, j].rearrange("(a e) k v -> e k a v", e=2)[e_], s0f[e_ * 64:(e_ + 1) * 64, :, :], [s0f.name], [],
                          sem=f"st_ss{b}")
            acopy(OB[0:n_, i, :], PS[1][0:n_, :], [], ["ps1", f"ob{i}"])
            obv = OB[0:n_, i, :].rearrange("p (h d) -> p h d", h=4)
            tt(obv[:, 1::2, :], PS[3][0:n_, :].rearrange("p (h d) -> p h d", h=4)[:, 1::2, :], obv[:, 1::2, :], ALU.add, [f"ob{i}"], ["ps3", f"ob{i}"])

        P.dma("sp", s_src.ap(), S_F[:, :, :].rearrange("p a v -> p (a v)"), ["s_f"], ["s_src"], sem="sx")
        P.collective([s_src.ap()], [s_all.ap()], GROUPS, ["s_src"], ["s_all"])
        P.dma("sp", SA_F[:, :, :].rearrange("p a v -> p (a v)"), s_all.ap()[0:128, :], ["s_all"], ["sa_f"], sem="sx2")
        ts(SA_F[:, :, :], SA_F[:, :, :], C["flag"][:, 0:1], None, ALU.mult, ALU.bypass, ["sa_f", "c_flag"], ["sa_f"])
        vcopy(SA_B[:, :, :], SA_F[:, :, :], ["sa_f"], ["sa_b"])
        ETOT = ETOTs[0]
        act(ETOT[:, :], TOTA[:, :], AF.Exp, ["tota"], [ETOT.name], scale=-1.0 / 16)
        tt(SFIN[:, :, :], SA_F[:, :, :], ETOT[:, :].unsqueeze(2).broadcast_to([128, 2, 128]), ALU.mult, ["sa_f", ETOT.name], ["sfin"])
        tt(SFIN[:, :, :], SFIN[:, :, :], S_F[:, :, :], ALU.add, ["sfin", "s_f"], ["sfin"])
        for e_ in range(2):
            P.dma("sp", spo[l].rearrange("(a e) k v -> e k a v", e=2)[e_], SFIN[e_ * 64:(e_ + 1) * 64, :, :], ["sfin"], [], sem="st_s")

        if dbg == "m1bx":
            break
        P.barrier()
        ph.reset(keep2_off)
        WO = ph.alloc("WO", [128, 8, D], BF16)
        GB = ph.alloc("GB", [128, D], F32)
        WR = ph.alloc("WR", [128, 8, 512], BF16)
        HB3 = ph.alloc("hb3", [128, D], BF16)
        HT3 = ph.alloc("hT3", [128, 8, 128], BF16)
        ER = ph.alloc("er3", [128, 512], F32)
        P.dma("sp", WR[:, :, :].rearrange("p k c -> p (k c)"), wr_bf[l].ap(), [f"wr_bf{l}"], ["WR"], sem="win")
        SQ = ph.alloc("sq", [128, 512], F32)
        T5 = ph.alloc("t5", [128, 512], F32)
        OMB = ph.alloc("omb", [128, 512], BF16)
        OBT = ph.alloc("obt", [128, 4, 128], BF16)
        OAT = ph.alloc("oat", [128, 4, NS], BF16)
        TMP = ph.alloc("tmp", [128, D], F32)
        JK = ph.alloc("jk", [128, 512], BF16)
        SS4 = ph.alloc("ss4", [128, 8], F32)
        R4 = ph.alloc("r4", [128, 8], F32)
        SSM = ph.alloc("ssm", [128, 4], F32)
        RM = ph.alloc("rm", [128, 4], F32)
        P.dma("sp", WO[:, :, :].rearrange("p k c -> p (k c)"), wo_bf[l].ap(), [f"wo_bf{l}"], ["WO"], sem="wout")
        P.dma("sp", GB[:], g_mix_post[l].partition_broadcast(128), [], ["GB"], sem="lp")
        OBTs = [OBT, ph.alloc("obt1", [128, 4, 128], BF16)]

        def m3_front(i):
            n_ = tile_np(i)
            tcols = slice(i * 128, (i + 1) * 128)
            OBT = OBTs[i % 2]
            if i < NT:
                for h in range(4):
                    p_, e_ = h // 2, h % 2
                    rs = slice(e_ * 64, (e_ + 1) * 64)
                    cb = 0 if e_ == 0 else 4
                    mm(PS[cb][0:n_, p_ * 128:(p_ + 1) * 128], QH[rs, p_, tcols], SA_B[rs, p_, :], True, True, [f"QH{i}", "sa_b"], [f"ps{cb}"])
                obv = OB[0:n_, i, :].rearrange("p (h d) -> p h d", h=4)
                for e_ in range(2):
                    cb = 0 if e_ == 0 else 4
                    tt(obv[:, e_::2, :], PS[cb][0:n_, 0:256].rearrange("p (h d) -> p h d", h=2), obv[:, e_::2, :], ALU.add,
                       [f"ob{i}"], [f"ps{cb}", f"ob{i}"])
            else:
                for j in range(4):
                    tr(psb(0)[:, j * 128:j * 128 + n_], OMS[0:n_, j * 128:(j + 1) * 128], C["identb"][0:n_, 0:n_], ["oms_a", "c_identb"], ["ps0"])
                acopy(OAT[:, :, 0:n_], psb(0)[:, 0:512].rearrange("p (j t) -> p j t", j=4)[:, :, 0:n_], [], ["ps0", "oat"])
            norm_and_transpose(i, HB3, HT3, 6)
            for k in range(8):
                mm(PS[7][0:n_, :], HT3[:, k, 0:n_], WR[:, k, :], k == 0, k == 7, [HT3.name, "WR"], ["ps7"])
            act(ER[0:n_, :], PS[7][0:n_, :], AF.Exp, [], ["ps7", "er"], scale=-1.0)
            ts(ER[0:n_, :], ER[0:n_, :], 1.0, None, ALU.add, ALU.bypass, ["er"], ["er"])
            tt(SQ[0:n_, :], OB[0:n_, i, :], OB[0:n_, i, :], ALU.mult, [f"ob{i}"], ["sq"])
            P.op("dve", lambda e, n_=n_: e.reciprocal(out=ER[0:n_, :], in_=ER[0:n_, :]), ["er"], ["er"])
            P.op("dve", lambda e, n_=n_: e.tensor_reduce(out=SS4[0:n_, 0:4], in_=SQ[0:n_, :].rearrange("p (h d) -> p h d", h=4),
                                                         axis=AX.X, op=ALU.add), ["sq"], ["ss4"])
            act(SS4[0:n_, 4:8], SS4[0:n_, 0:4], AF.Ln, ["ss4", "epst"], ["ss4b"], scale=1.0 / 128, bias=EPST[0:n_, 0:1])
            act(R4[0:n_, 0:4], SS4[0:n_, 4:8], AF.Exp, ["ss4b"], ["r4"], scale=-0.5)
            for h in range(4):
                stt(T5[0:n_, h * 128:(h + 1) * 128], OB[0:n_, i, h * 128:(h + 1) * 128], R4[0:n_, h:h + 1], GGLA[0:n_, :],
                    ALU.mult, ALU.mult, [f"ob{i}", "r4", "ggla"], ["t5"])
            tt(ER[0:n_, :], PS[7][0:n_, :], ER[0:n_, :], ALU.mult, ["er"], ["ps7", "er"])
            tt(OMB[0:n_, :], T5[0:n_, :], ER[0:n_, :], ALU.mult, ["t5", "er"], ["omb"])
            for j in range(4):
                tr(psb(1)[:, j * 128:j * 128 + n_], OMB[0:n_, j * 128:(j + 1) * 128], C["identb"][0:n_, 0:n_], ["omb", "c_identb"], ["ps1"])
            acopy(OBT[:, :, 0:n_], psb(1)[:, 0:512].rearrange("p (j t) -> p j t", j=4)[:, :, 0:n_], [], ["ps1", OBT.name])

        def m3_back(i):
            n_ = tile_np(i)
            tcols = slice(i * 128, (i + 1) * 128)
            OBT = OBTs[i % 2]
            for c_ in range(2):
                mb = 2 + c_
                for j in range(4):
                    if i < NT:
                        mm(PS[mb][0:n_, :], OT[:, j, tcols], WO[:, j, c_ * 512:(c_ + 1) * 512], j == 0, False,
                           [f"OT{h_}" for h_ in (2 * j, 2 * j + 1)] + ["WO"], [f"ps{mb}"])
                    else:
                        mm(PS[mb][0:n_, :], OAT[:, j, 0:n_], WO[:, j, c_ * 512:(c_ + 1) * 512], j == 0, False, ["oat", "WO"], [f"ps{mb}"])
                for j in range(4):
                    mm(PS[mb][0:n_, :], OBT[:, j, 0:n_], WO[:, 4 + j, c_ * 512:(c_ + 1) * 512], False, j == 3, [OBT.name, "WO"], [f"ps{mb}"])
                act(JK[0:n_, :], PS[mb][0:n_, :], AF.Square, [], [f"ps{mb}", "jk", f"ssm{c_}"], accum_out=SSM[0:n_, c_:c_ + 1])
            tt(SSM[0:n_, 2:3], SSM[0:n_, 0:1], SSM[0:n_, 1:2], ALU.add, ["ssm0", "ssm1"], ["ssm2"])
            act(SSM[0:n_, 3:4], SSM[0:n_, 2:3], AF.Ln, ["ssm2", "epst"], ["ssm3"], scale=1.0 / D, bias=EPST[0:n_, 0:1])
            act(RM[0:n_, 0:1], SSM[0:n_, 3:4], AF.Exp, ["ssm3"], ["rm"], scale=-0.5)
            for c_ in range(2):
                stt(TMP[0:n_, c_ * 512:(c_ + 1) * 512], PS[2 + c_][0:n_, :], RM[0:n_, 0:1], GB[0:n_, c_ * 512:(c_ + 1) * 512],
                    ALU.mult, ALU.mult, ["rm", "GB"], [f"ps{2 + c_}", "tmp"])
            tt(X[0:n_, i, :], X[0:n_, i, :], TMP[0:n_, :], ALU.add, [f"x{i}", "tmp"], [f"x{i}"])

        m3_tiles = list(range(NT + (0 if "sgla" in skip else 1)))
        prev_back = None
        for i in m3_tiles + [None]:
            front = None
            if i is not None:
                P.begin_rec()
                m3_front(i)
                front = P.end_rec()
            P.play(front, prev_back)
            prev_back = None
            if i is not None:
                P.begin_rec()
                m3_back(i)
                prev_back = P.end_rec()
        if dbg == "m3":
            P.dma("sp", dbg_x2[:, :], X[0:NS, NT, :], [f"x{NT}"], [], sem="dbg")
            for i in range(NT):
                P.dma("sp", dbg_x[i * 128:(i + 1) * 128, :], X[:, i, :], [f"x{i}"], [], sem="dbg")
            break

        P.barrier()
        ph.reset()
        WDR = ph.alloc("WDR", [128, NBLK, D], BF16)
        YT = ph.alloc("YT", [128, NBLK, 528], BF16)
        HTG = ph.alloc("HTG", [128, 8, 530], BF16)
        HTS = ph.alloc("HTS", [128, 8, NS], BF16)
        WU = [ph.alloc(f"WU{b}", [128, 8, 256], BF16) for b in range(2)]
        UB = [[ph.alloc(f"ub{a}{b}", [128, 514], F32) for b in range(2)] for a in range(2)]
        CBF = [[ph.alloc(f"cb{a}{b}", [128, 512], F32) for b in range(2)] for a in range(2)]
        HBF = ph.alloc("hbf", [128, D], BF16)
        TMPF = ph.alloc("tmpf", [128, D], F32)
        GBF = ph.alloc("GBF", [128, D], F32)
        JKF = ph.alloc("jkf", [128, D], BF16)
        ULAST = ph.alloc("ulast", [128, 2, 2 * NBLK], F32)
        ULS = ph.alloc("uls", [128, 8, 2 * NBLK], F32)
        UBS = [ph.alloc(f"ubs{a}", [128, 4, 6], F32) for a in range(2)]
        CBS = [ph.alloc(f"cbs{a}", [128, 4, 4], F32) for a in range(2)]
        CAR = ph.alloc("car", [128, 8, 2], BF16)
        CARN = ph.alloc("carn", [128, 8, 2], BF16)
        LNF = ph.alloc("lnf", [128, NT + 1], F32)
        SSF = ph.alloc("ssf", [128, 4], F32)
        RMF = ph.alloc("rmf", [128, 4], F32)
        ULT = ph.alloc("ult", [128, 128], F32)

        P.dma("sp", WDR[:, :, :].rearrange("p b c -> p (b c)"), wd_bf[l].ap(), [f"wd_bf{l}"], ["WDR"], sem="wdn")
        P.dma("sp", GA[:], g_ffn_pre[l].partition_broadcast(128), [], ["GA"], sem="lp")
        P.dma("sp", GBF[:], g_ffn_post[l].partition_broadcast(128), [], ["GBF"], sem="lp")
        P.op("dve", lambda e: e.memset(SSQ[:, NT:NT + 1], 1.0), [], [f"ssq{NT}"])
        for i in range(NT + 1):
            n_ = tile_np(i)
            stt(JKF[0:n_, :], X[0:n_, i, :], 1.0, X[0:n_, i, :], ALU.mult, ALU.mult, [f"x{i}", f"ssq{i}"], ["jkf", f"ssq{i}"],
                accum_out=SSQ[0:n_, i:i + 1])
        allss = [f"ssq{i}" for i in range(NT + 1)]
        act(LNF[:, :], SSQ[:, :], AF.Ln, allss + ["epst"], ["lnf"], scale=1.0 / D, bias=EPST[:, 0:1])
        act(RSTD[:, :], LNF[:, :], AF.Exp, ["lnf"], ["rstd"], scale=-0.5)

        def ffn_norm_T(i, dst, c0, dkey):
            n_ = tile_np(i)
            stt(HBF[0:n_, :], X[0:n_, i, :], RSTD[0:n_, i:i + 1], GA[0:n_, :], ALU.mult, ALU.mult, [f"x{i}", "rstd", "GA"], ["hbf"])
            pv = psb(0)
            for k in range(8):
                tr(pv[:, k * 128:k * 128 + n_], HBF[0:n_, k * 128:(k + 1) * 128], C["identb"][0:n_, 0:n_], ["hbf", "c_identb"], ["ps0"])
            acopy(dst[:, :, c0:c0 + n_], pv.rearrange("p (k t) -> p k t", k=8)[:, :, 0:n_], [], ["ps0", dkey])

        ffn_norm_T(NT - 1, HTG, 2 + 384, "htg_x")
        P.dma("sp", h_src.ap().rearrange("p (k c) -> p k c", k=8), HTG[:, :, 2 + 510:2 + 512], ["htg_x"], ["h_src"], sem="hx")
        P.collective([h_src.ap()], [h_all.ap()], GROUPS, ["h_src"], ["h_all"])
        P.dma("sp", CAR[:, :, :], h_all.ap()[0:128, :].rearrange("p (k c) -> p k c", k=8), ["h_all"], ["car"], sem="hx2")
        ts(CAR[:, :, :], CAR[:, :, :], C["flag"][:, 0:1], None, ALU.mult, ALU.bypass, ["car", "c_flag"], ["car"])

        NG = 4
        wcnt = 0

        def group_norms(g):
            vcopy(HTG[:, :, 0:2], (CAR if g == 0 else CARN)[:, :, :], ["car" if g == 0 else "carn", "htg_x"], ["htg_c"] + [f"htg{t}" for t in range(4)])
            for t in range(4):
                ffn_norm_T(4 * g + t, HTG, 2 + t * 128, f"htg{t}")
            if g + 1 < NG:
                vcopy(CARN[:, :, :], HTG[:, :, 512:514], ["htg3"], ["carn"])
            if g == NG - 1:
                ffn_norm_T(NT, HTS, 0, "hts")

        group_norms(0)
        for g in range(NG):
            has_s = (g == NG - 1)
            pend_f2 = [None]
            for blk in range(NBLK):
                wu = WU[wcnt % 2]; wcnt += 1
                P.dma("sp", wu[:, :, :].rearrange("p k c -> p (k c)"), wup_bf[l].ap()[blk * 128:(blk + 1) * 128, :], [f"wupbf{l}"], [wu.name],
                      sem=wu.name)
                sl = blk % 2
                hkeys = ["htg_c"] + [f"htg{t}" for t in range(4)]
                for a in range(2):
                    pb = 1 + 2 * sl + a
                    for k in range(8):
                        mm(PS[pb][:, :], wu[:, k, a * 128:(a + 1) * 128], HTG[:, k, 2:514], k == 0, k == 7, [wu.name] + hkeys, [f"ps{pb}"])
                    cc = (2 * sl + a) * 2
                    if g == 0:
                        for k in range(8):
                            mm(PS[5][:, cc:cc + 2], wu[:, k, a * 128:(a + 1) * 128], HTG[:, k, 0:2], k == 0, k == 7, [wu.name] + hkeys, ["ps5"])
                    if has_s:
                        sc0 = 8 + (2 * sl + a) * 16
                        for k in range(8):
                            mm(PS[5][:, sc0:sc0 + NS], wu[:, k, a * 128:(a + 1) * 128], HTS[:, k, :], k == 0, k == 7, [wu.name, "hts"], ["ps5"])
                for a in range(2):
                    pb = 1 + 2 * sl + a
                    acopy(UB[a][sl][:, 2:514], PS[pb][:, :], [], [f"ps{pb}", UB[a][sl].name])
                for a in range(2):
                    cc = (2 * sl + a) * 2
                    ub = UB[a][sl]
                    if g == 0:
                        acopy(ub[:, 0:2], PS[5][:, cc:cc + 2], [], ["ps5", ub.name])
                    else:
                        P.op("pool", lambda e, ub=ub, a=a, blk=blk: e.tensor_copy(out=ub[:, 0:2], in_=ULAST[:, :, a * NBLK + blk]),
                             [f"ulast{a}_{blk}"], [ub.name])
                for a in range(2):
                    pb = 1 + 2 * sl + a
                    cw = CONVP[:, a * NBLK + blk, :]
                    act(CBF[a][sl][:, :], PS[pb][:, :], AF.Identity, ["convp"], [f"ps{pb}", CBF[a][sl].name], scale=cw[:, 2:3], bias=cw[:, 3:4])
                for a in range(2):
                    ub, cb = UB[a][sl], CBF[a][sl]
                    cw = CONVP[:, a * NBLK + blk, :]
                    stt(cb[:, :], ub[:, 1:513], cw[:, 1:2], cb[:, :], ALU.mult, ALU.add, [ub.name, "convp", cb.name], [cb.name])
                for a in range(2):
                    ub, cb = UB[a][sl], CBF[a][sl]
                    cw = CONVP[:, a * NBLK + blk, :]
                    stt(cb[:, :], ub[:, 0:512], cw[:, 0:1], cb[:, :], ALU.mult, ALU.add, [ub.name, "convp", cb.name], [cb.name])
                for a in range(2):
                    ub = UB[a][sl]
                    P.op("pool", lambda e, ub=ub, a=a, blk=blk: e.tensor_copy(out=ULAST[:, :, a * NBLK + blk], in_=ub[:, 512:514]),
                         [ub.name], [f"ulast{a}_{blk}"])
                def f2(sl=sl, blk=blk):
                    act(CBF[0][sl][:, :], CBF[0][sl][:, :], AF.Gelu_apprx_tanh, [CBF[0][sl].name], [CBF[0][sl].name])
                    tt(YT[:, blk, 0:512], CBF[0][sl][:, :], CBF[1][sl][:, :], ALU.mult, [CBF[0][sl].name, CBF[1][sl].name], [f"yt{blk}"])
                if pend_f2[0] is not None:
                    pend_f2[0]()
                pend_f2[0] = f2
                if has_s:
                    for a in range(2):
                        sc0 = 8 + (2 * sl + a) * 16
                        ubs, cbs = UBS[a], CBS[a]
                        cw = CONVP[:, a * NBLK + blk, :]
                        vcopy(ubs[:, :, 0:2], CPRE[:, a * NBLK + blk, :].rearrange("p (s r) -> p s r", s=4), ["cpre"], [ubs.name])
                        vcopy(ubs[:, :, 2:6], PS[5][:, sc0:sc0 + NS].rearrange("p (s t) -> p s t", s=4), [], ["ps5", ubs.name])
                        ts(cbs[:, :, :], ubs[:, :, 2:6], cw[:, 2:3], cw[:, 3:4], ALU.mult, ALU.add, [ubs.name, "convp"], [cbs.name])
                        stt(cbs[:, :, :], ubs[:, :, 1:5], cw[:, 1:2], cbs[:, :, :], ALU.mult, ALU.add, [ubs.name, "convp", cbs.name], [cbs.name])
                        stt(cbs[:, :, :], ubs[:, :, 0:4], cw[:, 0:1], cbs[:, :, :], ALU.mult, ALU.add, [ubs.name, "convp", cbs.name], [cbs.name])
                        vcopy(ULS[:, :, a * NBLK + blk].rearrange("p (s r) -> p s r", s=4), ubs[:, :, 4:6], [ubs.name], ["uls"])
                    act(CBS[0][:, :, :], CBS[0][:, :, :], AF.Gelu_apprx_tanh, [CBS[0].name], [CBS[0].name])
                    tt(YT[:, blk, 512:528].rearrange("p (s t) -> p s t", s=4), CBS[0][:, :, :], CBS[1][:, :, :], ALU.mult,
                       [CBS[0].name, CBS[1].name], [f"yts{blk}"])
            pend_f2[0]()
            P.begin_rec()
            tiles = [(4 * g + t, slice(t * 128, (t + 1) * 128)) for t in range(4)] + ([(NT, slice(512, 528))] if has_s else [])
            for ti_, (i, tc) in enumerate(tiles):
                n_ = tile_np(i)
                ykeys = [f"yt{b_}" for b_ in range(NBLK)] if i < NT else [f"yts{b_}" for b_ in range(NBLK)]
                fbs = (6, 7) if ti_ % 2 == 0 else (1, 2)
                for c_ in range(2):
                    fb = fbs[c_]
                    for blk in range(NBLK):
                        mm(PS[fb][0:n_, :], YT[:, blk, tc], WDR[:, blk, c_ * 512:(c_ + 1) * 512], blk == 0, blk == NBLK - 1,
                           ykeys + ["WDR"], [f"ps{fb}"])
                    acopy(TMPF[0:n_, c_ * 512:(c_ + 1) * 512], PS[fb][0:n_, :], [], [f"ps{fb}", f"tmpf{c_}"])
                stt(JKF[0:n_, :], TMPF[0:n_, :], 1.0, TMPF[0:n_, :], ALU.mult, ALU.mult, ["tmpf0", "tmpf1"], ["jkf", "ssf2"],
                    accum_out=SSF[0:n_, 2:3])
                act(SSF[0:n_, 3:4], SSF[0:n_, 2:3], AF.Ln, ["ssf2", "epst"], ["ssf3"], scale=1.0 / D, bias=EPST[0:n_, 0:1])
                act(RMF[0:n_, 0:1], SSF[0:n_, 3:4], AF.Exp, ["ssf3"], ["rmf"], scale=-0.5)
                stt(TMPF[0:n_, :], TMPF[0:n_, :], RMF[0:n_, 0:1], GBF[0:n_, :], ALU.mult, ALU.mult, ["tmpf0", "tmpf1", "rmf", "GBF"],
                    ["tmpf0", "tmpf1"])
                tt(X[0:n_, i, :], X[0:n_, i, :], TMPF[0:n_, :], ALU.add, [f"x{i}", "tmpf0", "tmpf1"], [f"x{i}"])
                if last:
                    if i < NT:
                        P.dma("sp", yp[i * 128:(i + 1) * 128, :], X[:, i, :], [f"x{i}"], [], sem="st_y")
                    else:
                        P.dma("sp", ys, X[0:NS, NT, :], [f"x{i}"], [], sem="st_y")
            p2ops = P.end_rec()
            nops = None
            if g + 1 < NG:
                P.begin_rec()
                group_norms(g + 1)
                nops = P.end_rec()
            P.play(p2ops, nops)
        tr(PS[1][0:88, 0:128], ULAST[:, :, :].rearrange("p r b -> p (r b)"), C["identf"][:, :],
           [f"ulast{a_}_{b_}" for a_ in range(2) for b_ in range(NBLK)] + ["c_identf"], ["ps1"])
        acopy(ULT[0:88, :], PS[1][0:88, 0:128], [], ["ps1", "ult"])
        P.dma("sp", cpo[l].rearrange("r (b p) -> (r b) p", p=128), ULT[0:88, :], ["ult"], [], sem="st_c")
        ulsf = ULS[:, :, :].rearrange("p q b -> p (q b)")
        for q_ in range(3):
            rows = 128 if q_ < 2 else 96
            tr(PS[2][0:rows, 0:128], ulsf[:, q_ * 128:q_ * 128 + rows], C["identf"][:, :], ["uls", "c_identf"], ["ps2"])
            acopy(ULT[0:rows, :], PS[2][0:rows, 0:128], ["ult"], ["ps2", "ult"])
            P.dma("sp", cso[l].rearrange("s r (b p) -> (s r b) p", p=128)[q_ * 128:q_ * 128 + rows, :], ULT[0:rows, :], ["ult"], [], sem="st_c")
        if dbg == "ffn":
            for i in range(NT):
                P.dma("sp", dbg_x[i * 128:(i + 1) * 128, :], X[:, i, :], [f"x{i}"], [], sem="dbg")
            break

    P.barrier(final=True)
    if P.unknown:
        print("WARNING: keys read but never written:", sorted(P.unknown))
    P.emit()
    return nc


_NC_CACHE = {}


def _in_maps(inputs):
    f = lambda a: np.ascontiguousarray(np.asarray(a, dtype=np.float32))
    maps = []
    shared = {k: f(inputs[k]) for k in ("g_mix_pre", "g_mix_post", "g_ffn_pre", "g_ffn_post", "w_in", "w_gate2", "b_gate",
                                         "g_gla", "w_out", "w_up", "conv_w", "conv_b", "w_down")}
    x_prompt, x_sample = f(inputs["x_prompt"]), f(inputs["x_sample"])
    ckw, cvw = f(inputs["cache_k_win"]), f(inputs["cache_v_win"])
    sgl, sfc = f(inputs["state_gla"]), f(inputs["state_ffn_conv"])
    consts = [_consts(0), _consts(1)]
    for c in range(8):
        s, half = c // 2, c % 2
        m = dict(shared)
        m["xp"] = np.ascontiguousarray(x_prompt[s, half * TL:(half + 1) * TL, :])
        m["xs"] = np.ascontiguousarray(x_sample[4 * c:4 * c + 4].reshape(NS, D))
        m["ck"] = np.ascontiguousarray(ckw[:, 4 * c:4 * c + 4].reshape(DEPTH, 4, 2048, 512))
        m["cv"] = np.ascontiguousarray(cvw[:, 4 * c:4 * c + 4].reshape(DEPTH, 4, 2048, 512))
        m["sg"] = np.ascontiguousarray(sgl[:, 4 * c:4 * c + 4])
        m["sc"] = np.ascontiguousarray(sfc[:, 4 * c:4 * c + 4])
        for n, v in consts[half].items():
            m["c_" + n] = v
        maps.append(m)
    return maps


def kernel(**inputs):
    if "nc" not in _NC_CACHE:
        _NC_CACHE["nc"] = build_program()
    nc = _NC_CACHE["nc"]
    res = run_bass_kernel_spmd(nc, _in_maps(inputs), core_ids=list(range(8)))
    R = res.results
    B, T = 4, 4096
    y_prompt = np.zeros((B, T, D), np.float32)
    y_sample = np.zeros((32, 4, D), np.float32)
    nk = np.zeros((DEPTH, B, TL, 8, 64), np.float32); nv = np.zeros_like(nk)
    nsp = np.zeros((DEPTH, B, 4, 64, 128), np.float32)
    ncp = np.zeros((DEPTH, B, 2, 2 * DFF), np.float32)
    ksn = np.zeros((DEPTH, 32, 4, 8, 64), np.float32); vsn = np.zeros_like(ksn)
    ssn = np.zeros((DEPTH, 32, 4, 64, 128), np.float32)
    csn = np.zeros((DEPTH, 32, 2, 2 * DFF), np.float32)
    for c in range(8):
        s, half = c // 2, c % 2
        r = R[c]
        y_prompt[s, half * TL:(half + 1) * TL] = r["yp"]
        y_sample[4 * c:4 * c + 4] = r["ys"].reshape(4, 4, D)
        if half == 1:
            nk[:, s] = r["kp"].reshape(DEPTH, TL, 8, 64); nv[:, s] = r["vp"].reshape(DEPTH, TL, 8, 64)
            nsp[:, s] = r["spo"]; ncp[:, s] = r["cpo"]
        ksn[:, 4 * c:4 * c + 4] = r["kso"].reshape(DEPTH, 4, 4, 8, 64)
        vsn[:, 4 * c:4 * c + 4] = r["vso"].reshape(DEPTH, 4, 4, 8, 64)
        ssn[:, 4 * c:4 * c + 4] = r["sso"]; csn[:, 4 * c:4 * c + 4] = r["cso"]
    return (y_prompt, y_sample, nk, nv, nsp, ncp, ksn, vsn, ssn, csn)
```

```python
import numpy as np
import ml_dtypes
import concourse.bass as bass
import concourse.mybir as mybir
from concourse.bass_utils import run_bass_kernel_spmd

F32, BF16 = mybir.dt.float32, mybir.dt.bfloat16
AF = mybir.ActivationFunctionType
ALU = mybir.AluOpType
AX = mybir.AxisListType

D = 1024
TL = 2048
NT = TL // 128
NS = 16
DEPTH = 2
DFF = 2816
NBLK = DFF // 128
INC = 3088
EPS = 1e-6
PAST = 16384
NEG = -30000.0
DEBUG_LAYERS = None


class Prog:
    ENGS = ("pe", "act", "dve", "pool", "sp")

    def __init__(self, nc):
        self.nc = nc
        self.q = {e: [] for e in self.ENGS}
        self.cnt = {e: 0 for e in self.ENGS}
        self.sem = {e: nc.alloc_semaphore("c_" + e) for e in ("pe", "act", "dve", "pool")}
        self.dsem = {}
        self.seen = {e: {} for e in self.ENGS}
        self.lastw = {}
        self.readers = {}
        self.n_wait = 0
        self.log = None
        self.ever = set()
        self.unknown = set()
        self.rec = None
        self.pe_last = None
        self.pend = []
        self.background = set()

    def _dsem(self, name):
        if name not in self.dsem:
            self.dsem[name] = [self.nc.alloc_semaphore("d_" + name), 0]
        return self.dsem[name]

    def _handle(self, tok):
        return self.sem[tok[1]] if tok[0] == "e" else self.dsem[tok[1]][0]

    def _wait(self, eng, tok):
        key = (tok[0], tok[1])
        if self.seen[eng].get(key, 0) >= tok[2]:
            return
        self.seen[eng][key] = tok[2]
        if self.log is not None:
            self.log.append(f"    {eng} WAIT {tok}")
        h, v = self._handle(tok), tok[2]
        self.pend.append((h, v))
        self.n_wait += 1

    def _flush_waits(self, eng, keep_last):
        pend, self.pend = self.pend, []
        last = pend.pop() if (keep_last and pend) else None
        for (h, v) in pend:
            self.q[eng].append(lambda e, h=h, v=v: e.wait_ge(h, v))
        return last

    def _deps(self, eng, reads, writes, is_dma):
        toks = []
        for r in reads:
            if r not in self.ever:
                self.unknown.add(r)
            t = self.lastw.get(r)
            if t is not None:
                toks.append(t)
        for w in writes:
            t = self.lastw.get(w)
            if t is not None and (is_dma or not (t[0] == "e" and t[1] == eng)):
                toks.append(t)
            for t in self.readers.get(w, ()):
                if is_dma or not (t[0] == "e" and t[1] == eng):
                    toks.append(t)
        best = {}
        for t in toks:
            k = (t[0], t[1])
            if best.get(k, 0) < t[2]:
                best[k] = t[2]
        for k, v in best.items():
            self._wait(eng, (k[0], k[1], v))

    def _commit(self, tok, reads, writes):
        self.ever.update(writes)
        for w in writes:
            self.lastw[w] = tok
            self.readers[w] = []
        for r in reads:
            if r not in writes:
                lst = self.readers.setdefault(r, [])
                for i_, t in enumerate(lst):
                    if t[0] == tok[0] and t[1] == tok[1]:
                        if t[2] < tok[2]:
                            lst[i_] = tok
                        break
                else:
                    lst.append(tok)

    def begin_rec(self):
        self.rec = []

    def end_rec(self):
        r, self.rec = self.rec, None
        return r

    def play(self, *lists):
        lists = [l for l in lists if l]
        idx = [0] * len(lists)
        total = sum(len(l) for l in lists)
        for _ in range(total):
            best, bi = None, -1
            for li, l in enumerate(lists):
                if idx[li] < len(l):
                    frac = (idx[li] + 0.5) / len(l)
                    if best is None or frac < best:
                        best, bi = frac, li
            kind, args, kw = lists[bi][idx[bi]]
            idx[bi] += 1
            getattr(self, kind)(*args, **kw)

    def op(self, eng, fn, reads=(), writes=(), pe=None, simple=False):
        if self.rec is not None:
            self.rec.append(("op", (eng, fn, reads, writes), {"pe": pe, "simple": simple}))
            return
        if pe is not None:
            g = set(range(pe[0] // 32, (pe[0] + pe[1] - 1) // 32 + 1))
            b = set(k for k in writes if k.startswith("ps"))
            if self.pe_last is not None and not (g & self.pe_last[0]) and (b & self.pe_last[1]):
                raise RuntimeError(f"PE row-group hazard: disjoint row groups {g}/{self.pe_last[0]} share PSUM bank {b}")
            self.pe_last = (g, b)
        self._deps(eng, reads, writes, False)
        last = self._flush_waits(eng, ATTACH_WAITS and simple)
        self.cnt[eng] += 1
        if self.log is not None:
            self.log.append(f"{eng} #{self.cnt[eng]} r={list(reads)} w={list(writes)}")
        sem = self.sem[eng]
        if last is None:
            self.q[eng].append(lambda e, fn=fn, sem=sem: fn(e).then_inc(sem, 1))
        else:
            def emit_(e, fn=fn, sem=sem, last=last):
                ins = fn(e)
                ins.wait_op(last[0], last[1], "sem-ge")
                ins.then_inc(sem, 1)
            self.q[eng].append(emit_)
        self._commit(("e", eng, self.cnt[eng]), reads, writes)

    def dma(self, eng, out, in_, reads=(), writes=(), sem="misc", **kw):
        if self.rec is not None:
            self.rec.append(("dma", (eng, out, in_, reads, writes, sem), kw))
            return
        self._deps(eng, reads, writes, True)
        self._flush_waits(eng, False)
        ds = self._dsem(sem)
        ds[1] += 16
        h = ds[0]
        self.q[eng].append(lambda e, out=out, in_=in_, h=h, kw=kw: e.dma_start(out=out, in_=in_, **kw).then_inc(h, 16))
        self._commit(("d", sem, ds[1]), reads, writes)

    def collective(self, ins, outs, groups, reads=(), writes=()):
        self._deps("pool", reads, writes, True)
        self._flush_waits("pool", False)
        ds = self._dsem("cc")
        ds[1] += 1
        h = ds[0]
        self.q["pool"].append(lambda e, ins=ins, outs=outs, h=h: e.collective_compute(
            "AllGather", ALU.bypass, replica_groups=groups, ins=ins, outs=outs).then_inc(h, 1))
        self._commit(("d", "cc", ds[1]), reads, writes)

    def barrier(self, final=False):
        toks = [("e", e, self.cnt[e]) for e in self.sem if self.cnt[e] > 0]
        toks += [("d", n, v[1]) for n, v in self.dsem.items() if v[1] > 0 and (final or n not in self.background)]
        for eng in self.ENGS:
            for t in toks:
                if not (t[0] == "e" and t[1] == eng):
                    self._wait(eng, t)
            self._flush_waits(eng, False)
        keep = {k: t for k, t in self.lastw.items() if t[0] == "d" and t[1] in self.background} if not final else {}
        self.lastw.clear()
        self.lastw.update(keep)
        self.readers.clear()

    def emit(self):
        nc = self.nc
        with nc.Block() as block:
            @block.tensor
            def _(e):
                for f in self.q["pe"]:
                    f(e)

            @block.scalar
            def _(e):
                for f in self.q["act"]:
                    f(e)

            @block.vector
            def _(e):
                for f in self.q["dve"]:
                    f(e)

            @block.gpsimd
            def _(e):
                for f in self.q["pool"]:
                    f(e)

            @block.sync
            def _(e):
                for f in self.q["sp"]:
                    f(e)


DBG_T = {}
ATTACH_WAITS = True
PLOG = []
LOG_ON = False


class Arena:
    def __init__(self, nc, base, limit):
        self.nc, self.base, self.limit, self.off, self.n = nc, base, limit, base, 0
        self.offs = {}

    def reset(self, off=None):
        self.off = self.base if off is None else off

    def alloc(self, name, shape, dtype):
        esz = 4 if dtype == F32 else 2
        size = esz
        for s in shape[1:]:
            size *= s
        size = (size + 63) // 64 * 64
        assert self.off + size <= self.limit, (name, self.off, size, self.limit)
        self.n += 1
        t = self.nc.alloc_sbuf_tensor_at(f"{name}_{self.n}", list(shape), dtype, offset=self.off)
        self.off += size
        DBG_T[name] = t.name
        self.offs[t.name] = self.off - size
        return t

    def alias(self, name, shape, dtype, base, byte_off=0):
        self.n += 1
        return self.nc.alloc_sbuf_tensor_at(f"{name}_{self.n}", list(shape), dtype, offset=self.offs[base.name] + byte_off)


def _rope_tables(pos):
    half = 8
    inv = (np.float32(500000.0) ** (-(np.arange(half, dtype=np.float32) / np.float32(half)))).astype(np.float32)
    ang = pos.astype(np.float32)[:, None] * inv[None, :]
    c, s = np.cos(ang).astype(np.float32), np.sin(ang).astype(np.float32)
    k = np.concatenate([c, c, s, s], axis=1)
    return (k * np.float32(0.125)).astype(np.float32), k.astype(np.float32)


def _consts(half):
    bf = ml_dtypes.bfloat16
    c = {}
    c["identb"] = np.eye(128, dtype=np.float32).astype(bf)
    c["identf"] = np.eye(128, dtype=np.float32)
    k = np.arange(128)[:, None]
    q = np.arange(128)[None, :]
    c["ucs"] = (k <= q).astype(np.float32)
    c["onesf"] = np.ones((128, 128), np.float32)
    c["onesb"] = np.ones((128, 64), np.float32).astype(bf)
    mdiag = np.where(k <= q, 0.0, NEG).astype(np.float32)
    mprev = np.where(k >= q, 0.0, NEG).astype(np.float32)
    mpre = mprev if half == 1 else np.full((128, 128), NEG, np.float32)
    c["ma"] = (np.concatenate([mprev, mdiag], axis=1) == 0.0).astype(np.float32).astype(bf)
    c["mb"] = (np.concatenate([mpre, mdiag], axis=1) == 0.0).astype(np.float32).astype(bf)
    pos = half * TL + np.arange(TL)
    rq, rk = _rope_tables(pos)
    c["ropeq"] = np.ascontiguousarray(rq.reshape(NT, 128, 32).transpose(1, 0, 2))
    c["ropek"] = np.ascontiguousarray(rk.reshape(NT, 128, 32).transpose(1, 0, 2))
    sq, sk = _rope_tables(PAST + (np.arange(NS) % 4))
    c["ropesq"], c["ropesk"] = sq, sk
    c["flag"] = np.full((128, 1), float(half), np.float32)
    sm = np.full((128, 4, 9, 16), NEG, np.float32)
    m = np.arange(128)
    for j in range(4):
        for i in range(4):
            qq = 4 * j + i
            sm[m >= i, j, 0, qq] = 0.0
            sm[:, j, 1 + i, qq] = 0.0
            sm[:, j, 5 + i, qq] = 0.0
    c["smask"] = sm.reshape(128, 4 * 9 * 16).astype(bf)
    t = np.arange(16)
    same = (t[:, None] // 4) == (t[None, :] // 4)
    c["mnew"] = (same * ((t[:, None] < t[None, :]) * 1.0 + (t[:, None] == t[None, :]) * 3.0)).astype(np.float32)
    c["ucs_s"] = (same & (t[:, None] <= t[None, :])).astype(np.float32)
    c["seqsel"] = ((t[:, None] // 4) == np.arange(4)[None, :]).astype(np.float32)
    sc = np.zeros((128, 4, 16), np.float32)
    for j in range(4):
        sc[:, j, 4 * j:4 * j + 4] = 1.0
    c["seqcol"] = sc.reshape(128, 64).astype(bf)
    return c


CONST_SPECS = [
    ("identb", [128, 128], BF16), ("identf", [128, 128], F32), ("ucs", [128, 128], F32),
    ("onesf", [128, 128], F32), ("onesb", [128, 64], BF16), ("ma", [128, 256], BF16), ("mb", [128, 256], BF16),
    ("ropeq", [128, NT, 32], F32), ("ropek", [128, NT, 32], F32), ("ropesq", [NS, 32], F32), ("ropesk", [NS, 32], F32),
    ("flag", [128, 1], F32), ("smask", [128, 576], BF16), ("mnew", [16, 16], F32), ("ucs_s", [16, 16], F32),
    ("seqsel", [16, 4], F32), ("seqcol", [128, 64], BF16),
]


def build_program(n_layers=DEPTH, dbg=False, ncores=8, skip=()):
    nc = bass.Bass("TRN2", target_bir_lowering=False)
    P = Prog(nc)
    if dbg and LOG_ON:
        P.log = PLOG

    def din(name, shape, dt=F32):
        return nc.dram_tensor(name, list(shape), dt, kind="ExternalInput").ap()

    def dout(name, shape, dt=F32):
        return nc.dram_tensor(name, list(shape), dt, kind="ExternalOutput").ap()

    xp = din("xp", [TL, D]); xs = din("xs", [NS, D])
    ck = din("ck", [DEPTH, 4, 2048, 512]); cv = din("cv", [DEPTH, 4, 2048, 512])
    sg_in = din("sg", [DEPTH, 4, 4, 64, 128]); sc_in = din("sc", [DEPTH, 4, 2, 2 * DFF])
    g_mix_pre = din("g_mix_pre", [DEPTH, D]); g_mix_post = din("g_mix_post", [DEPTH, D])
    g_ffn_pre = din("g_ffn_pre", [DEPTH, D]); g_ffn_post = din("g_ffn_post", [DEPTH, D])
    w_in = din("w_in", [DEPTH, D, INC]); w_gate2 = din("w_gate2", [DEPTH, 16, 256]); b_gate = din("b_gate", [DEPTH, 256])
    g_gla = din("g_gla", [DEPTH, 128]); w_out = din("w_out", [DEPTH, D, D]); w_up = din("w_up", [DEPTH, D, 2 * DFF])
    conv_w = din("conv_w", [DEPTH, 3, 2 * DFF]); conv_b = din("conv_b", [DEPTH, 2 * DFF]); w_down = din("w_down", [DEPTH, DFF, D])
    cin = {n: din("c_" + n, s, d) for n, s, d in CONST_SPECS}

    yp = dout("yp", [TL, D]); ys = dout("ys", [NS, D])
    kp = dout("kp", [DEPTH, TL, 512]); vp = dout("vp", [DEPTH, TL, 512])
    spo = dout("spo", [DEPTH, 4, 64, 128]); cpo = dout("cpo", [DEPTH, 2, 2 * DFF])
    kso = dout("kso", [DEPTH, NS, 512]); vso = dout("vso", [DEPTH, NS, 512])
    sso = dout("sso", [DEPTH, 4, 4, 64, 128]); cso = dout("cso", [DEPTH, 4, 2, 2 * DFF])
    if dbg:
        dbg_ot = dout("dbg_ot", [512, TL], BF16)
        dbg_x = dout("dbg_x", [TL, D])
        dbg_x2 = dout("dbg_x2", [NS, D])

    k_src = nc.dram_tensor("k_src", [512, TL], BF16)
    v_src = nc.dram_tensor("v_src", [512, TL], BF16)
    k_all = nc.dram_tensor("k_all", [1024, TL], BF16)
    v_all = nc.dram_tensor("v_all", [1024, TL], BF16)
    s_src = nc.dram_tensor("s_src", [128, 256], F32)
    s_all = nc.dram_tensor("s_all", [2 * 128, 256], F32)
    h_src = nc.dram_tensor("h_src", [128, 16], BF16)
    h_all = nc.dram_tensor("h_all", [2 * 128, 16], BF16)
    wup_bf = [nc.dram_tensor(f"wup_bf{l_}", [NBLK * 128, 8 * 256], BF16) for l_ in range(DEPTH)]
    wa_bf = [nc.dram_tensor(f"wa_bf{l_}", [128, 8 * 1536], BF16) for l_ in range(DEPTH)]
    wb_bf = [nc.dram_tensor(f"wb_bf{l_}", [128, 8 * 1040], BF16) for l_ in range(DEPTH)]
    wr_bf = [nc.dram_tensor(f"wr_bf{l_}", [128, 8 * 512], BF16) for l_ in range(DEPTH)]
    wo_bf = [nc.dram_tensor(f"wo_bf{l_}", [128, 8 * D], BF16) for l_ in range(DEPTH)]
    wd_bf = [nc.dram_tensor(f"wd_bf{l_}", [128, NBLK * D], BF16) for l_ in range(DEPTH)]
    GROUPS = [[2 * g_, 2 * g_ + 1] for g_ in range(ncores // 2)]

    B0 = (nc.sbuf_base + 63) // 64 * 64
    TOP = nc.sbuf_top // 64 * 64
    pers = Arena(nc, B0, TOP)
    X = pers.alloc("X", [128, NT + 1, D], F32)
    C = {n: pers.alloc("c_" + n, s, d) for n, s, d in CONST_SPECS}
    RSTD = pers.alloc("rstd", [128, NT + 1], F32)
    SSQ = pers.alloc("ssq", [128, NT + 1], F32)
    EPST = pers.alloc("epst", [128, 1], F32)
    GA = pers.alloc("GA", [128, D], F32)
    GGLA = pers.alloc("ggla", [128, 128], F32)
    BGATE = pers.alloc("bgate", [128, 256], F32)
    WG2 = pers.alloc("wg2", [16, 256], BF16)
    CONVP = pers.alloc("convp", [128, 2 * NBLK, 4], F32)
    CPRE = pers.alloc("cpre", [128, 2 * NBLK, 8], F32)
    ph = Arena(nc, pers.off, TOP)

    PS = [nc.alloc_psum_tensor(f"ps{i}", [128, 512], F32) for i in range(8)]

    def psb(i):
        return PS[i][:, :].bitcast(BF16)

    def act(out, in_, func, r, w, **kw):
        P.op("act", lambda e: e.activation(out=out, in_=in_, func=func, **kw), r, w, simple=("accum_out" not in kw))

    def tt(out, in0, in1, op, r, w, eng="dve"):
        P.op(eng, lambda e: e.tensor_tensor(out=out, in0=in0, in1=in1, op=op), r, w, simple=True)

    def stt(out, in0, scalar, in1, op0, op1, r, w, accum_out=None):
        P.op("dve", lambda e: e.scalar_tensor_tensor(out=out, in0=in0, scalar=scalar, in1=in1, op0=op0, op1=op1,
                                                      accum_out=accum_out), r, w, simple=(accum_out is None))

    def ts(out, in0, s1, s2, op0, op1, r, w):
        P.op("dve", lambda e: e.tensor_scalar(out=out, in0=in0, scalar1=s1, scalar2=s2, op0=op0, op1=op1), r, w, simple=True)

    def vcopy(out, in_, r, w):
        P.op("dve", lambda e: e.tensor_copy(out=out, in_=in_), r, w, simple=True)

    def acopy(out, in_, r, w):
        P.op("act", lambda e: e.copy(out=out, in_=in_), r, w, simple=True)

    def mm(out, lhsT, rhs, start, stop, r, w):
        P.op("pe", lambda e: e.matmul(out, lhsT, rhs, start=start, stop=stop, skip_group_check=True), r, w,
             pe=(lhsT.start_partition(), lhsT.partition_size()), simple=True)

    def tr(out, in_, ident, r, w):
        P.op("pe", lambda e: e.transpose(out, in_, ident), r, w, pe=(in_.start_partition(), in_.partition_size()), simple=True)

    def rsqrt_cols(dst, src, n, scale, r, w, tmp):
        act(tmp, src, AF.Ln, r, [w + "_t"], scale=scale, bias=EPST[0:n, 0:1])
        act(dst, tmp, AF.Exp, [w + "_t"], [w], scale=-0.5)

    for n, s, d in CONST_SPECS:
        P.dma("sp", C[n][:], cin[n], [], ["c_" + n], sem="const")
    P.op("dve", lambda e: e.memset(EPST[:], EPS), [], ["epst"])
    P.op("dve", lambda e: e.memset(X[:, NT, :], 0.0), [], ["x16"])
    for i in range(NT):
        P.dma("sp", X[:, i, :], xp[i * 128:(i + 1) * 128, :], [], [f"x{i}"], sem="xin")
    P.dma("sp", X[0:NS, NT, :], xs, [], ["x16"], sem="xin")
    ALLC = ["c_" + n for n, _, _ in CONST_SPECS] + ["epst"]

    def convert_weights(l_, which):
        win_ = w_in[l_].rearrange("(k p) c -> p k c", p=128)
        def bsem(nm):
            P.background.add(f"bg{nm}{l_}")
            return f"bg{nm}{l_}"
        if which == "a":
            P.dma("pool", wa_bf[l_].ap().rearrange("p (k c) -> p k c", k=8), win_[:, :, 0:1536], [], [f"wa_bf{l_}"], sem=bsem("wa"))
            return
        wbv = wb_bf[l_].ap().rearrange("p (k c) -> p k c", k=8)
        P.dma("pool", wbv[:, :, 0:1024], win_[:, :, 1536:2560], [], [f"wb_bf{l_}"], sem=bsem("wb"))
        P.dma("pool", wbv[:, :, 1024:1040], win_[:, :, 3072:3088], [], [f"wb_bf{l_}"], sem=bsem("wb"))
        P.dma("pool", wo_bf[l_].ap().rearrange("p (k c) -> p k c", k=8), w_out[l_].rearrange("(k p) c -> p k c", p=128), [], [f"wo_bf{l_}"], sem=bsem("wo"))
        P.dma("pool", wr_bf[l_].ap().rearrange("p (k c) -> p k c", k=8), win_[:, :, 2560:3072], [], [f"wr_bf{l_}"], sem=bsem("wr"))
        P.dma("pool", wd_bf[l_].ap().rearrange("p (b c) -> p b c", b=NBLK), w_down[l_].rearrange("(b p) c -> p b c", p=128), [], [f"wd_bf{l_}"], sem=bsem("wd"))
        wsrc_ = w_up[l_].rearrange("(k p) c -> p k c", p=128)
        for blk in range(NBLK):
            dst_ = wup_bf[l_].ap()[blk * 128:(blk + 1) * 128, :].rearrange("p (k c) -> p k c", k=8)
            P.dma("pool", dst_[:, :, 0:128], wsrc_[:, :, blk * 128:(blk + 1) * 128], [], [f"wupbf{l_}"], sem=bsem("wu"))
            P.dma("pool", dst_[:, :, 128:256], wsrc_[:, :, DFF + blk * 128:DFF + (blk + 1) * 128], [], [f"wupbf{l_}"], sem=bsem("wu"))

    def tile_np(i):
        return 128 if i < NT else NS

    for l in range(n_layers):
        last = (l == n_layers - 1)
        P.barrier()
        ph.reset()
        OT = ph.alloc("OT", [128, 4, TL], BF16)
        SQT = ph.alloc("sqT", [128, 4, NS], BF16)
        SKT = ph.alloc("skT", [128, 4, NS], BF16)
        SVB = ph.alloc("svb", [NS, 512], BF16)
        OMS = ph.alloc("oms", [NS, D], BF16)
        ot_end = ph.off
        QT = ph.alloc("QT", [128, 4, TL], BF16)
        KT = ph.alloc("KT", [128, 4, TL], BF16)
        VT = ph.alloc("VT", [128, 4, TL], BF16)
        keep_off = ph.off
        WA = ph.alloc("WA", [128, 8, 1536], BF16)
        HB = [ph.alloc(f"hb{b}", [128, D], BF16) for b in range(2)]
        HT = [ph.alloc(f"hT{b}", [128, 8, 128], BF16) for b in range(2)]
        JUNK = ph.alloc("junk", [128, D], BF16)
        KF = ph.alloc("kf", [128, 512], F32)
        VF = ph.alloc("vf", [128, 512], F32)
        QB = ph.alloc("qb", [128, 512], BF16)
        KB = ph.alloc("kb", [128, 512], BF16)
        VB = ph.alloc("vb", [128, 512], BF16)
        T1 = ph.alloc("t1", [128, 8, 16], F32)
        T2 = ph.alloc("t2", [128, 8, 16], F32)
        LNT = ph.alloc("lnt", [128, NT + 1], F32)

        if skip:
            for t_ in (OT, QT, KT, VT):
                P.op("dve", lambda e, t_=t_: e.memset(t_[:, :, :], 0.0), [], [t_.name] + [f"{t_.name}{i}" for i in range(NT)])
        if l == 0:
            for cg in range(3):
                P.dma("pool", WA[:, :, cg * 512:(cg + 1) * 512], w_in[l].rearrange("(k p) c -> p k c", p=128)[:, :, cg * 512:(cg + 1) * 512],
                      [], [f"WA{cg}"], sem=f"win{cg}")
        else:
            P.dma("sp", WA[:, :, :].rearrange("p k c -> p (k c)"), wa_bf[l].ap(), [f"wa_bf{l}"], ["WA0", "WA1", "WA2"], sem="win")
        P.dma("sp", GA[:], g_mix_pre[l].partition_broadcast(128), [], ["GA"], sem="lp")

        for r_ in range(3):
            P.dma("sp", CONVP[:, :, r_], conv_w[l, r_].rearrange("(b p) -> p b", p=128), [], ["convp"], sem="lp3",
                  allow_slow_non_contiguous=True)
        P.dma("sp", CONVP[:, :, 3], conv_b[l].rearrange("(b p) -> p b", p=128), [], ["convp"], sem="lp3",
              allow_slow_non_contiguous=True)
        for s_ in range(4):
            for r_ in range(2):
                P.dma("sp", CPRE[:, :, s_ * 2 + r_], sc_in[l, s_, r_].rearrange("(b p) -> p b", p=128), [], ["cpre"], sem="lp3",
                      allow_slow_non_contiguous=True)

        P.op("dve", lambda e: e.memset(SSQ[:, NT:NT + 1], 1.0), [], [f"ssq{NT}"])
        for i in range(NT + 1):
            n_ = tile_np(i)
            stt(JUNK[0:n_, :], X[0:n_, i, :], 1.0, X[0:n_, i, :], ALU.mult, ALU.mult, [f"x{i}", f"ssq{i}"], ["junk", f"ssq{i}"],
                accum_out=SSQ[0:n_, i:i + 1])
        if True:
            allss = [f"ssq{i}" for i in range(NT + 1)]
            act(LNT[:, :], SSQ[:, :], AF.Ln, allss + ["epst"], ["lnt"], scale=1.0 / D, bias=EPST[:, 0:1])
            act(RSTD[:, :], LNT[:, :], AF.Exp, ["lnt"], ["rstd"], scale=-0.5)

        def norm_and_transpose(i, hb, hT, pbank, gkey="GA"):
            n_ = tile_np(i)
            stt(hb[0:n_, :], X[0:n_, i, :], RSTD[0:n_, i:i + 1], GA[0:n_, :], ALU.mult, ALU.mult,
                [f"x{i}", "rstd", gkey], [hb.name])
            pv = psb(pbank)
            for k in range(8):
                tr(pv[:, k * 128:k * 128 + n_], hb[0:n_, k * 128:(k + 1) * 128], C["identb"][0:n_, 0:n_],
                   [hb.name, "c_identb"], [f"ps{pbank}"])
            src = pv.rearrange("p (k t) -> p k t", k=8)[:, :, 0:n_]
            acopy(hT[:, :, 0:n_], src, [], [f"ps{pbank}", hT.name])

        def rope(ps_ap, tab, out_ap, n_, rkeys, wkeys, tabkey):
            pv = ps_ap.rearrange("p (h d) -> p h d", h=8)[:, :, 0:16]
            ov = out_ap.rearrange("p (h d) -> p h d", h=8)
            cc = tab[:, 0:16].unsqueeze(1).broadcast_to([n_, 8, 16])
            ss = tab[:, 16:32].unsqueeze(1).broadcast_to([n_, 8, 16])
            tt(T1[0:n_], pv, cc, ALU.mult, rkeys + [tabkey], ["t1"] + [k for k in wkeys if k.startswith("ps")])
            tt(T2[0:n_], pv, ss, ALU.mult, rkeys + [tabkey], ["t2"] + [k for k in wkeys if k.startswith("ps")])
            tt(ov[:, :, 0:8], T1[0:n_, :, 0:8], T2[0:n_, :, 8:16], ALU.subtract, ["t1", "t2"], wkeys)
            tt(ov[:, :, 8:16], T1[0:n_, :, 8:16], T2[0:n_, :, 0:8], ALU.add, ["t1", "t2"], wkeys)

        PBANKS = ((2, 3, 4), (1, 6, 7))

        def m1a_front(i):
            n_ = tile_np(i)
            b = i % 2
            hb, hT = HB[b], HT[b]
            norm_and_transpose(i, hb, hT, 0)
            for cg in range(3):
                pb_ = PBANKS[b][cg]
                for k in range(8):
                    mm(PS[pb_][0:n_, :], hT[:, k, 0:n_], WA[:, k, cg * 512:(cg + 1) * 512], k == 0, k == 7,
                       [hT.name, f"WA{cg}"], [f"ps{pb_}"])

        def m1a_back(i):
            n_ = tile_np(i)
            bq, bk, bv = PBANKS[i % 2]
            tq = C["ropeq"][:, i, :] if i < NT else C["ropesq"][:, :]
            tk = C["ropek"][:, i, :] if i < NT else C["ropesk"][:, :]
            tqk = "c_ropeq" if i < NT else "c_ropesq"
            tkk = "c_ropek" if i < NT else "c_ropesk"
            act(QB[0:n_, :], PS[bq][0:n_, :], AF.Copy, [], [f"ps{bq}", QB.name], scale=0.125)
            rope(PS[bq][0:n_, :], tq[0:n_], QB[0:n_, :], n_, [], [f"ps{bq}", QB.name], tqk)
            acopy(KF[0:n_, :], PS[bk][0:n_, :], [], [f"ps{bk}", "kf"])
            rope(PS[bk][0:n_, :], tk[0:n_], KF[0:n_, :], n_, [], [f"ps{bk}", "kf"], tkk)
            acopy(KB[0:n_, :], KF[0:n_, :], ["kf"], [KB.name])
            acopy(VF[0:n_, :], PS[bv][0:n_, :], [], [f"ps{bv}", "vf"])
            vdst = VB if i < NT else SVB
            vcopy(vdst[0:n_, :], VF[0:n_, :], ["vf"], [vdst.name])
            if i < NT:
                P.dma("sp", kp[l, i * 128:(i + 1) * 128, :], KF[:, :], ["kf"], [], sem="st_kf")
                P.dma("sp", vp[l, i * 128:(i + 1) * 128, :], VF[:, :], ["vf"], [], sem="st_vf")
            else:
                P.dma("sp", kso[l], KF[0:NS, :], ["kf"], [], sem="st_kf")
                P.dma("sp", vso[l], VF[0:NS, :], ["vf"], [], sem="st_vf")
            for (src, half_, dstT, sdst) in ((QB, 0, QT, SQT), (KB, 1, KT, SKT), (VB, 0, VT, None)):
                if i == NT and sdst is None:
                    continue
                pv = psb(5)[:, half_ * 512:(half_ + 1) * 512]
                for j in range(4):
                    tr(pv[:, j * 128:j * 128 + n_], src[0:n_, j * 128:(j + 1) * 128], C["identb"][0:n_, 0:n_],
                       [src.name, "c_identb"], ["ps5"])
                srcv = pv.rearrange("p (j t) -> p j t", j=4)[:, :, 0:n_]
                if i < NT:
                    vcopy(dstT[:, :, i * 128:(i + 1) * 128], srcv, [], ["ps5", f"{dstT.name}{i}"])
                else:
                    vcopy(sdst[:, :, :], srcv, [], ["ps5", sdst.name])

        def emit_exchange():
            ktk = [f"{KT.name}{i}" for i in range(NT)]
            vtk = [f"{VT.name}{i}" for i in range(NT)]
            for j in range(4):
                P.dma("sp", k_src.ap()[j * 128:(j + 1) * 128, :], KT[:, j, :], ktk, ["k_src"], sem="kvx")
                P.dma("sp", v_src.ap()[j * 128:(j + 1) * 128, :], VT[:, j, :], vtk, ["v_src"], sem="kvx")
            P.collective([k_src.ap()], [k_all.ap()], GROUPS, ["k_src"], ["k_all"])
            P.collective([v_src.ap()], [v_all.ap()], GROUPS, ["v_src"], ["v_all"])

        prev_back = None
        for i in ([] if "m1a" in skip else list(range(NT + 1)) + [None]):
            front = None
            if i is not None:
                P.begin_rec()
                m1a_front(i)
                front = P.end_rec()
            P.play(front, prev_back)
            prev_back = None
            if i == NT:
                emit_exchange()
            if i is not None:
                P.begin_rec()
                m1a_back(i)
                prev_back = P.end_rec()

        if dbg == "m1a":
            break

        P.barrier()
        ph.reset(keep_off)
        KTP = [ph.alloc(f"ktp{b}", [128, TL], BF16) for b in range(2)]
        VTP = [ph.alloc(f"vtp{b}", [128, TL], BF16) for b in range(2)]
        ACC = [ph.alloc(f"acc{b}", [128, TL], F32) for b in range(2)]
        RDEN = ph.alloc("rden", [128, TL], F32)
        PT = [ph.alloc(f"pt{b}", [128, 256], BF16) for b in range(4)]
        NVA = 12
        VA = [[ph.alloc(f"va{e}_{s}", [128, 128], BF16) for s in range(NVA)] for e in range(2)]
        for e_ in range(2):
            for s_ in range(NVA):
                t_ = VA[e_][s_]
                P.op("dve", lambda e, t_=t_: e.memset(t_[:, :], 1.0), [], [t_.name])

        def load_prefix(j):
            b = j % 2
            P.dma("sp", KTP[b][:, :], k_all.ap()[j * 128:(j + 1) * 128, :], ["k_all"], [KTP[b].name], sem=f"pre{b}")
            P.dma("sp", VTP[b][:, :], v_all.ap()[j * 128:(j + 1) * 128, :], ["v_all"], [VTP[b].name], sem=f"pre{b}")

        if l == 0:
            convert_weights(0, "rest")
        load_prefix(0)
        cnt = {"sb": 0, "vb": 0, "pt": 0, "va": [0, 0], "tb": 0, "cp": 0}
        for h in ([] if "m2" in skip else range(8)):
            j, e_ = h // 2, h % 2
            if e_ == 0 and j + 1 < 4:
                load_prefix(j + 1)
            rs = slice(e_ * 64, (e_ + 1) * 64)
            ab = h % 2
            acc = ACC[ab]
            ktp, vtp = KTP[j % 2], VTP[j % 2]
            vcols = slice(0, 64) if e_ == 0 else slice(64, 128)

            def build_v(src, srckeys, cols):
                s_ = cnt["va"][e_] % NVA
                cnt["va"][e_] += 1
                va = VA[e_][s_]
                tb = 6 + (cnt["tb"] % 2)
                sl = (cnt["tb"] // 2) % 8
                cnt["tb"] += 1
                pv = psb(tb)[:, sl * 64:(sl + 1) * 64]
                tr(pv, src[rs, cols], C["identb"][rs, rs], srckeys + ["c_identb"], [f"ps{tb}"])
                if cnt["cp"] % 2 == 0:
                    vcopy(va[:, vcols], pv, [], [f"ps{tb}", va.name])
                else:
                    acopy(va[:, vcols], pv, [], [f"ps{tb}", va.name])
                cnt["cp"] += 1
                return va

            jobs = []
            for d_ in (1, 4, 16):
                for r in range(d_):
                    for i in range(16 // d_):
                        jobs.append({"d": d_, "r": r, "i": i})
            chain = {}

            def s1(jb):
                d_, r, i = jb["d"], jb["r"], jb["i"]
                span = d_ * 127 + 1
                p0 = TL - 128 * d_ + r
                if i == 0:
                    chain[(d_, r)] = build_v(vtp, [vtp.name], slice(p0, p0 + span, d_))
                vprev = chain[(d_, r)]
                c0 = d_ * 128 * i + r
                cols = slice(c0, c0 + span, d_)
                blks = list(range((d_ * 128 * i) // 128, (d_ * 128 * (i + 1)) // 128))
                vdiag = build_v(VT[:, j, :], [f"{VT.name}{b_}" for b_ in blks], cols)
                chain[(d_, r)] = vdiag
                sb = cnt["sb"] % 4; cnt["sb"] += 1
                qk_r = [f"{QT.name}{b_}" for b_ in blks]
                if i == 0:
                    mm(PS[sb][:, 0:128], ktp[rs, slice(p0, p0 + span, d_)], QT[rs, j, cols], True, False,
                       [ktp.name] + qk_r, [f"ps{sb}"])
                else:
                    pc0 = d_ * 128 * (i - 1) + r
                    pblks = list(range((d_ * 128 * (i - 1)) // 128, (d_ * 128 * i) // 128))
                    mm(PS[sb][:, 0:128], KT[rs, j, slice(pc0, pc0 + span, d_)], QT[rs, j, cols], True, False,
                       [f"{KT.name}{b_}" for b_ in pblks] + qk_r, [f"ps{sb}"])
                mm(PS[sb][:, 128:256], KT[rs, j, cols], QT[rs, j, cols], False, True,
                   [f"{KT.name}{b_}" for b_ in blks] + qk_r, [f"ps{sb}"])
                jb.update(vprev=vprev, vdiag=vdiag, sb=sb, cols=cols, blks=blks)

            def s2a(jb):
                sb = jb["sb"]
                pt = PT[cnt["pt"] % 4]; cnt["pt"] += 1
                act(pt[:, :], PS[sb][:, 0:256], AF.Exp, [], [f"ps{sb}", pt.name])
                msk = C["mb"] if jb["i"] == 0 else C["ma"]
                tt(pt[:, :], pt[:, :], msk[:, :], ALU.mult, [pt.name, "c_ma", "c_mb"], [pt.name])
                jb["pt"] = pt

            def s2(jb):
                sb, vprev, vdiag, pt = jb["sb"], jb["vprev"], jb["vdiag"], jb["pt"]
                vb = 4 + cnt["vb"] % 2; cnt["vb"] += 1
                mm(PS[vb][:, 0:128], vprev[:, :], pt[:, 0:128], True, False, [vprev.name, pt.name], [f"ps{vb}"])
                mm(PS[vb][:, 0:128], vdiag[:, :], pt[:, 128:256], False, True, [vdiag.name, pt.name], [f"ps{vb}"])
                jb["vb"] = vb

            def s3(jb):
                d_, r, i, vb, cols, blks = jb["d"], jb["r"], jb["i"], jb["vb"], jb["cols"], jb["blks"]
                if d_ == 1:
                    acopy(acc[:, cols], PS[vb][:, 0:128], [], [f"ps{vb}", f"A1_{ab}_{i}"])
                elif d_ == 4:
                    tt(acc[:, cols], PS[vb][:, 0:128], acc[:, cols], ALU.add,
                       [f"A1_{ab}_{b_}" for b_ in blks], [f"ps{vb}", f"A4_{ab}_{r}_{i}"])
                else:
                    tt(acc[:, cols], PS[vb][:, 0:128], acc[:, cols], ALU.add,
                       [f"A1_{ab}_{b_}" for b_ in range(16)] + [f"A4_{ab}_{r % 4}_{i_}" for i_ in range(4)],
                       [f"ps{vb}", f"A16_{ab}_{r}"])

            nj = len(jobs)
            for k_ in range(nj + 3):
                if k_ < nj:
                    s1(jobs[k_])
                if 0 <= k_ - 1 < nj:
                    s2a(jobs[k_ - 1])
                if 0 <= k_ - 2 < nj:
                    s2(jobs[k_ - 2])
                if 0 <= k_ - 3 < nj:
                    s3(jobs[k_ - 3])
            allacc = ([f"A1_{ab}_{b_}" for b_ in range(16)] + [f"A4_{ab}_{r}_{i_}" for r in range(4) for i_ in range(4)]
                      + [f"A16_{ab}_{r}" for r in range(16)])
            urs = rs
            drs = slice(64, 128) if e_ == 0 else slice(0, 64)
            P.op("dve", lambda e, urs=urs, drs=drs, acc=acc: e.reciprocal(out=RDEN[urs, :], in_=acc[drs, :]), allacc, ["rden"])
            tt(OT[urs, j, :], acc[urs, :], RDEN[urs, :], ALU.mult, allacc + ["rden"], [f"OT{h}"] + allacc)
        P.barrier()
        ph.reset(ot_end)
        KC = [ph.alloc(f"kc{b}", [128, 9, 512], BF16) for b in range(2)]
        VC = [ph.alloc(f"vc{b}", [128, 9, 512], BF16) for b in range(2)]
        KCT = ph.alloc("kct", [128, 4, 9, 128], BF16)
        PTS = [ph.alloc(f"pts{b}", [128, 144], BF16) for b in range(2)]
        PNF = ph.alloc("pnf", [16, 2, 4, 16], F32)
        PNB = ph.alloc("pnb", [16, 2, 4, 16], BF16)
        RDS = ph.alloc("rds", [16, 8], F32)

        def load_cache(j):
            b = j % 2
            for (dst, src) in ((KC[b], ck), (VC[b], cv)):
                P.dma("pool", dst[:, 0, :], src[l, j, 1920:2048, :], [], [dst.name], sem=dst.name)
                P.dma("pool", dst[:, 1:5, :], src[l, j, 1536:2048, :].rearrange("(m i) c -> m i c", i=4), [], [dst.name], sem=dst.name)
                P.dma("pool", dst[:, 5:9, :], src[l, j].rearrange("(m r) c -> m r c", r=16)[:, 0:4, :], [], [dst.name], sem=dst.name)

        P.op("dve", lambda e: e.memset(PS[4][0:16, :], 0.0), [], ["ps4"])
        P.op("dve", lambda e: e.memset(PS[5][0:16, 0:8], 0.0), [], ["ps5"])
        load_cache(0)
        scn = 0
        for j in range(4):
            if j + 1 < 4:
                load_cache(j + 1)
            kc, vc = KC[j % 2], VC[j % 2]
            for p_ in range(4):
                for tau in range(9):
                    bank, col = (0, tau * 128) if tau < 8 else (1, 0)
                    tr(psb(bank)[:, col:col + 128], kc[:, tau, p_ * 128:(p_ + 1) * 128], C["identb"][:, :], [kc.name, "c_identb"], [f"ps{bank}"])
                acopy(KCT[:, p_, 0:8, :], psb(0)[:, :].rearrange("p (t k) -> p t k", t=8), [], ["ps0", f"kct{p_}"])
                vcopy(KCT[:, p_, 8, :], psb(1)[:, 0:128], [], ["ps1", f"kct{p_}"])
            for h in range(8):
                p_, e_ = h // 2, h % 2
                rs = slice(e_ * 64, (e_ + 1) * 64)
                sb = 2 + scn % 2
                pts = PTS[scn % 2]; scn += 1
                mm(PS[sb][:, 0:144], C["identb"][:, :], C["smask"][:, j * 144:(j + 1) * 144], True, False, ["c_identb", "c_smask"], [f"ps{sb}"])
                for tau in range(9):
                    mm(PS[sb][:, tau * 16:(tau + 1) * 16], KCT[rs, p_, tau, :], SQT[rs, p_, :], False, tau == 8, [f"kct{p_}", SQT.name], [f"ps{sb}"])
                act(pts[:, :], PS[sb][:, 0:144], AF.Exp, [], [f"ps{sb}", pts.name])
                for tau in range(9):
                    mm(PS[4][0:16, h * 64:(h + 1) * 64], pts[:, tau * 16:(tau + 1) * 16], vc[:, tau, h * 64:(h + 1) * 64], False, False,
                       [pts.name, vc.name], ["ps4"])
                    mm(PS[5][0:16, h:h + 1], pts[:, tau * 16:(tau + 1) * 16], C["onesb"][:, 0:1], False, False, [pts.name, "c_onesb"], ["ps5"])
        for e_ in range(2):
            rs = slice(e_ * 64, (e_ + 1) * 64)
            nb = 6 + e_
            for p_ in range(4):
                mm(PS[nb][0:16, p_ * 16:(p_ + 1) * 16], SKT[rs, p_, :], SQT[rs, p_, :], True, True, [SKT.name, SQT.name], [f"ps{nb}"])
            act(PNF[:, e_, :, :], PS[nb][0:16, 0:64].rearrange("p (a t) -> p a t", a=4), AF.Exp, [], [f"ps{nb}", f"pnf{e_}"])
            tt(PNB[:, e_, :, :], PNF[:, e_, :, :], C["mnew"][:, :].unsqueeze(1).broadcast_to([16, 4, 16]), ALU.mult,
               [f"pnf{e_}", "c_mnew"], [f"pnb{e_}"])
        for h in range(8):
            p_, e_ = h // 2, h % 2
            mm(PS[4][0:16, h * 64:(h + 1) * 64], PNB[:, e_, p_, :], SVB[0:16, h * 64:(h + 1) * 64], False, False, [f"pnb{e_}", SVB.name], ["ps4"])
            mm(PS[5][0:16, h:h + 1], PNB[:, e_, p_, :], C["onesb"][0:16, 0:1], False, False, [f"pnb{e_}", "c_onesb"], ["ps5"])
        P.op("dve", lambda e: e.reciprocal(out=RDS[:, :], in_=PS[5][0:16, 0:8]), [], ["ps5", "rds"])
        tt(OMS[0:16, 0:512].rearrange("p (h d) -> p h d", h=8), PS[4][0:16, :].rearrange("p (h d) -> p h d", h=8),
           RDS[:, :].unsqueeze(2).broadcast_to([16, 8, 64]), ALU.mult, ["rds"], ["ps4", "oms_a"])
        if dbg == "m2s":
            P.dma("pool", dbg_x[0:16, 0:512], OMS[0:16, 0:512], ["oms_a"], [], sem="dbg")
            break
        if dbg == "m2":
            for j in range(4):
                P.dma("sp", dbg_ot[j * 128:(j + 1) * 128, :], OT[:, j, :], [f"OT{h_}" for h_ in range(8)], [], sem="dbg")
            break

        P.barrier()
        ph.reset(ot_end)
        OB = ph.alloc("OB", [128, NT + 1, 512], F32)
        QH = ph.alloc("QH", [128, 2, TL], BF16)
        SFIN = ph.alloc("sfin", [128, 2, 128], F32)
        SA_F = ph.alloc("sa_f", [128, 2, 128], F32)
        SA_B = ph.alloc("sa_b", [128, 2, 128], BF16)
        keep2_off = ph.off
        WB = ph.alloc("WB", [128, 8, 1040], BF16)
        par = lambda nm, shp, dt: [ph.alloc(f"{nm}{b}", shp, dt) for b in range(2)]
        HB1s, HT1s = par("hb", [128, D], BF16), par("hT", [128, 8, 128], BF16)
        LAs, EBs, ENBs, EGs = par("la", [128, 256], F32), par("eb", [128, 256], F32), par("enb", [128, 256], F32), par("eg", [128, 256], F32)
        QTLs, KTLs, QHLs = par("qtl", [128, 256], BF16), par("ktl", [128, 256], BF16), par("qhl", [128, 256], BF16)
        VBFs = par("vbf", [128, 512], BF16)
        QTTs, KTTs = par("qtt", [128, 2, 128], BF16), par("ktt", [128, 2, 128], BF16)
        AMs_ = par("am", [128, 4, 128], BF16)
        GLs, GLTs = par("gl", [128, 16], BF16), par("glt", [16, 128], BF16)
        ETOTs = par("etot", [128, 2], F32)
        GACC = ph.alloc("gacc", [128, 256], F32)
        S_F = ph.alloc("s_f", [128, 2, 128], F32)
        S_B = ph.alloc("s_b", [128, 2, 128], BF16)
        TOTA = ph.alloc("tota", [128, 2], F32)

        P.dma("sp", WB[:, :, :].rearrange("p k c -> p (k c)"), wb_bf[l].ap(), [f"wb_bf{l}"], ["WB"], sem="win")
        P.dma("sp", GGLA[:], g_gla[l].partition_broadcast(128), [], ["ggla"], sem="lp")
        P.dma("sp", BGATE[:], b_gate[l].partition_broadcast(128), [], ["bgate"], sem="lp")
        P.dma("pool", WG2[:], w_gate2[l], [], ["wg2"], sem="lp2")
        if l == 0 and n_layers > 1:
            convert_weights(1, "a")
            convert_weights(1, "rest")
        P.op("dve", lambda e: e.memset(S_F[:], 0.0), [], ["s_f"])
        P.op("dve", lambda e: e.memset(S_B[:], 0.0), [], ["s_b"])
        P.op("dve", lambda e: e.memset(GACC[:], 0.0), [], ["gacc"])
        P.op("dve", lambda e: e.memset(TOTA[:], 0.0), [], ["tota"])
        ONE1 = C["onesf"][:, 0:1]

        def gla_front(i, n_, b, sample=False):
            HB1, HT1, LA, EB, ENB, EG = HB1s[b], HT1s[b], LAs[b], EBs[b], ENBs[b], EGs[b]
            QTL, KTL, QHL, VBF, GL, GLT, ETOT = QTLs[b], KTLs[b], QHLs[b], VBFs[b], GLs[b], GLTs[b], ETOTs[b]
            norm_and_transpose(i, HB1, HT1, 0)
            for (bank, c0, cw) in ((1, 0, 512), (2, 512, 512), (4, 1024, 16)):
                for k in range(8):
                    mm(PS[bank][0:n_, 0:cw], HT1[:, k, 0:n_], WB[:, k, c0:c0 + cw], k == 0, k == 7, [HT1.name, "WB"], [f"ps{bank}"])
            acopy(GL[0:n_, :], PS[4][0:n_, 0:16], [], ["ps4", GL.name])
            tr(psb(3)[0:16, 0:n_], GL[0:n_, 0:16], C["identb"][0:n_, 0:n_], [GL.name, "c_identb"], ["ps3"])
            vcopy(GLT[0:16, 0:n_], psb(3)[0:16, 0:n_], [], ["ps3", GLT.name])
            mm(PS[4][0:n_, 256:512], GLT[0:16, 0:n_], WG2[0:16, :], True, True, [GLT.name, "wg2"], ["ps4"])
            tt(LA[0:n_, :], PS[4][0:n_, 256:512], BGATE[0:n_, :], ALU.add, ["bgate"], ["ps4", LA.name])
            act(LA[0:n_, :], LA[0:n_, :], AF.Exp, [LA.name], [LA.name], scale=-1.0)
            act(LA[0:n_, :], LA[0:n_, :], AF.Ln, [LA.name, "c_onesf"], [LA.name], bias=ONE1[0:n_, :])
            cs = C["ucs_s"] if sample else C["ucs"]
            mm(PS[5][0:n_, 0:256], cs[0:n_, 0:n_], LA[0:n_, :], True, True, [LA.name, "c_ucs", "c_ucs_s"], ["ps5"])
            act(EB[0:n_, :], PS[5][0:n_, 0:256], AF.Exp, [], ["ps5", EB.name], scale=-1.0 / 16)
            act(ENB[0:n_, :], PS[5][0:n_, 0:256], AF.Exp, [], ["ps5", ENB.name], scale=1.0 / 16)
            stt(QTL[0:n_, :], PS[1][0:n_, 0:256], 0.125, EB[0:n_, :], ALU.mult, ALU.mult, [EB.name], ["ps1", QTL.name])
            tt(KTL[0:n_, :], PS[1][0:n_, 256:512], ENB[0:n_, :], ALU.mult, [ENB.name], ["ps1", KTL.name])
            acopy(VBF[0:n_, :], PS[2][0:n_, :], [], ["ps2", VBF.name])
            if sample:
                return
            mm(PS[5][0:n_, 256:512], C["onesf"][0:n_, 0:n_], LA[0:n_, :], True, True, [LA.name, "c_onesf"], ["ps5"])
            for p_ in range(2):
                mm(PS[3][:, 128 + p_:129 + p_], LA[0:n_, p_ * 128:(p_ + 1) * 128], C["onesf"][0:n_, 0:1], True, True,
                   [LA.name, "c_onesf"], ["ps3"])
            act(EG[0:n_, :], GACC[0:n_, :], AF.Exp, ["gacc"], [EG.name], scale=-1.0 / 16)
            tt(GACC[0:n_, :], PS[5][0:n_, 256:512], GACC[0:n_, :], ALU.add, [], ["ps5", "gacc"])
            act(ETOT[:, :], PS[3][:, 128:130], AF.Exp, [], ["ps3", ETOT.name], scale=-1.0 / 16)
            tt(TOTA[:, :], PS[3][:, 128:130], TOTA[:, :], ALU.add, [], ["ps3", "tota"])
            tt(QHL[0:n_, :], QTL[0:n_, :], EG[0:n_, :], ALU.mult, [QTL.name, EG.name], [QHL.name])
            for p_ in range(2):
                tr(psb(3)[:, 512 + p_ * 128:512 + p_ * 128 + n_], QHL[0:n_, p_ * 128:(p_ + 1) * 128],
                   C["identb"][0:n_, 0:n_], [QHL.name, "c_identb"], ["ps3"])
            acopy(QH[:, :, i * 128:(i + 1) * 128], psb(3)[:, 512:768].rearrange("p (a t) -> p a t", a=2), [], ["ps3", f"QH{i}"])

        def gla_back(i, b):
            n_ = 128
            QTL, KTL, VBF, QTT, KTT, AM, ETOT = QTLs[b], KTLs[b], VBFs[b], QTTs[b], KTTs[b], AMs_[b], ETOTs[b]
            for (src, c0) in ((QTL, 0), (KTL, 256)):
                for p_ in range(2):
                    tr(psb(6)[:, c0 + p_ * 128:c0 + p_ * 128 + n_], src[0:n_, p_ * 128:(p_ + 1) * 128],
                       C["identb"][0:n_, 0:n_], [src.name, "c_identb"], ["ps6"])
            v6 = psb(6)
            acopy(QTT[:, :, :], v6[:, 0:256].rearrange("p (a t) -> p a t", a=2), [], ["ps6", QTT.name])
            vcopy(KTT[:, :, :], v6[:, 256:512].rearrange("p (a t) -> p a t", a=2), [], ["ps6", KTT.name])
            for h in range(4):
                p_, e_ = h // 2, h % 2
                rs = slice(e_ * 64, (e_ + 1) * 64)
                if e_ == 0:
                    mm(PS[7][:, p_ * 128:(p_ + 1) * 128], KTT[rs, p_, :], QTT[rs, p_, :], True, True, [KTT.name, QTT.name], ["ps7"])
                else:
                    mm(PS[6][:, 256 + p_ * 128:256 + (p_ + 1) * 128], KTT[rs, p_, :], QTT[rs, p_, :], True, True, [KTT.name, QTT.name], ["ps6"])
            ucb = C["ucs"][:, :].unsqueeze(1).broadcast_to([128, 2, 128])
            tt(AM[:, 0::2, :], PS[7][:, 0:256].rearrange("p (h t) -> p h t", h=2), ucb, ALU.mult, ["c_ucs"], ["ps7", AM.name + "0"])
            tt(AM[:, 1::2, :], PS[6][:, 256:512].rearrange("p (h t) -> p h t", h=2), ucb, ALU.mult, ["c_ucs"], ["ps6", AM.name + "1"])
            for h in range(4):
                p_, e_ = h // 2, h % 2
                rs = slice(e_ * 64, (e_ + 1) * 64)
                mm(PS[7][:, h * 128:(h + 1) * 128], AM[:, h, :], VBF[:, h * 128:(h + 1) * 128], True, False, [AM.name + str(e_), VBF.name], ["ps7"])
                mm(PS[7][:, h * 128:(h + 1) * 128], QTT[rs, p_, :], S_B[rs, p_, :], False, True, [QTT.name, "s_b"], ["ps7"])
            acopy(OB[:, i, :], PS[7][:, :], [], ["ps7", f"ob{i}"])
            for p_ in range(2):
                mm(PS[6][:, p_ * 256:(p_ + 1) * 256], KTL[:, p_ * 128:(p_ + 1) * 128], VBF[:, p_ * 256:(p_ + 1) * 256], True, True,
                   [KTL.name, VBF.name], ["ps6"])
            tt(S_F[:, :, :], S_F[:, :, :], ETOT[:, :].unsqueeze(2).broadcast_to([128, 2, 128]), ALU.mult, ["s_f", ETOT.name], ["s_f"])
            for p_ in range(2):
                for e_ in range(2):
                    rs = slice(e_ * 64, (e_ + 1) * 64)
                    stt(S_F[rs, p_, :], PS[6][rs, p_ * 256 + e_ * 128:p_ * 256 + (e_ + 1) * 128], ETOT[rs, p_:p_ + 1], S_F[rs, p_, :],
                        ALU.mult, ALU.add, ["s_f", ETOT.name], ["ps6", "s_f"])
            vcopy(S_B[:, :, :], S_F[:, :, :], ["s_f"], ["s_b"])

        prev_back = None
        for i in range(NT + 1):
            front = None
            if i < NT:
                P.begin_rec()
                gla_front(i, 128, i % 2)
                front = P.end_rec()
            P.play(front, prev_back)
            prev_back = None
            if i < NT:
                P.begin_rec()
                gla_back(i, i % 2)
                prev_back = P.end_rec()
        if dbg and dbg.startswith("m1b") and dbg != "m1bx":
            break

        if "sgla" not in skip:
            QTL, KTL, VBF, QTT, KTT, LA = QTLs[0], KTLs[0], VBFs[0], QTTs[0], KTTs[0], LAs[0]
            S0F = [ph.alias("s0f0", [128, 2, 128], F32, EGs[1]), ph.alias("s0f1", [128, 2, 128], F32, EBs[1])]
            S0K = [[EGs[1].name], [EBs[1].name]]
            S0B = [ph.alloc(f"s0b{b}", [128, 2, 128], BF16) for b in range(2)]
            ETS = ph.alloc("ets", [128, 2, 4], F32)
            QTM = [ph.alloc(f"qtm{b}", [128, 2, 16], BF16) for b in range(2)]
            AMS = ph.alias("ams", [128, 4, 16], BF16, AMs_[1])
            KTM = [ph.alias("ktm0", [16, 256], BF16, AMs_[1], 256), ph.alias("ktm1", [16, 256], BF16, QHLs[1])]
            KTK = [[AMs_[1].name + "0", AMs_[1].name + "1"], [QHLs[1].name]]
            i = NT
            n_ = NS
            P.op("dve", lambda e: e.memset(AMS[:, :, :], 0.0), [], ["ams0", "ams1", AMs_[1].name + "0", AMs_[1].name + "1"])
            gla_front(i, n_, 0, sample=True)
            for p_ in range(2):
                mm(PS[3][:, 128 + p_ * 4:128 + (p_ + 1) * 4], LA[0:n_, p_ * 128:(p_ + 1) * 128], C["seqsel"][0:n_, 0:4], True, True,
                   [LA.name, "c_seqsel"], ["ps3"])
            act(ETS[:, :, :], PS[3][:, 128:136].rearrange("p (a s) -> p a s", a=2), AF.Exp, [], ["ps3", "ets"], scale=-1.0 / 16)
            for (src, c0) in ((QTL, 0), (KTL, 256)):
                for p_ in range(2):
                    tr(psb(6)[:, c0 + p_ * 128:c0 + p_ * 128 + n_], src[0:n_, p_ * 128:(p_ + 1) * 128],
                       C["identb"][0:n_, 0:n_], [src.name, "c_identb"], ["ps6"])
            v6 = psb(6)
            acopy(QTT[:, :, 0:n_], v6[:, 0:256].rearrange("p (a t) -> p a t", a=2)[:, :, 0:n_], [], ["ps6", QTT.name])
            vcopy(KTT[:, :, 0:n_], v6[:, 256:512].rearrange("p (a t) -> p a t", a=2)[:, :, 0:n_], [], ["ps6", KTT.name])
            for h in range(4):
                p_, e_ = h // 2, h % 2
                rs = slice(e_ * 64, (e_ + 1) * 64)
                ab_ = 7 if e_ == 0 else 0
                mm(PS[ab_][0:n_, p_ * 16:(p_ + 1) * 16], KTT[rs, p_, 0:n_], QTT[rs, p_, 0:n_], True, True, [KTT.name, QTT.name], [f"ps{ab_}"])
            for e_ in range(2):
                ab_ = 7 if e_ == 0 else 0
                tt(AMS[0:n_, e_::2, :], PS[ab_][0:n_, 0:32].rearrange("p (h t) -> p h t", h=2),
                   C["ucs_s"][:, :].unsqueeze(1).broadcast_to([16, 2, 16]), ALU.mult, ["c_ucs_s"], [f"ps{ab_}", f"ams{e_}"])
            P.op("dve", lambda e: e.memset(PS[1][0:16, :], 0.0), [], ["ps1"])
            P.op("dve", lambda e: e.memset(PS[3][0:16, :], 0.0), [], ["ps3"])
            for h in range(4):
                e_ = h % 2
                mm(PS[1][0:n_, h * 128:(h + 1) * 128], AMS[:, h, :], VBF[:, h * 128:(h + 1) * 128], False, False, [f"ams{e_}", VBF.name], ["ps1"])
            for j in range(4):
                b = j % 2
                s0f, s0b, qtm, ktm = S0F[b], S0B[b], QTM[b], KTM[b]
                for e_ in range(2):
                    P.dma("sp", s0f[e_ * 64:(e_ + 1) * 64, :, :], sg_in[l, j].rearrange("(a e) k v -> e k a v", e=2)[e_], [],
                          [s0f.name] + S0K[b], sem=f"s0ld{b}")
                vcopy(s0b[:, :, :], s0f[:, :, :], [s0f.name], [s0b.name])
                tt(qtm[:, :, :], QTT[:, :, 0:n_], C["seqcol"][:, j * 16:(j + 1) * 16].unsqueeze(1).broadcast_to([128, 2, 16]), ALU.mult,
                   [QTT.name, "c_seqcol"], [qtm.name])
                ts(ktm[0:n_, :], KTL[0:n_, :], C["seqsel"][0:n_, j:j + 1], None, ALU.mult, ALU.bypass, [KTL.name, "c_seqsel"], [ktm.name] + KTK[b])
                for e_ in range(2):
                    rs = slice(e_ * 64, (e_ + 1) * 64)
                    ob_ = 1 if e_ == 0 else 3
                    for p_ in range(2):
                        h = 2 * p_ + e_
                        mm(PS[ob_][0:n_, h * 128:(h + 1) * 128], qtm[rs, p_, :], s0b[rs, p_, :], False, False, [qtm.name, s0b.name], [f"ps{ob_}"])
                for p_ in range(2):
                    mm(PS[2][:, p_ * 256:(p_ + 1) * 256], ktm[0:n_, p_ * 128:(p_ + 1) * 128], VBF[0:n_, p_ * 256:(p_ + 1) * 256], True, True,
                       [ktm.name, VBF.name], ["ps2"])
                tt(s0f[:, :, :], s0f[:, :, :], ETS[:, :, j:j + 1].broadcast_to([128, 2, 128]), ALU.mult, [s0f.name, "ets", s0b.name], [s0f.name])
                for p_ in range(2):
                    for e_ in range(2):
                        rs = slice(e_ * 64, (e_ + 1) * 64)
                        stt(s0f[rs, p_, :], PS[2][rs, p_ * 256 + e_ * 128:p_ * 256 + (e_ + 1) * 128], ETS[rs, p_, j:j + 1], s0f[rs, p_, :],
                            ALU.mult, ALU.add, [s0f.name, "ets"], ["ps2", s0f.name])
                for e_ in range(2):
                    P.dma("sp", sso[l, j].rearrange("(a e) k v -> e k a v", e=2)[e_], s0f[e_ * 64:(e_ + 1) * 64, :, :], [s0f.name], [],
                          sem=f"st_ss{b}")
            acopy(OB[0:n_, i, :], PS[1][0:n_, :], [], ["ps1", f"ob{i}"])
            obv = OB[0:n_, i, :].rearrange("p (h d) -> p h d", h=4)
            tt(obv[:, 1::2, :], PS[3][0:n_, :].rearrange("p (h d) -> p h d", h=4)[:, 1::2, :], obv[:, 1::2, :], ALU.add, [f"ob{i}"], ["ps3", f"ob{i}"])

        P.dma("sp", s_src.ap(), S_F[:, :, :].rearrange("p a v -> p (a v)"), ["s_f"], ["s_src"], sem="sx")
        P.collective([s_src.ap()], [s_all.ap()], GROUPS, ["s_src"], ["s_all"])
        P.dma("sp", SA_F[:, :, :].rearrange("p a v -> p (a v)"), s_all.ap()[0:128, :], ["s_all"], ["sa_f"], sem="sx2")
        ts(SA_F[:, :, :], SA_F[:, :, :], C["flag"][:, 0:1], None, ALU.mult, ALU.bypass, ["sa_f", "c_flag"], ["sa_f"])
        vcopy(SA_B[:, :, :], SA_F[:, :, :], ["sa_f"], ["sa_b"])
        ETOT = ETOTs[0]
        act(ETOT[:, :], TOTA[:, :], AF.Exp, ["tota"], [ETOT.name], scale=-1.0 / 16)
        tt(SFIN[:, :, :], SA_F[:, :, :], ETOT[:, :].unsqueeze(2).broadcast_to([128, 2, 128]), ALU.mult, ["sa_f", ETOT.name], ["sfin"])
        tt(SFIN[:, :, :], SFIN[:, :, :], S_F[:, :, :], ALU.add, ["sfin", "s_f"], ["sfin"])
        for e_ in range(2):
            P.dma("sp", spo[l].rearrange("(a e) k v -> e k a v", e=2)[e_], SFIN[e_ * 64:(e_ + 1) * 64, :, :], ["sfin"], [], sem="st_s")

        if dbg == "m1bx":
            break
        P.barrier()
        ph.reset(keep2_off)
        WO = ph.alloc("WO", [128, 8, D], BF16)
        GB = ph.alloc("GB", [128, D], F32)
        WR = ph.alloc("WR", [128, 8, 512], BF16)
        HB3 = ph.alloc("hb3", [128, D], BF16)
        HT3 = ph.alloc("hT3", [128, 8, 128], BF16)
        ER = ph.alloc("er3", [128, 512], F32)
        P.dma("sp", WR[:, :, :].rearrange("p k c -> p (k c)"), wr_bf[l].ap(), [f"wr_bf{l}"], ["WR"], sem="win")
        SQ = ph.alloc("sq", [128, 512], F32)
        T5 = ph.alloc("t5", [128, 512], F32)
        OMB = ph.alloc("omb", [128, 512], BF16)
        OBT = ph.alloc("obt", [128, 4, 128], BF16)
        OAT = ph.alloc("oat", [128, 4, NS], BF16)
        TMP = ph.alloc("tmp", [128, D], F32)
        JK = ph.alloc("jk", [128, 512], BF16)
        SS4 = ph.alloc("ss4", [128, 8], F32)
        R4 = ph.alloc("r4", [128, 8], F32)
        SSM = ph.alloc("ssm", [128, 4], F32)
        RM = ph.alloc("rm", [128, 4], F32)
        P.dma("sp", WO[:, :, :].rearrange("p k c -> p (k c)"), wo_bf[l].ap(), [f"wo_bf{l}"], ["WO"], sem="wout")
        P.dma("sp", GB[:], g_mix_post[l].partition_broadcast(128), [], ["GB"], sem="lp")
        OBTs = [OBT, ph.alloc("obt1", [128, 4, 128], BF16)]

        def m3_front(i):
            n_ = tile_np(i)
            tcols = slice(i * 128, (i + 1) * 128)
            OBT = OBTs[i % 2]
            if i < NT:
                for h in range(4):
                    p_, e_ = h // 2, h % 2
                    rs = slice(e_ * 64, (e_ + 1) * 64)
                    cb = 0 if e_ == 0 else 4
                    mm(PS[cb][0:n_, p_ * 128:(p_ + 1) * 128], QH[rs, p_, tcols], SA_B[rs, p_, :], True, True, [f"QH{i}", "sa_b"], [f"ps{cb}"])
                obv = OB[0:n_, i, :].rearrange("p (h d) -> p h d", h=4)
                for e_ in range(2):
                    cb = 0 if e_ == 0 else 4
                    tt(obv[:, e_::2, :], PS[cb][0:n_, 0:256].rearrange("p (h d) -> p h d", h=2), obv[:, e_::2, :], ALU.add,
                       [f"ob{i}"], [f"ps{cb}", f"ob{i}"])
            else:
                for j in range(4):
                    tr(psb(0)[:, j * 128:j * 128 + n_], OMS[0:n_, j * 128:(j + 1) * 128], C["identb"][0:n_, 0:n_], ["oms_a", "c_identb"], ["ps0"])
                acopy(OAT[:, :, 0:n_], psb(0)[:, 0:512].rearrange("p (j t) -> p j t", j=4)[:, :, 0:n_], [], ["ps0", "oat"])
            norm_and_transpose(i, HB3, HT3, 6)
            for k in range(8):
                mm(PS[7][0:n_, :], HT3[:, k, 0:n_], WR[:, k, :], k == 0, k == 7, [HT3.name, "WR"], ["ps7"])
            act(ER[0:n_, :], PS[7][0:n_, :], AF.Exp, [], ["ps7", "er"], scale=-1.0)
            ts(ER[0:n_, :], ER[0:n_, :], 1.0, None, ALU.add, ALU.bypass, ["er"], ["er"])
            tt(SQ[0:n_, :], OB[0:n_, i, :], OB[0:n_, i, :], ALU.mult, [f"ob{i}"], ["sq"])
            P.op("dve", lambda e, n_=n_: e.reciprocal(out=ER[0:n_, :], in_=ER[0:n_, :]), ["er"], ["er"])
            P.op("dve", lambda e, n_=n_: e.tensor_reduce(out=SS4[0:n_, 0:4], in_=SQ[0:n_, :].rearrange("p (h d) -> p h d", h=4),
                                                         axis=AX.X, op=ALU.add), ["sq"], ["ss4"])
            act(SS4[0:n_, 4:8], SS4[0:n_, 0:4], AF.Ln, ["ss4", "epst"], ["ss4b"], scale=1.0 / 128, bias=EPST[0:n_, 0:1])
            act(R4[0:n_, 0:4], SS4[0:n_, 4:8], AF.Exp, ["ss4b"], ["r4"], scale=-0.5)
            for h in range(4):
                stt(T5[0:n_, h * 128:(h + 1) * 128], OB[0:n_, i, h * 128:(h + 1) * 128], R4[0:n_, h:h + 1], GGLA[0:n_, :],
                    ALU.mult, ALU.mult, [f"ob{i}", "r4", "ggla"], ["t5"])
            tt(ER[0:n_, :], PS[7][0:n_, :], ER[0:n_, :], ALU.mult, ["er"], ["ps7", "er"])
            tt(OMB[0:n_, :], T5[0:n_, :], ER[0:n_, :], ALU.mult, ["t5", "er"], ["omb"])
            for j in range(4):
                tr(psb(1)[:, j * 128:j * 128 + n_], OMB[0:n_, j * 128:(j + 1) * 128], C["identb"][0:n_, 0:n_], ["omb", "c_identb"], ["ps1"])
            acopy(OBT[:, :, 0:n_], psb(1)[:, 0:512].rearrange("p (j t) -> p j t", j=4)[:, :, 0:n_], [], ["ps1", OBT.name])

        def m3_back(i):
            n_ = tile_np(i)
            tcols = slice(i * 128, (i + 1) * 128)
            OBT = OBTs[i % 2]
            for c_ in range(2):
                mb = 2 + c_
                for j in range(4):
                    if i < NT:
                        mm(PS[mb][0:n_, :], OT[:, j, tcols], WO[:, j, c_ * 512:(c_ + 1) * 512], j == 0, False,
                           [f"OT{h_}" for h_ in (2 * j, 2 * j + 1)] + ["WO"], [f"ps{mb}"])
                    else:
                        mm(PS[mb][0:n_, :], OAT[:, j, 0:n_], WO[:, j, c_ * 512:(c_ + 1) * 512], j == 0, False, ["oat", "WO"], [f"ps{mb}"])
                for j in range(4):
                    mm(PS[mb][0:n_, :], OBT[:, j, 0:n_], WO[:, 4 + j, c_ * 512:(c_ + 1) * 512], False, j == 3, [OBT.name, "WO"], [f"ps{mb}"])
                act(JK[0:n_, :], PS[mb][0:n_, :], AF.Square, [], [f"ps{mb}", "jk", f"ssm{c_}"], accum_out=SSM[0:n_, c_:c_ + 1])
            tt(SSM[0:n_, 2:3], SSM[0:n_, 0:1], SSM[0:n_, 1:2], ALU.add, ["ssm0", "ssm1"], ["ssm2"])
            act(SSM[0:n_, 3:4], SSM[0:n_, 2:3], AF.Ln, ["ssm2", "epst"], ["ssm3"], scale=1.0 / D, bias=EPST[0:n_, 0:1])
            act(RM[0:n_, 0:1], SSM[0:n_, 3:4], AF.Exp, ["ssm3"], ["rm"], scale=-0.5)
            for c_ in range(2):
                stt(TMP[0:n_, c_ * 512:(c_ + 1) * 512], PS[2 + c_][0:n_, :], RM[0:n_, 0:1], GB[0:n_, c_ * 512:(c_ + 1) * 512],
                    ALU.mult, ALU.mult, ["rm", "GB"], [f"ps{2 + c_}", "tmp"])
            tt(X[0:n_, i, :], X[0:n_, i, :], TMP[0:n_, :], ALU.add, [f"x{i}", "tmp"], [f"x{i}"])

        m3_tiles = list(range(NT + (0 if "sgla" in skip else 1)))
        prev_back = None
        for i in m3_tiles + [None]:
            front = None
            if i is not None:
                P.begin_rec()
                m3_front(i)
                front = P.end_rec()
            P.play(front, prev_back)
            prev_back = None
            if i is not None:
                P.begin_rec()
                m3_back(i)
                prev_back = P.end_rec()
        if dbg == "m3":
            P.dma("sp", dbg_x2[:, :], X[0:NS, NT, :], [f"x{NT}"], [], sem="dbg")
            for i in range(NT):
                P.dma("sp", dbg_x[i * 128:(i + 1) * 128, :], X[:, i, :], [f"x{i}"], [], sem="dbg")
            break

        P.barrier()
        ph.reset()
        WDR = ph.alloc("WDR", [128, NBLK, D], BF16)
        YT = ph.alloc("YT", [128, NBLK, 528], BF16)
        HTG = ph.alloc("HTG", [128, 8, 530], BF16)
        HTS = ph.alloc("HTS", [128, 8, NS], BF16)
        WU = [ph.alloc(f"WU{b}", [128, 8, 256], BF16) for b in range(2)]
        UB = [[ph.alloc(f"ub{a}{b}", [128, 514], F32) for b in range(2)] for a in range(2)]
        CBF = [[ph.alloc(f"cb{a}{b}", [128, 512], F32) for b in range(2)] for a in range(2)]
        HBF = ph.alloc("hbf", [128, D], BF16)
        TMPF = ph.alloc("tmpf", [128, D], F32)
        GBF = ph.alloc("GBF", [128, D], F32)
        JKF = ph.alloc("jkf", [128, D], BF16)
        ULAST = ph.alloc("ulast", [128, 2, 2 * NBLK], F32)
        ULS = ph.alloc("uls", [128, 8, 2 * NBLK], F32)
        UBS = [ph.alloc(f"ubs{a}", [128, 4, 6], F32) for a in range(2)]
        CBS = [ph.alloc(f"cbs{a}", [128, 4, 4], F32) for a in range(2)]
        CAR = ph.alloc("car", [128, 8, 2], BF16)
        CARN = ph.alloc("carn", [128, 8, 2], BF16)
        LNF = ph.alloc("lnf", [128, NT + 1], F32)
        SSF = ph.alloc("ssf", [128, 4], F32)
        RMF = ph.alloc("rmf", [128, 4], F32)
        ULT = ph.alloc("ult", [128, 128], F32)

        P.dma("sp", WDR[:, :, :].rearrange("p b c -> p (b c)"), wd_bf[l].ap(), [f"wd_bf{l}"], ["WDR"], sem="wdn")
        P.dma("sp", GA[:], g_ffn_pre[l].partition_broadcast(128), [], ["GA"], sem="lp")
        P.dma("sp", GBF[:], g_ffn_post[l].partition_broadcast(128), [], ["GBF"], sem="lp")
        P.op("dve", lambda e: e.memset(SSQ[:, NT:NT + 1], 1.0), [], [f"ssq{NT}"])
        for i in range(NT + 1):
            n_ = tile_np(i)
            stt(JKF[0:n_, :], X[0:n_, i, :], 1.0, X[0:n_, i, :], ALU.mult, ALU.mult, [f"x{i}", f"ssq{i}"], ["jkf", f"ssq{i}"],
                accum_out=SSQ[0:n_, i:i + 1])
        allss = [f"ssq{i}" for i in range(NT + 1)]
        act(LNF[:, :], SSQ[:, :], AF.Ln, allss + ["epst"], ["lnf"], scale=1.0 / D, bias=EPST[:, 0:1])
        act(RSTD[:, :], LNF[:, :], AF.Exp, ["lnf"], ["rstd"], scale=-0.5)

        def ffn_norm_T(i, dst, c0, dkey):
            n_ = tile_np(i)
            stt(HBF[0:n_, :], X[0:n_, i, :], RSTD[0:n_, i:i + 1], GA[0:n_, :], ALU.mult, ALU.mult, [f"x{i}", "rstd", "GA"], ["hbf"])
            pv = psb(0)
            for k in range(8):
                tr(pv[:, k * 128:k * 128 + n_], HBF[0:n_, k * 128:(k + 1) * 128], C["identb"][0:n_, 0:n_], ["hbf", "c_identb"], ["ps0"])
            acopy(dst[:, :, c0:c0 + n_], pv.rearrange("p (k t) -> p k t", k=8)[:, :, 0:n_], [], ["ps0", dkey])

        ffn_norm_T(NT - 1, HTG, 2 + 384, "htg_x")
        P.dma("sp", h_src.ap().rearrange("p (k c) -> p k c", k=8), HTG[:, :, 2 + 510:2 + 512], ["htg_x"], ["h_src"], sem="hx")
        P.collective([h_src.ap()], [h_all.ap()], GROUPS, ["h_src"], ["h_all"])
        P.dma("sp", CAR[:, :, :], h_all.ap()[0:128, :].rearrange("p (k c) -> p k c", k=8), ["h_all"], ["car"], sem="hx2")
        ts(CAR[:, :, :], CAR[:, :, :], C["flag"][:, 0:1], None, ALU.mult, ALU.bypass, ["car", "c_flag"], ["car"])

        NG = 4
        wcnt = 0

        def group_norms(g):
            vcopy(HTG[:, :, 0:2], (CAR if g == 0 else CARN)[:, :, :], ["car" if g == 0 else "carn", "htg_x"], ["htg_c"] + [f"htg{t}" for t in range(4)])
            for t in range(4):
                ffn_norm_T(4 * g + t, HTG, 2 + t * 128, f"htg{t}")
            if g + 1 < NG:
                vcopy(CARN[:, :, :], HTG[:, :, 512:514], ["htg3"], ["carn"])
            if g == NG - 1:
                ffn_norm_T(NT, HTS, 0, "hts")

        group_norms(0)
        for g in range(NG):
            has_s = (g == NG - 1)
            pend_f2 = [None]
            for blk in range(NBLK):
                wu = WU[wcnt % 2]; wcnt += 1
                P.dma("sp", wu[:, :, :].rearrange("p k c -> p (k c)"), wup_bf[l].ap()[blk * 128:(blk + 1) * 128, :], [f"wupbf{l}"], [wu.name],
                      sem=wu.name)
                sl = blk % 2
                hkeys = ["htg_c"] + [f"htg{t}" for t in range(4)]
                for a in range(2):
                    pb = 1 + 2 * sl + a
                    for k in range(8):
                        mm(PS[pb][:, :], wu[:, k, a * 128:(a + 1) * 128], HTG[:, k, 2:514], k == 0, k == 7, [wu.name] + hkeys, [f"ps{pb}"])
                    cc = (2 * sl + a) * 2
                    if g == 0:
                        for k in range(8):
                            mm(PS[5][:, cc:cc + 2], wu[:, k, a * 128:(a + 1) * 128], HTG[:, k, 0:2], k == 0, k == 7, [wu.name] + hkeys, ["ps5"])
                    if has_s:
                        sc0 = 8 + (2 * sl + a) * 16
                        for k in range(8):
                            mm(PS[5][:, sc0:sc0 + NS], wu[:, k, a * 128:(a + 1) * 128], HTS[:, k, :], k == 0, k == 7, [wu.name, "hts"], ["ps5"])
                for a in range(2):
                    pb = 1 + 2 * sl + a
                    acopy(UB[a][sl][:, 2:514], PS[pb][:, :], [], [f"ps{pb}", UB[a][sl].name])
                for a in range(2):
                    cc = (2 * sl + a) * 2
                    ub = UB[a][sl]
                    if g == 0:
                        acopy(ub[:, 0:2], PS[5][:, cc:cc + 2], [], ["ps5", ub.name])
                    else:
                        P.op("pool", lambda e, ub=ub, a=a, blk=blk: e.tensor_copy(out=ub[:, 0:2], in_=ULAST[:, :, a * NBLK + blk]),
                             [f"ulast{a}_{blk}"], [ub.name])
                for a in range(2):
                    pb = 1 + 2 * sl + a
                    cw = CONVP[:, a * NBLK + blk, :]
                    act(CBF[a][sl][:, :], PS[pb][:, :], AF.Identity, ["convp"], [f"ps{pb}", CBF[a][sl].name], scale=cw[:, 2:3], bias=cw[:, 3:4])
                for a in range(2):
                    ub, cb = UB[a][sl], CBF[a][sl]
                    cw = CONVP[:, a * NBLK + blk, :]
                    stt(cb[:, :], ub[:, 1:513], cw[:, 1:2], cb[:, :], ALU.mult, ALU.add, [ub.name, "convp", cb.name], [cb.name])
                for a in range(2):
                    ub, cb = UB[a][sl], CBF[a][sl]
                    cw = CONVP[:, a * NBLK + blk, :]
                    stt(cb[:, :], ub[:, 0:512], cw[:, 0:1], cb[:, :], ALU.mult, ALU.add, [ub.name, "convp", cb.name], [cb.name])
                for a in range(2):
                    ub = UB[a][sl]
                    P.op("pool", lambda e, ub=ub, a=a, blk=blk: e.tensor_copy(out=ULAST[:, :, a * NBLK + blk], in_=ub[:, 512:514]),
                         [ub.name], [f"ulast{a}_{blk}"])
                def f2(sl=sl, blk=blk):
                    act(CBF[0][sl][:, :], CBF[0][sl][:, :], AF.Gelu_apprx_tanh, [CBF[0][sl].name], [CBF[0][sl].name])
                    tt(YT[:, blk, 0:512], CBF[0][sl][:, :], CBF[1][sl][:, :], ALU.mult, [CBF[0][sl].name, CBF[1][sl].name], [f"yt{blk}"])
                if pend_f2[0] is not None:
                    pend_f2[0]()
                pend_f2[0] = f2
                if has_s:
                    for a in range(2):
                        sc0 = 8 + (2 * sl + a) * 16
                        ubs, cbs = UBS[a], CBS[a]
                        cw = CONVP[:, a * NBLK + blk, :]
                        vcopy(ubs[:, :, 0:2], CPRE[:, a * NBLK + blk, :].rearrange("p (s r) -> p s r", s=4), ["cpre"], [ubs.name])
                        vcopy(ubs[:, :, 2:6], PS[5][:, sc0:sc0 + NS].rearrange("p (s t) -> p s t", s=4), [], ["ps5", ubs.name])
                        ts(cbs[:, :, :], ubs[:, :, 2:6], cw[:, 2:3], cw[:, 3:4], ALU.mult, ALU.add, [ubs.name, "convp"], [cbs.name])
                        stt(cbs[:, :, :], ubs[:, :, 1:5], cw[:, 1:2], cbs[:, :, :], ALU.mult, ALU.add, [ubs.name, "convp", cbs.name], [cbs.name])
                        stt(cbs[:, :, :], ubs[:, :, 0:4], cw[:, 0:1], cbs[:, :, :], ALU.mult, ALU.add, [ubs.name, "convp", cbs.name], [cbs.name])
                        vcopy(ULS[:, :, a * NBLK + blk].rearrange("p (s r) -> p s r", s=4), ubs[:, :, 4:6], [ubs.name], ["uls"])
                    act(CBS[0][:, :, :], CBS[0][:, :, :], AF.Gelu_apprx_tanh, [CBS[0].name], [CBS[0].name])
                    tt(YT[:, blk, 512:528].rearrange("p (s t) -> p s t", s=4), CBS[0][:, :, :], CBS[1][:, :, :], ALU.mult,
                       [CBS[0].name, CBS[1].name], [f"yts{blk}"])
            pend_f2[0]()
            P.begin_rec()
            tiles = [(4 * g + t, slice(t * 128, (t + 1) * 128)) for t in range(4)] + ([(NT, slice(512, 528))] if has_s else [])
            for ti_, (i, tc) in enumerate(tiles):
                n_ = tile_np(i)
                ykeys = [f"yt{b_}" for b_ in range(NBLK)] if i < NT else [f"yts{b_}" for b_ in range(NBLK)]
                fbs = (6, 7) if ti_ % 2 == 0 else (1, 2)
                for c_ in range(2):
                    fb = fbs[c_]
                    for blk in range(NBLK):
                        mm(PS[fb][0:n_, :], YT[:, blk, tc], WDR[:, blk, c_ * 512:(c_ + 1) * 512], blk == 0, blk == NBLK - 1,
                           ykeys + ["WDR"], [f"ps{fb}"])
                    acopy(TMPF[0:n_, c_ * 512:(c_ + 1) * 512], PS[fb][0:n_, :], [], [f"ps{fb}", f"tmpf{c_}"])
                stt(JKF[0:n_, :], TMPF[0:n_, :], 1.0, TMPF[0:n_, :], ALU.mult, ALU.mult, ["tmpf0", "tmpf1"], ["jkf", "ssf2"],
                    accum_out=SSF[0:n_, 2:3])
                act(SSF[0:n_, 3:4], SSF[0:n_, 2:3], AF.Ln, ["ssf2", "epst"], ["ssf3"], scale=1.0 / D, bias=EPST[0:n_, 0:1])
                act(RMF[0:n_, 0:1], SSF[0:n_, 3:4], AF.Exp, ["ssf3"], ["rmf"], scale=-0.5)
                stt(TMPF[0:n_, :], TMPF[0:n_, :], RMF[0:n_, 0:1], GBF[0:n_, :], ALU.mult, ALU.mult, ["tmpf0", "tmpf1", "rmf", "GBF"],
                    ["tmpf0", "tmpf1"])
                tt(X[0:n_, i, :], X[0:n_, i, :], TMPF[0:n_, :], ALU.add, [f"x{i}", "tmpf0", "tmpf1"], [f"x{i}"])
                if last:
                    if i < NT:
                        P.dma("sp", yp[i * 128:(i + 1) * 128, :], X[:, i, :], [f"x{i}"], [], sem="st_y")
                    else:
                        P.dma("sp", ys, X[0:NS, NT, :], [f"x{i}"], [], sem="st_y")
            p2ops = P.end_rec()
            nops = None
            if g + 1 < NG:
                P.begin_rec()
                group_norms(g + 1)
                nops = P.end_rec()
            P.play(p2ops, nops)
        tr(PS[1][0:88, 0:128], ULAST[:, :, :].rearrange("p r b -> p (r b)"), C["identf"][:, :],
           [f"ulast{a_}_{b_}" for a_ in range(2) for b_ in range(NBLK)] + ["c_identf"], ["ps1"])
        acopy(ULT[0:88, :], PS[1][0:88, 0:128], [], ["ps1", "ult"])
        P.dma("sp", cpo[l].rearrange("r (b p) -> (r b) p", p=128), ULT[0:88, :], ["ult"], [], sem="st_c")
        ulsf = ULS[:, :, :].rearrange("p q b -> p (q b)")
        for q_ in range(3):
            rows = 128 if q_ < 2 else 96
            tr(PS[2][0:rows, 0:128], ulsf[:, q_ * 128:q_ * 128 + rows], C["identf"][:, :], ["uls", "c_identf"], ["ps2"])
            acopy(ULT[0:rows, :], PS[2][0:rows, 0:128], ["ult"], ["ps2", "ult"])
            P.dma("sp", cso[l].rearrange("s r (b p) -> (s r b) p", p=128)[q_ * 128:q_ * 128 + rows, :], ULT[0:rows, :], ["ult"], [], sem="st_c")
        if dbg == "ffn":
            for i in range(NT):
                P.dma("sp", dbg_x[i * 128:(i + 1) * 128, :], X[:, i, :], [f"x{i}"], [], sem="dbg")
            break

    P.barrier(final=True)
    if P.unknown:
        print("WARNING: keys read but never written:", sorted(P.unknown))
    P.emit()
    return nc


_NC_CACHE = {}


def _in_maps(inputs):
    f = lambda a: np.ascontiguousarray(np.asarray(a, dtype=np.float32))
    maps = []
    shared = {k: f(inputs[k]) for k in ("g_mix_pre", "g_mix_post", "g_ffn_pre", "g_ffn_post", "w_in", "w_gate2", "b_gate",
                                         "g_gla", "w_out", "w_up", "conv_w", "conv_b", "w_down")}
    x_prompt, x_sample = f(inputs["x_prompt"]), f(inputs["x_sample"])
    ckw, cvw = f(inputs["cache_k_win"]), f(inputs["cache_v_win"])
    sgl, sfc = f(inputs["state_gla"]), f(inputs["state_ffn_conv"])
    consts = [_consts(0), _consts(1)]
    for c in range(8):
        s, half = c // 2, c % 2
        m = dict(shared)
        m["xp"] = np.ascontiguousarray(x_prompt[s, half * TL:(half + 1) * TL, :])
        m["xs"] = np.ascontiguousarray(x_sample[4 * c:4 * c + 4].reshape(NS, D))
        m["ck"] = np.ascontiguousarray(ckw[:, 4 * c:4 * c + 4].reshape(DEPTH, 4, 2048, 512))
        m["cv"] = np.ascontiguousarray(cvw[:, 4 * c:4 * c + 4].reshape(DEPTH, 4, 2048, 512))
        m["sg"] = np.ascontiguousarray(sgl[:, 4 * c:4 * c + 4])
        m["sc"] = np.ascontiguousarray(sfc[:, 4 * c:4 * c + 4])
        for n, v in consts[half].items():
            m["c_" + n] = v
        maps.append(m)
    return maps


def kernel(**inputs):
    if "nc" not in _NC_CACHE:
        _NC_CACHE["nc"] = build_program()
    nc = _NC_CACHE["nc"]
    res = run_bass_kernel_spmd(nc, _in_maps(inputs), core_ids=list(range(8)))
    R = res.results
    B, T = 4, 4096
    y_prompt = np.zeros((B, T, D), np.float32)
    y_sample = np.zeros((32, 4, D), np.float32)
    nk = np.zeros((DEPTH, B, TL, 8, 64), np.float32); nv = np.zeros_like(nk)
    nsp = np.zeros((DEPTH, B, 4, 64, 128), np.float32)
    ncp = np.zeros((DEPTH, B, 2, 2 * DFF), np.float32)
    ksn = np.zeros((DEPTH, 32, 4, 8, 64), np.float32); vsn = np.zeros_like(ksn)
    ssn = np.zeros((DEPTH, 32, 4, 64, 128), np.float32)
    csn = np.zeros((DEPTH, 32, 2, 2 * DFF), np.float32)
    for c in range(8):
        s, half = c // 2, c % 2
        r = R[c]
        y_prompt[s, half * TL:(half + 1) * TL] = r["yp"]
        y_sample[4 * c:4 * c + 4] = r["ys"].reshape(4, 4, D)
        if half == 1:
            nk[:, s] = r["kp"].reshape(DEPTH, TL, 8, 64); nv[:, s] = r["vp"].reshape(DEPTH, TL, 8, 64)
            nsp[:, s] = r["spo"]; ncp[:, s] = r["cpo"]
        ksn[:, 4 * c:4 * c + 4] = r["kso"].reshape(DEPTH, 4, 4, 8, 64)
        vsn[:, 4 * c:4 * c + 4] = r["vso"].reshape(DEPTH, 4, 4, 8, 64)
        ssn[:, 4 * c:4 * c + 4] = r["sso"]; csn[:, 4 * c:4 * c + 4] = r["cso"]
    return (y_prompt, y_sample, nk, nv, nsp, ncp, ksn, vsn, ssn, csn)
```

```python
import numpy as np
import ml_dtypes
import concourse.bass as bass
import concourse.mybir as mybir
from concourse.bass_utils import run_bass_kernel_spmd

F32, BF16 = mybir.dt.float32, mybir.dt.bfloat16
AF = mybir.ActivationFunctionType
ALU = mybir.AluOpType
AX = mybir.AxisListType

D = 1024
TL = 2048
NT = TL // 128
NS = 16
DEPTH = 2
DFF = 2816
NBLK = DFF // 128
INC = 3088
EPS = 1e-6
PAST = 16384
NEG = -30000.0
DEBUG_LAYERS = None


class Prog:
    ENGS = ("pe", "act", "dve", "pool", "sp")

    def __init__(self, nc):
        self.nc = nc
        self.q = {e: [] for e in self.ENGS}
        self.cnt = {e: 0 for e in self.ENGS}
        self.sem = {e: nc.alloc_semaphore("c_" + e) for e in ("pe", "act", "dve", "pool")}
        self.dsem = {}
        self.seen = {e: {} for e in self.ENGS}
        self.lastw = {}
        self.readers = {}
        self.n_wait = 0
        self.log = None
        self.ever = set()
        self.unknown = set()
        self.rec = None
        self.pe_last = None
        self.pend = []
        self.background = set()

    def _dsem(self, name):
        if name not in self.dsem:
            self.dsem[name] = [self.nc.alloc_semaphore("d_" + name), 0]
        return self.dsem[name]

    def _handle(self, tok):
        return self.sem[tok[1]] if tok[0] == "e" else self.dsem[tok[1]][0]

    def _wait(self, eng, tok):
        key = (tok[0], tok[1])
        if self.seen[eng].get(key, 0) >= tok[2]:
            return
        self.seen[eng][key] = tok[2]
        if self.log is not None:
            self.log.append(f"    {eng} WAIT {tok}")
        h, v = self._handle(tok), tok[2]
        self.pend.append((h, v))
        self.n_wait += 1

    def _flush_waits(self, eng, keep_last):
        pend, self.pend = self.pend, []
        last = pend.pop() if (keep_last and pend) else None
        for (h, v) in pend:
            self.q[eng].append(lambda e, h=h, v=v: e.wait_ge(h, v))
        return last

    def _deps(self, eng, reads, writes, is_dma):
        toks = []
        for r in reads:
            if r not in self.ever:
                self.unknown.add(r)
            t = self.lastw.get(r)
            if t is not None:
                toks.append(t)
        for w in writes:
            t = self.lastw.get(w)
            if t is not None and (is_dma or not (t[0] == "e" and t[1] == eng)):
                toks.append(t)
            for t in self.readers.get(w, ()):
                if is_dma or not (t[0] == "e" and t[1] == eng):
                    toks.append(t)
        best = {}
        for t in toks:
            k = (t[0], t[1])
            if best.get(k, 0) < t[2]:
                best[k] = t[2]
        for k, v in best.items():
            self._wait(eng, (k[0], k[1], v))

    def _commit(self, tok, reads, writes):
        self.ever.update(writes)
        for w in writes:
            self.lastw[w] = tok
            self.readers[w] = []
        for r in reads:
            if r not in writes:
                lst = self.readers.setdefault(r, [])
                for i_, t in enumerate(lst):
                    if t[0] == tok[0] and t[1] == tok[1]:
                        if t[2] < tok[2]:
                            lst[i_] = tok
                        break
                else:
                    lst.append(tok)

    def begin_rec(self):
        self.rec = []

    def end_rec(self):
        r, self.rec = self.rec, None
        return r

    def play(self, *lists):
        lists = [l for l in lists if l]
        idx = [0] * len(lists)
        total = sum(len(l) for l in lists)
        for _ in range(total):
            best, bi = None, -1
            for li, l in enumerate(lists):
                if idx[li] < len(l):
                    frac = (idx[li] + 0.5) / len(l)
                    if best is None or frac < best:
                        best, bi = frac, li
            kind, args, kw = lists[bi][idx[bi]]
            idx[bi] += 1
            getattr(self, kind)(*args, **kw)

    def op(self, eng, fn, reads=(), writes=(), pe=None, simple=False):
        if self.rec is not None:
            self.rec.append(("op", (eng, fn, reads, writes), {"pe": pe, "simple": simple}))
            return
        if pe is not None:
            g = set(range(pe[0] // 32, (pe[0] + pe[1] - 1) // 32 + 1))
            b = set(k for k in writes if k.startswith("ps"))
            if self.pe_last is not None and not (g & self.pe_last[0]) and (b & self.pe_last[1]):
                raise RuntimeError(f"PE row-group hazard: disjoint row groups {g}/{self.pe_last[0]} share PSUM bank {b}")
            self.pe_last = (g, b)
        self._deps(eng, reads, writes, False)
        last = self._flush_waits(eng, ATTACH_WAITS and simple)
        self.cnt[eng] += 1
        if self.log is not None:
            self.log.append(f"{eng} #{self.cnt[eng]} r={list(reads)} w={list(writes)}")
        sem = self.sem[eng]
        if last is None:
            self.q[eng].append(lambda e, fn=fn, sem=sem: fn(e).then_inc(sem, 1))
        else:
            def emit_(e, fn=fn, sem=sem, last=last):
                ins = fn(e)
                ins.wait_op(last[0], last[1], "sem-ge")
                ins.then_inc(sem, 1)
            self.q[eng].append(emit_)
        self._commit(("e", eng, self.cnt[eng]), reads, writes)

    def dma(self, eng, out, in_, reads=(), writes=(), sem="misc", **kw):
        if self.rec is not None:
            self.rec.append(("dma", (eng, out, in_, reads, writes, sem), kw))
            return
        self._deps(eng, reads, writes, True)
        self._flush_waits(eng, False)
        ds = self._dsem(sem)
        ds[1] += 16
        h = ds[0]
        self.q[eng].append(lambda e, out=out, in_=in_, h=h, kw=kw: e.dma_start(out=out, in_=in_, **kw).then_inc(h, 16))
        self._commit(("d", sem, ds[1]), reads, writes)

    def collective(self, ins, outs, groups, reads=(), writes=()):
        self._deps("pool", reads, writes, True)
        self._flush_waits("pool", False)
        ds = self._dsem("cc")
        ds[1] += 1
        h = ds[0]
        self.q["pool"].append(lambda e, ins=ins, outs=outs, h=h: e.collective_compute(
            "AllGather", ALU.bypass, replica_groups=groups, ins=ins, outs=outs).then_inc(h, 1))
        self._commit(("d", "cc", ds[1]), reads, writes)

    def barrier(self, final=False):
        toks = [("e", e, self.cnt[e]) for e in self.sem if self.cnt[e] > 0]
        toks += [("d", n, v[1]) for n, v in self.dsem.items() if v[1] > 0 and (final or n not in self.background)]
        for eng in self.ENGS:
            for t in toks:
                if not (t[0] == "e" and t[1] == eng):
                    self._wait(eng, t)
            self._flush_waits(eng, False)
        keep = {k: t for k, t in self.lastw.items() if t[0] == "d" and t[1] in self.background} if not final else {}
        self.lastw.clear()
        self.lastw.update(keep)
        self.readers.clear()

    def emit(self):
        nc = self.nc
        with nc.Block() as block:
            @block.tensor
            def _(e):
                for f in self.q["pe"]:
                    f(e)

            @block.scalar
            def _(e):
                for f in self.q["act"]:
                    f(e)

            @block.vector
            def _(e):
                for f in self.q["dve"]:
                    f(e)

            @block.gpsimd
            def _(e):
                for f in self.q["pool"]:
                    f(e)

            @block.sync
            def _(e):
                for f in self.q["sp"]:
                    f(e)


DBG_T = {}
ATTACH_WAITS = True
PLOG = []
LOG_ON = False


class Arena:
    def __init__(self, nc, base, limit):
        self.nc, self.base, self.limit, self.off, self.n = nc, base, limit, base, 0
        self.offs = {}

    def reset(self, off=None):
        self.off = self.base if off is None else off

    def alloc(self, name, shape, dtype):
        esz = 4 if dtype == F32 else 2
        size = esz
        for s in shape[1:]:
            size *= s
        size = (size + 63) // 64 * 64
        assert self.off + size <= self.limit, (name, self.off, size, self.limit)
        self.n += 1
        t = self.nc.alloc_sbuf_tensor_at(f"{name}_{self.n}", list(shape), dtype, offset=self.off)
        self.off += size
        DBG_T[name] = t.name
        self.offs[t.name] = self.off - size
        return t

    def alias(self, name, shape, dtype, base, byte_off=0):
        self.n += 1
        return self.nc.alloc_sbuf_tensor_at(f"{name}_{self.n}", list(shape), dtype, offset=self.offs[base.name] + byte_off)


def _rope_tables(pos):
    half = 8
    inv = (np.float32(500000.0) ** (-(np.arange(half, dtype=np.float32) / np.float32(half)))).astype(np.float32)
    ang = pos.astype(np.float32)[:, None] * inv[None, :]
    c, s = np.cos(ang).astype(np.float32), np.sin(ang).astype(np.float32)
    k = np.concatenate([c, c, s, s], axis=1)
    return (k * np.float32(0.125)).astype(np.float32), k.astype(np.float32)


def _consts(half):
    bf = ml_dtypes.bfloat16
    c = {}
    c["identb"] = np.eye(128, dtype=np.float32).astype(bf)
    c["identf"] = np.eye(128, dtype=np.float32)
    k = np.arange(128)[:, None]
    q = np.arange(128)[None, :]
    c["ucs"] = (k <= q).astype(np.float32)
    c["onesf"] = np.ones((128, 128), np.float32)
    c["onesb"] = np.ones((128, 64), np.float32).astype(bf)
    mdiag = np.where(k <= q, 0.0, NEG).astype(np.float32)
    mprev = np.where(k >= q, 0.0, NEG).astype(np.float32)
    mpre = mprev if half == 1 else np.full((128, 128), NEG, np.float32)
    c["ma"] = (np.concatenate([mprev, mdiag], axis=1) == 0.0).astype(np.float32).astype(bf)
    c["mb"] = (np.concatenate([mpre, mdiag], axis=1) == 0.0).astype(np.float32).astype(bf)
    pos = half * TL + np.arange(TL)
    rq, rk = _rope_tables(pos)
    c["ropeq"] = np.ascontiguousarray(rq.reshape(NT, 128, 32).transpose(1, 0, 2))
    c["ropek"] = np.ascontiguousarray(rk.reshape(NT, 128, 32).transpose(1, 0, 2))
    sq, sk = _rope_tables(PAST + (np.arange(NS) % 4))
    c["ropesq"], c["ropesk"] = sq, sk
    c["flag"] = np.full((128, 1), float(half), np.float32)
    sm = np.full((128, 4, 9, 16), NEG, np.float32)
    m = np.arange(128)
    for j in range(4):
        for i in range(4):
            qq = 4 * j + i
            sm[m >= i, j, 0, qq] = 0.0
            sm[:, j, 1 + i, qq] = 0.0
            sm[:, j, 5 + i, qq] = 0.0
    c["smask"] = sm.reshape(128, 4 * 9 * 16).astype(bf)
    t = np.arange(16)
    same = (t[:, None] // 4) == (t[None, :] // 4)
    c["mnew"] = (same * ((t[:, None] < t[None, :]) * 1.0 + (t[:, None] == t[None, :]) * 3.0)).astype(np.float32)
    c["ucs_s"] = (same & (t[:, None] <= t[None, :])).astype(np.float32)
    c["seqsel"] = ((t[:, None] // 4) == np.arange(4)[None, :]).astype(np.float32)
    sc = np.zeros((128, 4, 16), np.float32)
    for j in range(4):
        sc[:, j, 4 * j:4 * j + 4] = 1.0
    c["seqcol"] = sc.reshape(128, 64).astype(bf)
    return c


CONST_SPECS = [
    ("identb", [128, 128], BF16), ("identf", [128, 128], F32), ("ucs", [128, 128], F32),
    ("onesf", [128, 128], F32), ("onesb", [128, 64], BF16), ("ma", [128, 256], BF16), ("mb", [128, 256], BF16),
    ("ropeq", [128, NT, 32], F32), ("ropek", [128, NT, 32], F32), ("ropesq", [NS, 32], F32), ("ropesk", [NS, 32], F32),
    ("flag", [128, 1], F32), ("smask", [128, 576], BF16), ("mnew", [16, 16], F32), ("ucs_s", [16, 16], F32),
    ("seqsel", [16, 4], F32), ("seqcol", [128, 64], BF16),
]


def build_program(n_layers=DEPTH, dbg=False, ncores=8, skip=()):
    nc = bass.Bass("TRN2", target_bir_lowering=False)
    P = Prog(nc)
    if dbg and LOG_ON:
        P.log = PLOG

    def din(name, shape, dt=F32):
        return nc.dram_tensor(name, list(shape), dt, kind="ExternalInput").ap()

    def dout(name, shape, dt=F32):
        return nc.dram_tensor(name, list(shape), dt, kind="ExternalOutput").ap()

    xp = din("xp", [TL, D]); xs = din("xs", [NS, D])
    ck = din("ck", [DEPTH, 4, 2048, 512]); cv = din("cv", [DEPTH, 4, 2048, 512])
    sg_in = din("sg", [DEPTH, 4, 4, 64, 128]); sc_in = din("sc", [DEPTH, 4, 2, 2 * DFF])
    g_mix_pre = din("g_mix_pre", [DEPTH, D]); g_mix_post = din("g_mix_post", [DEPTH, D])
    g_ffn_pre = din("g_ffn_pre", [DEPTH, D]); g_ffn_post = din("g_ffn_post", [DEPTH, D])
    w_in = din("w_in", [DEPTH, D, INC]); w_gate2 = din("w_gate2", [DEPTH, 16, 256]); b_gate = din("b_gate", [DEPTH, 256])
    g_gla = din("g_gla", [DEPTH, 128]); w_out = din("w_out", [DEPTH, D, D]); w_up = din("w_up", [DEPTH, D, 2 * DFF])
    conv_w = din("conv_w", [DEPTH, 3, 2 * DFF]); conv_b = din("conv_b", [DEPTH, 2 * DFF]); w_down = din("w_down", [DEPTH, DFF, D])
    cin = {n: din("c_" + n, s, d) for n, s, d in CONST_SPECS}

    yp = dout("yp", [TL, D]); ys = dout("ys", [NS, D])
    kp = dout("kp", [DEPTH, TL, 512]); vp = dout("vp", [DEPTH, TL, 512])
    spo = dout("spo", [DEPTH, 4, 64, 128]); cpo = dout("cpo", [DEPTH, 2, 2 * DFF])
    kso = dout("kso", [DEPTH, NS, 512]); vso = dout("vso", [DEPTH, NS, 512])
    sso = dout("sso", [DEPTH, 4, 4, 64, 128]); cso = dout("cso", [DEPTH, 4, 2, 2 * DFF])
    if dbg:
        dbg_ot = dout("dbg_ot", [512, TL], BF16)
        dbg_x = dout("dbg_x", [TL, D])
        dbg_x2 = dout("dbg_x2", [NS, D])

    k_src = nc.dram_tensor("k_src", [512, TL], BF16)
    v_src = nc.dram_tensor("v_src", [512, TL], BF16)
    k_all = nc.dram_tensor("k_all", [1024, TL], BF16)
    v_all = nc.dram_tensor("v_all", [1024, TL], BF16)
    s_src = nc.dram_tensor("s_src", [128, 256], F32)
    s_all = nc.dram_tensor("s_all", [2 * 128, 256], F32)
    h_src = nc.dram_tensor("h_src", [128, 16], BF16)
    h_all = nc.dram_tensor("h_all", [2 * 128, 16], BF16)
    wup_bf = [nc.dram_tensor(f"wup_bf{l_}", [NBLK * 128, 8 * 256], BF16) for l_ in range(DEPTH)]
    wa_bf = [nc.dram_tensor(f"wa_bf{l_}", [128, 8 * 1536], BF16) for l_ in range(DEPTH)]
    wb_bf = [nc.dram_tensor(f"wb_bf{l_}", [128, 8 * 1040], BF16) for l_ in range(DEPTH)]
    wr_bf = [nc.dram_tensor(f"wr_bf{l_}", [128, 8 * 512], BF16) for l_ in range(DEPTH)]
    wo_bf = [nc.dram_tensor(f"wo_bf{l_}", [128, 8 * D], BF16) for l_ in range(DEPTH)]
    wd_bf = [nc.dram_tensor(f"wd_bf{l_}", [128, NBLK * D], BF16) for l_ in range(DEPTH)]
    GROUPS = [[2 * g_, 2 * g_ + 1] for g_ in range(ncores // 2)]

    B0 = (nc.sbuf_base + 63) // 64 * 64
    TOP = nc.sbuf_top // 64 * 64
    pers = Arena(nc, B0, TOP)
    X = pers.alloc("X", [128, NT + 1, D], F32)
    C = {n: pers.alloc("c_" + n, s, d) for n, s, d in CONST_SPECS}
    RSTD = pers.alloc("rstd", [128, NT + 1], F32)
    SSQ = pers.alloc("ssq", [128, NT + 1], F32)
    EPST = pers.alloc("epst", [128, 1], F32)
    GA = pers.alloc("GA", [128, D], F32)
    GGLA = pers.alloc("ggla", [128, 128], F32)
    BGATE = pers.alloc("bgate", [128, 256], F32)
    WG2 = pers.alloc("wg2", [16, 256], BF16)
    CONVP = pers.alloc("convp", [128, 2 * NBLK, 4], F32)
    CPRE = pers.alloc("cpre", [128, 2 * NBLK, 8], F32)
    ph = Arena(nc, pers.off, TOP)

    PS = [nc.alloc_psum_tensor(f"ps{i}", [128, 512], F32) for i in range(8)]

    def psb(i):
        return PS[i][:, :].bitcast(BF16)

    def act(out, in_, func, r, w, **kw):
        P.op("act", lambda e: e.activation(out=out, in_=in_, func=func, **kw), r, w, simple=("accum_out" not in kw))

    def tt(out, in0, in1, op, r, w, eng="dve"):
        P.op(eng, lambda e: e.tensor_tensor(out=out, in0=in0, in1=in1, op=op), r, w, simple=True)

    def stt(out, in0, scalar, in1, op0, op1, r, w, accum_out=None):
        P.op("dve", lambda e: e.scalar_tensor_tensor(out=out, in0=in0, scalar=scalar, in1=in1, op0=op0, op1=op1,
                                                      accum_out=accum_out), r, w, simple=(accum_out is None))

    def ts(out, in0, s1, s2, op0, op1, r, w):
        P.op("dve", lambda e: e.tensor_scalar(out=out, in0=in0, scalar1=s1, scalar2=s2, op0=op0, op1=op1), r, w, simple=True)

    def vcopy(out, in_, r, w):
        P.op("dve", lambda e: e.tensor_copy(out=out, in_=in_), r, w, simple=True)

    def acopy(out, in_, r, w):
        P.op("act", lambda e: e.copy(out=out, in_=in_), r, w, simple=True)

    def mm(out, lhsT, rhs, start, stop, r, w):
        P.op("pe", lambda e: e.matmul(out, lhsT, rhs, start=start, stop=stop, skip_group_check=True), r, w,
             pe=(lhsT.start_partition(), lhsT.partition_size()), simple=True)

    def tr(out, in_, ident, r, w):
        P.op("pe", lambda e: e.transpose(out, in_, ident), r, w, pe=(in_.start_partition(), in_.partition_size()), simple=True)

    def rsqrt_cols(dst, src, n, scale, r, w, tmp):
        act(tmp, src, AF.Ln, r, [w + "_t"], scale=scale, bias=EPST[0:n, 0:1])
        act(dst, tmp, AF.Exp, [w + "_t"], [w], scale=-0.5)

    for n, s, d in CONST_SPECS:
        P.dma("sp", C[n][:], cin[n], [], ["c_" + n], sem="const")
    P.op("dve", lambda e: e.memset(EPST[:], EPS), [], ["epst"])
    P.op("dve", lambda e: e.memset(X[:, NT, :], 0.0), [], ["x16"])
    for i in range(NT):
        P.dma("sp", X[:, i, :], xp[i * 128:(i + 1) * 128, :], [], [f"x{i}"], sem="xin")
    P.dma("sp", X[0:NS, NT, :], xs, [], ["x16"], sem="xin")
    ALLC = ["c_" + n for n, _, _ in CONST_SPECS] + ["epst"]

    def convert_weights(l_, which):
        win_ = w_in[l_].rearrange("(k p) c -> p k c", p=128)
        def bsem(nm):
            P.background.add(f"bg{nm}{l_}")
            return f"bg{nm}{l_}"
        if which == "a":
            P.dma("pool", wa_bf[l_].ap().rearrange("p (k c) -> p k c", k=8), win_[:, :, 0:1536], [], [f"wa_bf{l_}"], sem=bsem("wa"))
            return
        wbv = wb_bf[l_].ap().rearrange("p (k c) -> p k c", k=8)
        P.dma("pool", wbv[:, :, 0:1024], win_[:, :, 1536:2560], [], [f"wb_bf{l_}"], sem=bsem("wb"))
        P.dma("pool", wbv[:, :, 1024:1040], win_[:, :, 3072:3088], [], [f"wb_bf{l_}"], sem=bsem("wb"))
        P.dma("pool", wo_bf[l_].ap().rearrange("p (k c) -> p k c", k=8), w_out[l_].rearrange("(k p) c -> p k c", p=128), [], [f"wo_bf{l_}"], sem=bsem("wo"))
        P.dma("pool", wr_bf[l_].ap().rearrange("p (k c) -> p k c", k=8), win_[:, :, 2560:3072], [], [f"wr_bf{l_}"], sem=bsem("wr"))
        P.dma("pool", wd_bf[l_].ap().rearrange("p (b c) -> p b c", b=NBLK), w_down[l_].rearrange("(b p) c -> p b c", p=128), [], [f"wd_bf{l_}"], sem=bsem("wd"))
        wsrc_ = w_up[l_].rearrange("(k p) c -> p k c", p=128)
        for blk in range(NBLK):
            dst_ = wup_bf[l_].ap()[blk * 128:(blk + 1) * 128, :].rearrange("p (k c) -> p k c", k=8)
            P.dma("pool", dst_[:, :, 0:128], wsrc_[:, :, blk * 128:(blk + 1) * 128], [], [f"wupbf{l_}"], sem=bsem("wu"))
            P.dma("pool", dst_[:, :, 128:256], wsrc_[:, :, DFF + blk * 128:DFF + (blk + 1) * 128], [], [f"wupbf{l_}"], sem=bsem("wu"))

    def tile_np(i):
        return 128 if i < NT else NS

    for l in range(n_layers):
        last = (l == n_layers - 1)
        P.barrier()
        ph.reset()
        OT = ph.alloc("OT", [128, 4, TL], BF16)
        SQT = ph.alloc("sqT", [128, 4, NS], BF16)
        SKT = ph.alloc("skT", [128, 4, NS], BF16)
        SVB = ph.alloc("svb", [NS, 512], BF16)
        OMS = ph.alloc("oms", [NS, D], BF16)
        ot_end = ph.off
        QT = ph.alloc("QT", [128, 4, TL], BF16)
        KT = ph.alloc("KT", [128, 4, TL], BF16)
        VT = ph.alloc("VT", [128, 4, TL], BF16)
        keep_off = ph.off
        WA = ph.alloc("WA", [128, 8, 1536], BF16)
        HB = [ph.alloc(f"hb{b}", [128, D], BF16) for b in range(2)]
        HT = [ph.alloc(f"hT{b}", [128, 8, 128], BF16) for b in range(2)]
        JUNK = ph.alloc("junk", [128, D], BF16)
        KF = ph.alloc("kf", [128, 512], F32)
        VF = ph.alloc("vf", [128, 512], F32)
        QB = ph.alloc("qb", [128, 512], BF16)
        KB = ph.alloc("kb", [128, 512], BF16)
        VB = ph.alloc("vb", [128, 512], BF16)
        T1 = ph.alloc("t1", [128, 8, 16], F32)
        T2 = ph.alloc("t2", [128, 8, 16], F32)
        LNT = ph.alloc("lnt", [128, NT + 1], F32)

        if skip:
            for t_ in (OT, QT, KT, VT):
                P.op("dve", lambda e, t_=t_: e.memset(t_[:, :, :], 0.0), [], [t_.name] + [f"{t_.name}{i}" for i in range(NT)])
        if l == 0:
            for cg in range(3):
                P.dma("pool", WA[:, :, cg * 512:(cg + 1) * 512], w_in[l].rearrange("(k p) c -> p k c", p=128)[:, :, cg * 512:(cg + 1) * 512],
                      [], [f"WA{cg}"], sem=f"win{cg}")
        else:
            P.dma("sp", WA[:, :, :].rearrange("p k c -> p (k c)"), wa_bf[l].ap(), [f"wa_bf{l}"], ["WA0", "WA1", "WA2"], sem="win")
        P.dma("sp", GA[:], g_mix_pre[l].partition_broadcast(128), [], ["GA"], sem="lp")

        for r_ in range(3):
            P.dma("sp", CONVP[:, :, r_], conv_w[l, r_].rearrange("(b p) -> p b", p=128), [], ["convp"], sem="lp3",
                  allow_slow_non_contiguous=True)
        P.dma("sp", CONVP[:, :, 3], conv_b[l].rearrange("(b p) -> p b", p=128), [], ["convp"], sem="lp3",
              allow_slow_non_contiguous=True)
        for s_ in range(4):
            for r_ in range(2):
                P.dma("sp", CPRE[:, :, s_ * 2 + r_], sc_in[l, s_, r_].rearrange("(b p) -> p b", p=128), [], ["cpre"], sem="lp3",
                      allow_slow_non_contiguous=True)

        P.op("dve", lambda e: e.memset(SSQ[:, NT:NT + 1], 1.0), [], [f"ssq{NT}"])
        for i in range(NT + 1):
            n_ = tile_np(i)
            stt(JUNK[0:n_, :], X[0:n_, i, :], 1.0, X[0:n_, i, :], ALU.mult, ALU.mult, [f"x{i}", f"ssq{i}"], ["junk", f"ssq{i}"],
                accum_out=SSQ[0:n_, i:i + 1])
        if True:
            allss = [f"ssq{i}" for i in range(NT + 1)]
            act(LNT[:, :], SSQ[:, :], AF.Ln, allss + ["epst"], ["lnt"], scale=1.0 / D, bias=EPST[:, 0:1])
            act(RSTD[:, :], LNT[:, :], AF.Exp, ["lnt"], ["rstd"], scale=-0.5)

        def norm_and_transpose(i, hb, hT, pbank, gkey="GA"):
            n_ = tile_np(i)
            stt(hb[0:n_, :], X[0:n_, i, :], RSTD[0:n_, i:i + 1], GA[0:n_, :], ALU.mult, ALU.mult,
                [f"x{i}", "rstd", gkey], [hb.name])
            pv = psb(pbank)
            for k in range(8):
                tr(pv[:, k * 128:k * 128 + n_], hb[0:n_, k * 128:(k + 1) * 128], C["identb"][0:n_, 0:n_],
                   [hb.name, "c_identb"], [f"ps{pbank}"])
            src = pv.rearrange("p (k t) -> p k t", k=8)[:, :, 0:n_]
            acopy(hT[:, :, 0:n_], src, [], [f"ps{pbank}", hT.name])

        def rope(ps_ap, tab, out_ap, n_, rkeys, wkeys, tabkey):
            pv = ps_ap.rearrange("p (h d) -> p h d", h=8)[:, :, 0:16]
            ov = out_ap.rearrange("p (h d) -> p h d", h=8)
            cc = tab[:, 0:16].unsqueeze(1).broadcast_to([n_, 8, 16])
            ss = tab[:, 16:32].unsqueeze(1).broadcast_to([n_, 8, 16])
            tt(T1[0:n_], pv, cc, ALU.mult, rkeys + [tabkey], ["t1"] + [k for k in wkeys if k.startswith("ps")])
            tt(T2[0:n_], pv, ss, ALU.mult, rkeys + [tabkey], ["t2"] + [k for k in wkeys if k.startswith("ps")])
            tt(ov[:, :, 0:8], T1[0:n_, :, 0:8], T2[0:n_, :, 8:16], ALU.subtract, ["t1", "t2"], wkeys)
            tt(ov[:, :, 8:16], T1[0:n_, :, 8:16], T2[0:n_, :, 0:8], ALU.add, ["t1", "t2"], wkeys)

        PBANKS = ((2, 3, 4), (1, 6, 7))

        def m1a_front(i):
            n_ = tile_np(i)
            b = i % 2
            hb, hT = HB[b], HT[b]
            norm_and_transpose(i, hb, hT, 0)
            for cg in range(3):
                pb_ = PBANKS[b][cg]
                for k in range(8):
                    mm(PS[pb_][0:n_, :], hT[:, k, 0:n_], WA[:, k, cg * 512:(cg + 1) * 512], k == 0, k == 7,
                       [hT.name, f"WA{cg}"], [f"ps{pb_}"])

        def m1a_back(i):
            n_ = tile_np(i)
            bq, bk, bv = PBANKS[i % 2]
            tq = C["ropeq"][:, i, :] if i < NT else C["ropesq"][:, :]
            tk = C["ropek"][:, i, :] if i < NT else C["ropesk"][:, :]
            tqk = "c_ropeq" if i < NT else "c_ropesq"
            tkk = "c_ropek" if i < NT else "c_ropesk"
            act(QB[0:n_, :], PS[bq][0:n_, :], AF.Copy, [], [f"ps{bq}", QB.name], scale=0.125)
            rope(PS[bq][0:n_, :], tq[0:n_], QB[0:n_, :], n_, [], [f"ps{bq}", QB.name], tqk)
            acopy(KF[0:n_, :], PS[bk][0:n_, :], [], [f"ps{bk}", "kf"])
            rope(PS[bk][0:n_, :], tk[0:n_], KF[0:n_, :], n_, [], [f"ps{bk}", "kf"], tkk)
            acopy(KB[0:n_, :], KF[0:n_, :], ["kf"], [KB.name])
            acopy(VF[0:n_, :], PS[bv][0:n_, :], [], [f"ps{bv}", "vf"])
            vdst = VB if i < NT else SVB
            vcopy(vdst[0:n_, :], VF[0:n_, :], ["vf"], [vdst.name])
            if i < NT:
                P.dma("sp", kp[l, i * 128:(i + 1) * 128, :], KF[:, :], ["kf"], [], sem="st_kf")
                P.dma("sp", vp[l, i * 128:(i + 1) * 128, :], VF[:, :], ["vf"], [], sem="st_vf")
            else:
                P.dma("sp", kso[l], KF[0:NS, :], ["kf"], [], sem="st_kf")
                P.dma("sp", vso[l], VF[0:NS, :], ["vf"], [], sem="st_vf")
            for (src, half_, dstT, sdst) in ((QB, 0, QT, SQT), (KB, 1, KT, SKT), (VB, 0, VT, None)):
                if i == NT and sdst is None:
                    continue
                pv = psb(5)[:, half_ * 512:(half_ + 1) * 512]
                for j in range(4):
                    tr(pv[:, j * 128:j * 128 + n_], src[0:n_, j * 128:(j + 1) * 128], C["identb"][0:n_, 0:n_],
                       [src.name, "c_identb"], ["ps5"])
                srcv = pv.rearrange("p (j t) -> p j t", j=4)[:, :, 0:n_]
                if i < NT:
                    vcopy(dstT[:, :, i * 128:(i + 1) * 128], srcv, [], ["ps5", f"{dstT.name}{i}"])
                else:
                    vcopy(sdst[:, :, :], srcv, [], ["ps5", sdst.name])

        def emit_exchange():
            ktk = [f"{KT.name}{i}" for i in range(NT)]
            vtk = [f"{VT.name}{i}" for i in range(NT)]
            for j in range(4):
                P.dma("sp", k_src.ap()[j * 128:(j + 1) * 128, :], KT[:, j, :], ktk, ["k_src"], sem="kvx")
                P.dma("sp", v_src.ap()[j * 128:(j + 1) * 128, :], VT[:, j, :], vtk, ["v_src"], sem="kvx")
            P.collective([k_src.ap()], [k_all.ap()], GROUPS, ["k_src"], ["k_all"])
            P.collective([v_src.ap()], [v_all.ap()], GROUPS, ["v_src"], ["v_all"])

        prev_back = None
        for i in ([] if "m1a" in skip else list(range(NT + 1)) + [None]):
            front = None
            if i is not None:
                P.begin_rec()
                m1a_front(i)
                front = P.end_rec()
            P.play(front, prev_back)
            prev_back = None
            if i == NT:
                emit_exchange()
            if i is not None:
                P.begin_rec()
                m1a_back(i)
                prev_back = P.end_rec()

        if dbg == "m1a":
            break

        P.barrier()
        ph.reset(keep_off)
        KTP = [ph.alloc(f"ktp{b}", [128, TL], BF16) for b in range(2)]
        VTP = [ph.alloc(f"vtp{b}", [128, TL], BF16) for b in range(2)]
        ACC = [ph.alloc(f"acc{b}", [128, TL], F32) for b in range(2)]
        RDEN = ph.alloc("rden", [128, TL], F32)
        PT = [ph.alloc(f"pt{b}", [128, 256], BF16) for b in range(4)]
        NVA = 12
        VA = [[ph.alloc(f"va{e}_{s}", [128, 128], BF16) for s in range(NVA)] for e in range(2)]
        for e_ in range(2):
            for s_ in range(NVA):
                t_ = VA[e_][s_]
                P.op("dve", lambda e, t_=t_: e.memset(t_[:, :], 1.0), [], [t_.name])

        def load_prefix(j):
            b = j % 2
            P.dma("sp", KTP[b][:, :], k_all.ap()[j * 128:(j + 1) * 128, :], ["k_all"], [KTP[b].name], sem=f"pre{b}")
            P.dma("sp", VTP[b][:, :], v_all.ap()[j * 128:(j + 1) * 128, :], ["v_all"], [VTP[b].name], sem=f"pre{b}")

        if l == 0:
            convert_weights(0, "rest")
        load_prefix(0)
        cnt = {"sb": 0, "vb": 0, "pt": 0, "va": [0, 0], "tb": 0, "cp": 0}
        for h in ([] if "m2" in skip else range(8)):
            j, e_ = h // 2, h % 2
            if e_ == 0 and j + 1 < 4:
                load_prefix(j + 1)
            rs = slice(e_ * 64, (e_ + 1) * 64)
            ab = h % 2
            acc = ACC[ab]
            ktp, vtp = KTP[j % 2], VTP[j % 2]
            vcols = slice(0, 64) if e_ == 0 else slice(64, 128)

            def build_v(src, srckeys, cols):
                s_ = cnt["va"][e_] % NVA
                cnt["va"][e_] += 1
                va = VA[e_][s_]
                tb = 6 + (cnt["tb"] % 2)
                sl = (cnt["tb"] // 2) % 8
                cnt["tb"] += 1
                pv = psb(tb)[:, sl * 64:(sl + 1) * 64]
                tr(pv, src[rs, cols], C["identb"][rs, rs], srckeys + ["c_identb"], [f"ps{tb}"])
                if cnt["cp"] % 2 == 0:
                    vcopy(va[:, vcols], pv, [], [f"ps{tb}", va.name])
                else:
                    acopy(va[:, vcols], pv, [], [f"ps{tb}", va.name])
                cnt["cp"] += 1
                return va

            jobs = []
            for d_ in (1, 4, 16):
                for r in range(d_):
                    for i in range(16 // d_):
                        jobs.append({"d": d_, "r": r, "i": i})
            chain = {}

            def s1(jb):
                d_, r, i = jb["d"], jb["r"], jb["i"]
                span = d_ * 127 + 1
                p0 = TL - 128 * d_ + r
                if i == 0:
                    chain[(d_, r)] = build_v(vtp, [vtp.name], slice(p0, p0 + span, d_))
                vprev = chain[(d_, r)]
                c0 = d_ * 128 * i + r
                cols = slice(c0, c0 + span, d_)
                blks = list(range((d_ * 128 * i) // 128, (d_ * 128 * (i + 1)) // 128))
                vdiag = build_v(VT[:, j, :], [f"{VT.name}{b_}" for b_ in blks], cols)
                chain[(d_, r)] = vdiag
                sb = cnt["sb"] % 4; cnt["sb"] += 1
                qk_r = [f"{QT.name}{b_}" for b_ in blks]
                if i == 0:
                    mm(PS[sb][:, 0:128], ktp[rs, slice(p0, p0 + span, d_)], QT[rs, j, cols], True, False,
                       [ktp.name] + qk_r, [f"ps{sb}"])
                else:
                    pc0 = d_ * 128 * (i - 1) + r
                    pblks = list(range((d_ * 128 * (i - 1)) // 128, (d_ * 128 * i) // 128))
                    mm(PS[sb][:, 0:128], KT[rs, j, slice(pc0, pc0 + span, d_)], QT[rs, j, cols], True, False,
                       [f"{KT.name}{b_}" for b_ in pblks] + qk_r, [f"ps{sb}"])
                mm(PS[sb][:, 128:256], KT[rs, j, cols], QT[rs, j, cols], False, True,
                   [f"{KT.name}{b_}" for b_ in blks] + qk_r, [f"ps{sb}"])
                jb.update(vprev=vprev, vdiag=vdiag, sb=sb, cols=cols, blks=blks)

            def s2a(jb):
                sb = jb["sb"]
                pt = PT[cnt["pt"] % 4]; cnt["pt"] += 1
                act(pt[:, :], PS[sb][:, 0:256], AF.Exp, [], [f"ps{sb}", pt.name])
                msk = C["mb"] if jb["i"] == 0 else C["ma"]
                tt(pt[:, :], pt[:, :], msk[:, :], ALU.mult, [pt.name, "c_ma", "c_mb"], [pt.name])
                jb["pt"] = pt

            def s2(jb):
                sb, vprev, vdiag, pt = jb["sb"], jb["vprev"], jb["vdiag"], jb["pt"]
                vb = 4 + cnt["vb"] % 2; cnt["vb"] += 1
                mm(PS[vb][:, 0:128], vprev[:, :], pt[:, 0:128], True, False, [vprev.name, pt.name], [f"ps{vb}"])
                mm(PS[vb][:, 0:128], vdiag[:, :], pt[:, 128:256], False, True, [vdiag.name, pt.name], [f"ps{vb}"])
                jb["vb"] = vb

            def s3(jb):
                d_, r, i, vb, cols, blks = jb["d"], jb["r"], jb["i"], jb["vb"], jb["cols"], jb["blks"]
                if d_ == 1:
                    acopy(acc[:, cols], PS[vb][:, 0:128], [], [f"ps{vb}", f"A1_{ab}_{i}"])
                elif d_ == 4:
                    tt(acc[:, cols], PS[vb][:, 0:128], acc[:, cols], ALU.add,
                       [f"A1_{ab}_{b_}" for b_ in blks], [f"ps{vb}", f"A4_{ab}_{r}_{i}"])
                else:
                    tt(acc[:, cols], PS[vb][:, 0:128], acc[:, cols], ALU.add,
                       [f"A1_{ab}_{b_}" for b_ in range(16)] + [f"A4_{ab}_{r % 4}_{i_}" for i_ in range(4)],
                       [f"ps{vb}", f"A16_{ab}_{r}"])

            nj = len(jobs)
            for k_ in range(nj + 3):
                if k_ < nj:
                    s1(jobs[k_])
                if 0 <= k_ - 1 < nj:
                    s2a(jobs[k_ - 1])
                if 0 <= k_ - 2 < nj:
                    s2(jobs[k_ - 2])
                if 0 <= k_ - 3 < nj:
                    s3(jobs[k_ - 3])
            allacc = ([f"A1_{ab}_{b_}" for b_ in range(16)] + [f"A4_{ab}_{r}_{i_}" for r in range(4) for i_ in range(4)]
                      + [f"A16_{ab}_{r}" for r in range(16)])
            urs = rs
            drs = slice(64, 128) if e_ == 0 else slice(0, 64)
            P.op("dve", lambda e, urs=urs, drs=drs, acc=acc: e.reciprocal(out=RDEN[urs, :], in_=acc[drs, :]), allacc, ["rden"])
            tt(OT[urs, j, :], acc[urs, :], RDEN[urs, :], ALU.mult, allacc + ["rden"], [f"OT{h}"] + allacc)
        P.barrier()
        ph.reset(ot_end)
        KC = [ph.alloc(f"kc{b}", [128, 9, 512], BF16) for b in range(2)]
        VC = [ph.alloc(f"vc{b}", [128, 9, 512], BF16) for b in range(2)]
        KCT = ph.alloc("kct", [128, 4, 9, 128], BF16)
        PTS = [ph.alloc(f"pts{b}", [128, 144], BF16) for b in range(2)]
        PNF = ph.alloc("pnf", [16, 2, 4, 16], F32)
        PNB = ph.alloc("pnb", [16, 2, 4, 16], BF16)
        RDS = ph.alloc("rds", [16, 8], F32)

        def load_cache(j):
            b = j % 2
            for (dst, src) in ((KC[b], ck), (VC[b], cv)):
                P.dma("pool", dst[:, 0, :], src[l, j, 1920:2048, :], [], [dst.name], sem=dst.name)
                P.dma("pool", dst[:, 1:5, :], src[l, j, 1536:2048, :].rearrange("(m i) c -> m i c", i=4), [], [dst.name], sem=dst.name)
                P.dma("pool", dst[:, 5:9, :], src[l, j].rearrange("(m r) c -> m r c", r=16)[:, 0:4, :], [], [dst.name], sem=dst.name)

        P.op("dve", lambda e: e.memset(PS[4][0:16, :], 0.0), [], ["ps4"])
        P.op("dve", lambda e: e.memset(PS[5][0:16, 0:8], 0.0), [], ["ps5"])
        load_cache(0)
        scn = 0
        for j in range(4):
            if j + 1 < 4:
                load_cache(j + 1)
            kc, vc = KC[j % 2], VC[j % 2]
            for p_ in range(4):
                for tau in range(9):
                    bank, col = (0, tau * 128) if tau < 8 else (1, 0)
                    tr(psb(bank)[:, col:col + 128], kc[:, tau, p_ * 128:(p_ + 1) * 128], C["identb"][:, :], [kc.name, "c_identb"], [f"ps{bank}"])
                acopy(KCT[:, p_, 0:8, :], psb(0)[:, :].rearrange("p (t k) -> p t k", t=8), [], ["ps0", f"kct{p_}"])
                vcopy(KCT[:, p_, 8, :], psb(1)[:, 0:128], [], ["ps1", f"kct{p_}"])
            for h in range(8):
                p_, e_ = h // 2, h % 2
                rs = slice(e_ * 64, (e_ + 1) * 64)
                sb = 2 + scn % 2
                pts = PTS[scn % 2]; scn += 1
                mm(PS[sb][:, 0:144], C["identb"][:, :], C["smask"][:, j * 144:(j + 1) * 144], True, False, ["c_identb", "c_smask"], [f"ps{sb}"])
                for tau in range(9):
                    mm(PS[sb][:, tau * 16:(tau + 1) * 16], KCT[rs, p_, tau, :], SQT[rs, p_, :], False, tau == 8, [f"kct{p_}", SQT.name], [f"ps{sb}"])
                act(pts[:, :], PS[sb][:, 0:144], AF.Exp, [], [f"ps{sb}", pts.name])
                for tau in range(9):
                    mm(PS[4][0:16, h * 64:(h + 1) * 64], pts[:, tau * 16:(tau + 1) * 16], vc[:, tau, h * 64:(h + 1) * 64], False, False,
                       [pts.name, vc.name], ["ps4"])
                    mm(PS[5][0:16, h:h + 1], pts[:, tau * 16:(tau + 1) * 16], C["onesb"][:, 0:1], False, False, [pts.name, "c_onesb"], ["ps5"])
        for e_ in range(2):
            rs = slice(e_ * 64, (e_ + 1) * 64)
            nb = 6 + e_
            for p_ in range(4):
                mm(PS[nb][0:16, p_ * 16:(p_ + 1) * 16], SKT[rs, p_, :], SQT[rs, p_, :], True, True, [SKT.name, SQT.name], [f"ps{nb}"])
            act(PNF[:, e_, :, :], PS[nb][0:16, 0:64].rearrange("p (a t) -> p a t", a=4), AF.Exp, [], [f"ps{nb}", f"pnf{e_}"])
            tt(PNB[:, e_, :, :], PNF[:, e_, :, :], C["mnew"][:, :].unsqueeze(1).broadcast_to([16, 4, 16]), ALU.mult,
               [f"pnf{e_}", "c_mnew"], [f"pnb{e_}"])
        for h in range(8):
            p_, e_ = h // 2, h % 2
            mm(PS[4][0:16, h * 64:(h + 1) * 64], PNB[:, e_, p_, :], SVB[0:16, h * 64:(h + 1) * 64], False, False, [f"pnb{e_}", SVB.name], ["ps4"])
            mm(PS[5][0:16, h:h + 1], PNB[:, e_, p_, :], C["onesb"][0:16, 0:1], False, False, [f"pnb{e_}", "c_onesb"], ["ps5"])
        P.op("dve", lambda e: e.reciprocal(out=RDS[:, :], in_=PS[5][0:16, 0:8]), [], ["ps5", "rds"])
        tt(OMS[0:16, 0:512].rearrange("p (h d) -> p h d", h=8), PS[4][0:16, :].rearrange("p (h d) -> p h d", h=8),
           RDS[:, :].unsqueeze(2).broadcast_to([16, 8, 64]), ALU.mult, ["rds"], ["ps4", "oms_a"])
        if dbg == "m2s":
            P.dma("pool", dbg_x[0:16, 0:512], OMS[0:16, 0:512], ["oms_a"], [], sem="dbg")
            break
        if dbg == "m2":
            for j in range(4):
                P.dma("sp", dbg_ot[j * 128:(j + 1) * 128, :], OT[:, j, :], [f"OT{h_}" for h_ in range(8)], [], sem="dbg")
            break

        P.barrier()
        ph.reset(ot_end)
        OB = ph.alloc("OB", [128, NT + 1, 512], F32)
        QH = ph.alloc("QH", [128, 2, TL], BF16)
        SFIN = ph.alloc("sfin", [128, 2, 128], F32)
        SA_F = ph.alloc("sa_f", [128, 2, 128], F32)
        SA_B = ph.alloc("sa_b", [128, 2, 128], BF16)
        keep2_off = ph.off
        WB = ph.alloc("WB", [128, 8, 1040], BF16)
        par = lambda nm, shp, dt: [ph.alloc(f"{nm}{b}", shp, dt) for b in range(2)]
        HB1s, HT1s = par("hb", [128, D], BF16), par("hT", [128, 8, 128], BF16)
        LAs, EBs, ENBs, EGs = par("la", [128, 256], F32), par("eb", [128, 256], F32), par("enb", [128, 256], F32), par("eg", [128, 256], F32)
        QTLs, KTLs, QHLs = par("qtl", [128, 256], BF16), par("ktl", [128, 256], BF16), par("qhl", [128, 256], BF16)
        VBFs = par("vbf", [128, 512], BF16)
        QTTs, KTTs = par("qtt", [128, 2, 128], BF16), par("ktt", [128, 2, 128], BF16)
        AMs_ = par("am", [128, 4, 128], BF16)
        GLs, GLTs = par("gl", [128, 16], BF16), par("glt", [16, 128], BF16)
        ETOTs = par("etot", [128, 2], F32)
        GACC = ph.alloc("gacc", [128, 256], F32)
        S_F = ph.alloc("s_f", [128, 2, 128], F32)
        S_B = ph.alloc("s_b", [128, 2, 128], BF16)
        TOTA = ph.alloc("tota", [128, 2], F32)

        P.dma("sp", WB[:, :, :].rearrange("p k c -> p (k c)"), wb_bf[l].ap(), [f"wb_bf{l}"], ["WB"], sem="win")
        P.dma("sp", GGLA[:], g_gla[l].partition_broadcast(128), [], ["ggla"], sem="lp")
        P.dma("sp", BGATE[:], b_gate[l].partition_broadcast(128), [], ["bgate"], sem="lp")
        P.dma("pool", WG2[:], w_gate2[l], [], ["wg2"], sem="lp2")
        if l == 0 and n_layers > 1:
            convert_weights(1, "a")
            convert_weights(1, "rest")
        P.op("dve", lambda e: e.memset(S_F[:], 0.0), [], ["s_f"])
        P.op("dve", lambda e: e.memset(S_B[:], 0.0), [], ["s_b"])
        P.op("dve", lambda e: e.memset(GACC[:], 0.0), [], ["gacc"])
        P.op("dve", lambda e: e.memset(TOTA[:], 0.0), [], ["tota"])
        ONE1 = C["onesf"][:, 0:1]

        def gla_front(i, n_, b, sample=False):
            HB1, HT1, LA, EB, ENB, EG = HB1s[b], HT1s[b], LAs[b], EBs[b], ENBs[b], EGs[b]
            QTL, KTL, QHL, VBF, GL, GLT, ETOT = QTLs[b], KTLs[b], QHLs[b], VBFs[b], GLs[b], GLTs[b], ETOTs[b]
            norm_and_transpose(i, HB1, HT1, 0)
            for (bank, c0, cw) in ((1, 0, 512), (2, 512, 512), (4, 1024, 16)):
                for k in range(8):
                    mm(PS[bank][0:n_, 0:cw], HT1[:, k, 0:n_], WB[:, k, c0:c0 + cw], k == 0, k == 7, [HT1.name, "WB"], [f"ps{bank}"])
            acopy(GL[0:n_, :], PS[4][0:n_, 0:16], [], ["ps4", GL.name])
            tr(psb(3)[0:16, 0:n_], GL[0:n_, 0:16], C["identb"][0:n_, 0:n_], [GL.name, "c_identb"], ["ps3"])
            vcopy(GLT[0:16, 0:n_], psb(3)[0:16, 0:n_], [], ["ps3", GLT.name])
            mm(PS[4][0:n_, 256:512], GLT[0:16, 0:n_], WG2[0:16, :], True, True, [GLT.name, "wg2"], ["ps4"])
            tt(LA[0:n_, :], PS[4][0:n_, 256:512], BGATE[0:n_, :], ALU.add, ["bgate"], ["ps4", LA.name])
            act(LA[0:n_, :], LA[0:n_, :], AF.Exp, [LA.name], [LA.name], scale=-1.0)
            act(LA[0:n_, :], LA[0:n_, :], AF.Ln, [LA.name, "c_onesf"], [LA.name], bias=ONE1[0:n_, :])
            cs = C["ucs_s"] if sample else C["ucs"]
            mm(PS[5][0:n_, 0:256], cs[0:n_, 0:n_], LA[0:n_, :], True, True, [LA.name, "c_ucs", "c_ucs_s"], ["ps5"])
            act(EB[0:n_, :], PS[5][0:n_, 0:256], AF.Exp, [], ["ps5", EB.name], scale=-1.0 / 16)
            act(ENB[0:n_, :], PS[5][0:n_, 0:256], AF.Exp, [], ["ps5", ENB.name], scale=1.0 / 16)
            stt(QTL[0:n_, :], PS[1][0:n_, 0:256], 0.125, EB[0:n_, :], ALU.mult, ALU.mult, [EB.name], ["ps1", QTL.name])
            tt(KTL[0:n_, :], PS[1][0:n_, 256:512], ENB[0:n_, :], ALU.mult, [ENB.name], ["ps1", KTL.name])
            acopy(VBF[0:n_, :], PS[2][0:n_, :], [], ["ps2", VBF.name])
            if sample:
                return
            mm(PS[5][0:n_, 256:512], C["onesf"][0:n_, 0:n_], LA[0:n_, :], True, True, [LA.name, "c_onesf"], ["ps5"])
            for p_ in range(2):
                mm(PS[3][:, 128 + p_:129 + p_], LA[0:n_, p_ * 128:(p_ + 1) * 128], C["onesf"][0:n_, 0:1], True, True,
                   [LA.name, "c_onesf"], ["ps3"])
            act(EG[0:n_, :], GACC[0:n_, :], AF.Exp, ["gacc"], [EG.name], scale=-1.0 / 16)
            tt(GACC[0:n_, :], PS[5][0:n_, 256:512], GACC[0:n_, :], ALU.add, [], ["ps5", "gacc"])
            act(ETOT[:, :], PS[3][:, 128:130], AF.Exp, [], ["ps3", ETOT.name], scale=-1.0 / 16)
            tt(TOTA[:, :], PS[3][:, 128:130], TOTA[:, :], ALU.add, [], ["ps3", "tota"])
            tt(QHL[0:n_, :], QTL[0:n_, :], EG[0:n_, :], ALU.mult, [QTL.name, EG.name], [QHL.name])
            for p_ in range(2):
                tr(psb(3)[:, 512 + p_ * 128:512 + p_ * 128 + n_], QHL[0:n_, p_ * 128:(p_ + 1) * 128],
                   C["identb"][0:n_, 0:n_], [QHL.name, "c_identb"], ["ps3"])
            acopy(QH[:, :, i * 128:(i + 1) * 128], psb(3)[:, 512:768].rearrange("p (a t) -> p a t", a=2), [], ["ps3", f"QH{i}"])

        def gla_back(i, b):
            n_ = 128
            QTL, KTL, VBF, QTT, KTT, AM, ETOT = QTLs[b], KTLs[b], VBFs[b], QTTs[b], KTTs[b], AMs_[b], ETOTs[b]
            for (src, c0) in ((QTL, 0), (KTL, 256)):
                for p_ in range(2):
                    tr(psb(6)[:, c0 + p_ * 128:c0 + p_ * 128 + n_], src[0:n_, p_ * 128:(p_ + 1) * 128],
                       C["identb"][0:n_, 0:n_], [src.name, "c_identb"], ["ps6"])
            v6 = psb(6)
            acopy(QTT[:, :, :], v6[:, 0:256].rearrange("p (a t) -> p a t", a=2), [], ["ps6", QTT.name])
            vcopy(KTT[:, :, :], v6[:, 256:512].rearrange("p (a t) -> p a t", a=2), [], ["ps6", KTT.name])
            for h in range(4):
                p_, e_ = h // 2, h % 2
                rs = slice(e_ * 64, (e_ + 1) * 64)
                if e_ == 0:
                    mm(PS[7][:, p_ * 128:(p_ + 1) * 128], KTT[rs, p_, :], QTT[rs, p_, :], True, True, [KTT.name, QTT.name], ["ps7"])
                else:
                    mm(PS[6][:, 256 + p_ * 128:256 + (p_ + 1) * 128], KTT[rs, p_, :], QTT[rs, p_, :], True, True, [KTT.name, QTT.name], ["ps6"])
            ucb = C["ucs"][:, :].unsqueeze(1).broadcast_to([128, 2, 128])
            tt(AM[:, 0::2, :], PS[7][:, 0:256].rearrange("p (h t) -> p h t", h=2), ucb, ALU.mult, ["c_ucs"], ["ps7", AM.name + "0"])
            tt(AM[:, 1::2, :], PS[6][:, 256:512].rearrange("p (h t) -> p h t", h=2), ucb, ALU.mult, ["c_ucs"], ["ps6", AM.name + "1"])
            for h in range(4):
                p_, e_ = h // 2, h % 2
                rs = slice(e_ * 64, (e_ + 1) * 64)
                mm(PS[7][:, h * 128:(h + 1) * 128], AM[:, h, :], VBF[:, h * 128:(h + 1) * 128], True, False, [AM.name + str(e_), VBF.name], ["ps7"])
                mm(PS[7][:, h * 128:(h + 1) * 128], QTT[rs, p_, :], S_B[rs, p_, :], False, True, [QTT.name, "s_b"], ["ps7"])
            acopy(OB[:, i, :], PS[7][:, :], [], ["ps7", f"ob{i}"])
            for p_ in range(2):
                mm(PS[6][:, p_ * 256:(p_ + 1) * 256], KTL[:, p_ * 128:(p_ + 1) * 128], VBF[:, p_ * 256:(p_ + 1) * 256], True, True,
                   [KTL.name, VBF.name], ["ps6"])
            tt(S_F[:, :, :], S_F[:, :, :], ETOT[:, :].unsqueeze(2).broadcast_to([128, 2, 128]), ALU.mult, ["s_f", ETOT.name], ["s_f"])
            for p_ in range(2):
                for e_ in range(2):
                    rs = slice(e_ * 64, (e_ + 1) * 64)
                    stt(S_F[rs, p_, :], PS[6][rs, p_ * 256 + e_ * 128:p_ * 256 + (e_ + 1) * 128], ETOT[rs, p_:p_ + 1], S_F[rs, p_, :],
                        ALU.mult, ALU.add, ["s_f", ETOT.name], ["ps6", "s_f"])
            vcopy(S_B[:, :, :], S_F[:, :, :], ["s_f"], ["s_b"])

        prev_back = None
        for i in range(NT + 1):
            front = None
            if i < NT:
                P.begin_rec()
                gla_front(i, 128, i % 2)
                front = P.end_rec()
            P.play(front, prev_back)
            prev_back = None
            if i < NT:
                P.begin_rec()
                gla_back(i, i % 2)
                prev_back = P.end_rec()
        if dbg and dbg.startswith("m1b") and dbg != "m1bx":
            break

        if "sgla" not in skip:
            QTL, KTL, VBF, QTT, KTT, LA = QTLs[0], KTLs[0], VBFs[0], QTTs[0], KTTs[0], LAs[0]
            S0F = [ph.alias("s0f0", [128, 2, 128], F32, EGs[1]), ph.alias("s0f1", [128, 2, 128], F32, EBs[1])]
            S0K = [[EGs[1].name], [EBs[1].name]]
            S0B = [ph.alloc(f"s0b{b}", [128, 2, 128], BF16) for b in range(2)]
            ETS = ph.alloc("ets", [128, 2, 4], F32)
            QTM = [ph.alloc(f"qtm{b}", [128, 2, 16], BF16) for b in range(2)]
            AMS = ph.alias("ams", [128, 4, 16], BF16, AMs_[1])
            KTM = [ph.alias("ktm0", [16, 256], BF16, AMs_[1], 256), ph.alias("ktm1", [16, 256], BF16, QHLs[1])]
            KTK = [[AMs_[1].name + "0", AMs_[1].name + "1"], [QHLs[1].name]]
            i = NT
            n_ = NS
            P.op("dve", lambda e: e.memset(AMS[:, :, :], 0.0), [], ["ams0", "ams1", AMs_[1].name + "0", AMs_[1].name + "1"])
            gla_front(i, n_, 0, sample=True)
            for p_ in range(2):
                mm(PS[3][:, 128 + p_ * 4:128 + (p_ + 1) * 4], LA[0:n_, p_ * 128:(p_ + 1) * 128], C["seqsel"][0:n_, 0:4], True, True,
                   [LA.name, "c_seqsel"], ["ps3"])
            act(ETS[:, :, :], PS[3][:, 128:136].rearrange("p (a s) -> p a s", a=2), AF.Exp, [], ["ps3", "ets"], scale=-1.0 / 16)
            for (src, c0) in ((QTL, 0), (KTL, 256)):
                for p_ in range(2):
                    tr(psb(6)[:, c0 + p_ * 128:c0 + p_ * 128 + n_], src[0:n_, p_ * 128:(p_ + 1) * 128],
                       C["identb"][0:n_, 0:n_], [src.name, "c_identb"], ["ps6"])
            v6 = psb(6)
            acopy(QTT[:, :, 0:n_], v6[:, 0:256].rearrange("p (a t) -> p a t", a=2)[:, :, 0:n_], [], ["ps6", QTT.name])
            vcopy(KTT[:, :, 0:n_], v6[:, 256:512].rearrange("p (a t) -> p a t", a=2)[:, :, 0:n_], [], ["ps6", KTT.name])
            for h in range(4):
                p_, e_ = h // 2, h % 2
                rs = slice(e_ * 64, (e_ + 1) * 64)
                ab_ = 7 if e_ == 0 else 0
                mm(PS[ab_][0:n_, p_ * 16:(p_ + 1) * 16], KTT[rs, p_, 0:n_], QTT[rs, p_, 0:n_], True, True, [KTT.name, QTT.name], [f"ps{ab_}"])
            for e_ in range(2):
                ab_ = 7 if e_ == 0 else 0
                tt(AMS[0:n_, e_::2, :], PS[ab_][0:n_, 0:32].rearrange("p (h t) -> p h t", h=2),
                   C["ucs_s"][:, :].unsqueeze(1).broadcast_to([16, 2, 16]), ALU.mult, ["c_ucs_s"], [f"ps{ab_}", f"ams{e_}"])
            P.op("dve", lambda e: e.memset(PS[1][0:16, :], 0.0), [], ["ps1"])
            P.op("dve", lambda e: e.memset(PS[3][0:16, :], 0.0), [], ["ps3"])
            for h in range(4):
                e_ = h % 2
                mm(PS[1][0:n_, h * 128:(h + 1) * 128], AMS[:, h, :], VBF[:, h * 128:(h + 1) * 128], False, False, [f"ams{e_}", VBF.name], ["ps1"])
            for j in range(4):
                b = j % 2
                s0f, s0b, qtm, ktm = S0F[b], S0B[b], QTM[b], KTM[b]
                for e_ in range(2):
                    P.dma("sp", s0f[e_ * 64:(e_ + 1) * 64, :, :], sg_in[l, j].rearrange("(a e) k v -> e k a v", e=2)[e_], [],
                          [s0f.name] + S0K[b], sem=f"s0ld{b}")
                vcopy(s0b[:, :, :], s0f[:, :, :], [s0f.name], [s0b.name])
                tt(qtm[:, :, :], QTT[:, :, 0:n_], C["seqcol"][:, j * 16:(j + 1) * 16].unsqueeze(1).broadcast_to([128, 2, 16]), ALU.mult,
                   [QTT.name, "c_seqcol"], [qtm.name])
                ts(ktm[0:n_, :], KTL[0:n_, :], C["seqsel"][0:n_, j:j + 1], None, ALU.mult, ALU.bypass, [KTL.name, "c_seqsel"], [ktm.name] + KTK[b])
                for e_ in range(2):
                    rs = slice(e_ * 64, (e_ + 1) * 64)
                    ob_ = 1 if e_ == 0 else 3
                    for p_ in range(2):
                        h = 2 * p_ + e_
                        mm(PS[ob_][0:n_, h * 128:(h + 1) * 128], qtm[rs, p_, :], s0b[rs, p_, :], False, False, [qtm.name, s0b.name], [f"ps{ob_}"])
                for p_ in range(2):
                    mm(PS[2][:, p_ * 256:(p_ + 1) * 256], ktm[0:n_, p_ * 128:(p_ + 1) * 128], VBF[0:n_, p_ * 256:(p_ + 1) * 256], True, True,
                       [ktm.name, VBF.name], ["ps2"])
                tt(s0f[:, :, :], s0f[:, :, :], ETS[:, :, j:j + 1].broadcast_to([128, 2, 128]), ALU.mult, [s0f.name, "ets", s0b.name], [s0f.name])
                for p_ in range(2):
                    for e_ in range(2):
                        rs = slice(e_ * 64, (e_ + 1) * 64)
                        stt(s0f[rs, p_, :], PS[2][rs, p_ * 256 + e_ * 128:p_ * 256 + (e_ + 1) * 128], ETS[rs, p_, j:j + 1], s0f[rs, p_, :],
                            ALU.mult, ALU.add, [s0f.name, "ets"], ["ps2", s0f.name])
                for e_ in range(2):
                    P.dma("sp", sso[l, j].rearrange("(a e) k v -> e k a v", e=2)[e_], s0f[e_ * 64:(e_ + 1) * 64, :, :], [s0f.name], [],
                          sem=f"st_ss{b}")
            acopy(OB[0:n_, i, :], PS[1][0:n_, :], [], ["ps1", f"ob{i}"])
            obv = OB[0:n_, i, :].rearrange("p (h d) -> p h d", h=4)
            tt(obv[:, 1::2, :], PS[3][0:n_, :].rearrange("p (h d) -> p h d", h=4)[:, 1::2, :], obv[:, 1::2, :], ALU.add, [f"ob{i}"], ["ps3", f"ob{i}"])

        P.dma("sp", s_src.ap(), S_F[:, :, :].rearrange("p a v -> p (a v)"), ["s_f"], ["s_src"], sem="sx")
        P.collective([s_src.ap()], [s_all.ap()], GROUPS, ["s_src"], ["s_all"])
        P.dma("sp", SA_F[:, :, :].rearrange("p a v -> p (a v)"), s_all.ap()[0:128, :], ["s_all"], ["sa_f"], sem="sx2")
        ts(SA_F[:, :, :], SA_F[:, :, :], C["flag"][:, 0:1], None, ALU.mult, ALU.bypass, ["sa_f", "c_flag"], ["sa_f"])
        vcopy(SA_B[:, :, :], SA_F[:, :, :], ["sa_f"], ["sa_b"])
        ETOT = ETOTs[0]
        act(ETOT[:, :], TOTA[:, :], AF.Exp, ["tota"], [ETOT.name], scale=-1.0 / 16)
        tt(SFIN[:, :, :], SA_F[:, :, :], ETOT[:, :].unsqueeze(2).broadcast_to([128, 2, 128]), ALU.mult, ["sa_f", ETOT.name], ["sfin"])
        tt(SFIN[:, :, :], SFIN[:, :, :], S_F[:, :, :], ALU.add, ["sfin", "s_f"], ["sfin"])
        for e_ in range(2):
            P.dma("sp", spo[l].rearrange("(a e) k v -> e k a v", e=2)[e_], SFIN[e_ * 64:(e_ + 1) * 64, :, :], ["sfin"], [], sem="st_s")

        if dbg == "m1bx":
            break
        P.barrier()
        ph.reset(keep2_off)
        WO = ph.alloc("WO", [128, 8, D], BF16)
        GB = ph.alloc("GB", [128, D], F32)
        WR = ph.alloc("WR", [128, 8, 512], BF16)
        HB3 = ph.alloc("hb3", [128, D], BF16)
        HT3 = ph.alloc("hT3", [128, 8, 128], BF16)
        ER = ph.alloc("er3", [128, 512], F32)
        P.dma("sp", WR[:, :, :].rearrange("p k c -> p (k c)"), wr_bf[l].ap(), [f"wr_bf{l}"], ["WR"], sem="win")
        SQ = ph.alloc("sq", [128, 512], F32)
        T5 = ph.alloc("t5", [128, 512], F32)
        OMB = ph.alloc("omb", [128, 512], BF16)
        OBT = ph.alloc("obt", [128, 4, 128], BF16)
        OAT = ph.alloc("oat", [128, 4, NS], BF16)
        TMP = ph.alloc("tmp", [128, D], F32)
        JK = ph.alloc("jk", [128, 512], BF16)
        SS4 = ph.alloc("ss4", [128, 8], F32)
        R4 = ph.alloc("r4", [128, 8], F32)
        SSM = ph.alloc("ssm", [128, 4], F32)
        RM = ph.alloc("rm", [128, 4], F32)
        P.dma("sp", WO[:, :, :].rearrange("p k c -> p (k c)"), wo_bf[l].ap(), [f"wo_bf{l}"], ["WO"], sem="wout")
        P.dma("sp", GB[:], g_mix_post[l].partition_broadcast(128), [], ["GB"], sem="lp")
        OBTs = [OBT, ph.alloc("obt1", [128, 4, 128], BF16)]

        def m3_front(i):
            n_ = tile_np(i)
            tcols = slice(i * 128, (i + 1) * 128)
            OBT = OBTs[i % 2]
            if i < NT:
                for h in range(4):
                    p_, e_ = h // 2, h % 2
                    rs = slice(e_ * 64, (e_ + 1) * 64)
                    cb = 0 if e_ == 0 else 4
                    mm(PS[cb][0:n_, p_ * 128:(p_ + 1) * 128], QH[rs, p_, tcols], SA_B[rs, p_, :], True, True, [f"QH{i}", "sa_b"], [f"ps{cb}"])
                obv = OB[0:n_, i, :].rearrange("p (h d) -> p h d", h=4)
                for e_ in range(2):
                    cb = 0 if e_ == 0 else 4
                    tt(obv[:, e_::2, :], PS[cb][0:n_, 0:256].rearrange("p (h d) -> p h d", h=2), obv[:, e_::2, :], ALU.add,
                       [f"ob{i}"], [f"ps{cb}", f"ob{i}"])
            else:
                for j in range(4):
                    tr(psb(0)[:, j * 128:j * 128 + n_], OMS[0:n_, j * 128:(j + 1) * 128], C["identb"][0:n_, 0:n_], ["oms_a", "c_identb"], ["ps0"])
                acopy(OAT[:, :, 0:n_], psb(0)[:, 0:512].rearrange("p (j t) -> p j t", j=4)[:, :, 0:n_], [], ["ps0", "oat"])
            norm_and_transpose(i, HB3, HT3, 6)
            for k in range(8):
                mm(PS[7][0:n_, :], HT3[:, k, 0:n_], WR[:, k, :], k == 0, k == 7, [HT3.name, "WR"], ["ps7"])
            act(ER[0:n_, :], PS[7][0:n_, :], AF.Exp, [], ["ps7", "er"], scale=-1.0)
            ts(ER[0:n_, :], ER[0:n_, :], 1.0, None, ALU.add, ALU.bypass, ["er"], ["er"])
            tt(SQ[0:n_, :], OB[0:n_, i, :], OB[0:n_, i, :], ALU.mult, [f"ob{i}"], ["sq"], eng="pool")
            P.op("dve", lambda e, n_=n_: e.reciprocal(out=ER[0:n_, :], in_=ER[0:n_, :]), ["er"], ["er"])
            P.op("dve", lambda e, n_=n_: e.tensor_reduce(out=SS4[0:n_, 0:4], in_=SQ[0:n_, :].rearrange("p (h d) -> p h d", h=4),
                                                         axis=AX.X, op=ALU.add), ["sq"], ["ss4"])
            act(SS4[0:n_, 4:8], SS4[0:n_, 0:4], AF.Ln, ["ss4", "epst"], ["ss4b"], scale=1.0 / 128, bias=EPST[0:n_, 0:1])
            act(R4[0:n_, 0:4], SS4[0:n_, 4:8], AF.Exp, ["ss4b"], ["r4"], scale=-0.5)
            for h in range(4):
                stt(T5[0:n_, h * 128:(h + 1) * 128], OB[0:n_, i, h * 128:(h + 1) * 128], R4[0:n_, h:h + 1], GGLA[0:n_, :],
                    ALU.mult, ALU.mult, [f"ob{i}", "r4", "ggla"], ["t5"])
            tt(ER[0:n_, :], PS[7][0:n_, :], ER[0:n_, :], ALU.mult, ["er"], ["ps7", "er"])
            tt(OMB[0:n_, :], T5[0:n_, :], ER[0:n_, :], ALU.mult, ["t5", "er"], ["omb"], eng="pool")
            for j in range(4):
                tr(psb(1)[:, j * 128:j * 128 + n_], OMB[0:n_, j * 128:(j + 1) * 128], C["identb"][0:n_, 0:n_], ["omb", "c_identb"], ["ps1"])
            acopy(OBT[:, :, 0:n_], psb(1)[:, 0:512].rearrange("p (j t) -> p j t", j=4)[:, :, 0:n_], [], ["ps1", OBT.name])

        def m3_back(i):
            n_ = tile_np(i)
            tcols = slice(i * 128, (i + 1) * 128)
            OBT = OBTs[i % 2]
            for c_ in range(2):
                mb = 2 + c_
                for j in range(4):
                    if i < NT:
                        mm(PS[mb][0:n_, :], OT[:, j, tcols], WO[:, j, c_ * 512:(c_ + 1) * 512], j == 0, False,
                           [f"OT{h_}" for h_ in (2 * j, 2 * j + 1)] + ["WO"], [f"ps{mb}"])
                    else:
                        mm(PS[mb][0:n_, :], OAT[:, j, 0:n_], WO[:, j, c_ * 512:(c_ + 1) * 512], j == 0, False, ["oat", "WO"], [f"ps{mb}"])
                for j in range(4):
                    mm(PS[mb][0:n_, :], OBT[:, j, 0:n_], WO[:, 4 + j, c_ * 512:(c_ + 1) * 512], False, j == 3, [OBT.name, "WO"], [f"ps{mb}"])
                act(JK[0:n_, :], PS[mb][0:n_, :], AF.Square, [], [f"ps{mb}", "jk", f"ssm{c_}"], accum_out=SSM[0:n_, c_:c_ + 1])
            tt(SSM[0:n_, 2:3], SSM[0:n_, 0:1], SSM[0:n_, 1:2], ALU.add, ["ssm0", "ssm1"], ["ssm2"])
            act(SSM[0:n_, 3:4], SSM[0:n_, 2:3], AF.Ln, ["ssm2", "epst"], ["ssm3"], scale=1.0 / D, bias=EPST[0:n_, 0:1])
            act(RM[0:n_, 0:1], SSM[0:n_, 3:4], AF.Exp, ["ssm3"], ["rm"], scale=-0.5)
            for c_ in range(2):
                stt(TMP[0:n_, c_ * 512:(c_ + 1) * 512], PS[2 + c_][0:n_, :], RM[0:n_, 0:1], GB[0:n_, c_ * 512:(c_ + 1) * 512],
                    ALU.mult, ALU.mult, ["rm", "GB"], [f"ps{2 + c_}", "tmp"])
            tt(X[0:n_, i, :], X[0:n_, i, :], TMP[0:n_, :], ALU.add, [f"x{i}", "tmp"], [f"x{i}"], eng="pool")

        m3_tiles = list(range(NT + (0 if "sgla" in skip else 1)))
        prev_back = None
        for i in m3_tiles + [None]:
            front = None
            if i is not None:
                P.begin_rec()
                m3_front(i)
                front = P.end_rec()
            P.play(front, prev_back)
            prev_back = None
            if i is not None:
                P.begin_rec()
                m3_back(i)
                prev_back = P.end_rec()
        if dbg == "m3":
            P.dma("sp", dbg_x2[:, :], X[0:NS, NT, :], [f"x{NT}"], [], sem="dbg")
            for i in range(NT):
                P.dma("sp", dbg_x[i * 128:(i + 1) * 128, :], X[:, i, :], [f"x{i}"], [], sem="dbg")
            break

        P.barrier()
        ph.reset()
        WDR = ph.alloc("WDR", [128, NBLK, D], BF16)
        YT = ph.alloc("YT", [128, NBLK, 528], BF16)
        HTG = ph.alloc("HTG", [128, 8, 530], BF16)
        HTS = ph.alloc("HTS", [128, 8, NS], BF16)
        WU = [ph.alloc(f"WU{b}", [128, 8, 256], BF16) for b in range(2)]
        UB = [[ph.alloc(f"ub{a}{b}", [128, 514], F32) for b in range(2)] for a in range(2)]
        CBF = [[ph.alloc(f"cb{a}{b}", [128, 512], F32) for b in range(2)] for a in range(2)]
        HBF = ph.alloc("hbf", [128, D], BF16)
        TMPF = ph.alloc("tmpf", [128, D], F32)
        GBF = ph.alloc("GBF", [128, D], F32)
        JKF = ph.alloc("jkf", [128, D], BF16)
        ULAST = ph.alloc("ulast", [128, 2, 2 * NBLK], F32)
        ULS = ph.alloc("uls", [128, 8, 2 * NBLK], F32)
        UBS = [ph.alloc(f"ubs{a}", [128, 4, 6], F32) for a in range(2)]
        CBS = [ph.alloc(f"cbs{a}", [128, 4, 4], F32) for a in range(2)]
        CAR = ph.alloc("car", [128, 8, 2], BF16)
        CARN = ph.alloc("carn", [128, 8, 2], BF16)
        LNF = ph.alloc("lnf", [128, NT + 1], F32)
        SSF = ph.alloc("ssf", [128, 4], F32)
        RMF = ph.alloc("rmf", [128, 4], F32)
        ULT = ph.alloc("ult", [128, 128], F32)

        P.dma("sp", WDR[:, :, :].rearrange("p b c -> p (b c)"), wd_bf[l].ap(), [f"wd_bf{l}"], ["WDR"], sem="wdn")
        P.dma("sp", GA[:], g_ffn_pre[l].partition_broadcast(128), [], ["GA"], sem="lp")
        P.dma("sp", GBF[:], g_ffn_post[l].partition_broadcast(128), [], ["GBF"], sem="lp")
        P.op("dve", lambda e: e.memset(SSQ[:, NT:NT + 1], 1.0), [], [f"ssq{NT}"])
        for i in range(NT + 1):
            n_ = tile_np(i)
            stt(JKF[0:n_, :], X[0:n_, i, :], 1.0, X[0:n_, i, :], ALU.mult, ALU.mult, [f"x{i}", f"ssq{i}"], ["jkf", f"ssq{i}"],
                accum_out=SSQ[0:n_, i:i + 1])
        allss = [f"ssq{i}" for i in range(NT + 1)]
        act(LNF[:, :], SSQ[:, :], AF.Ln, allss + ["epst"], ["lnf"], scale=1.0 / D, bias=EPST[:, 0:1])
        act(RSTD[:, :], LNF[:, :], AF.Exp, ["lnf"], ["rstd"], scale=-0.5)

        def ffn_norm_T(i, dst, c0, dkey):
            n_ = tile_np(i)
            stt(HBF[0:n_, :], X[0:n_, i, :], RSTD[0:n_, i:i + 1], GA[0:n_, :], ALU.mult, ALU.mult, [f"x{i}", "rstd", "GA"], ["hbf"])
            pv = psb(0)
            for k in range(8):
                tr(pv[:, k * 128:k * 128 + n_], HBF[0:n_, k * 128:(k + 1) * 128], C["identb"][0:n_, 0:n_], ["hbf", "c_identb"], ["ps0"])
            acopy(dst[:, :, c0:c0 + n_], pv.rearrange("p (k t) -> p k t", k=8)[:, :, 0:n_], [], ["ps0", dkey])

        ffn_norm_T(NT - 1, HTG, 2 + 384, "htg_x")
        P.dma("sp", h_src.ap().rearrange("p (k c) -> p k c", k=8), HTG[:, :, 2 + 510:2 + 512], ["htg_x"], ["h_src"], sem="hx")
        P.collective([h_src.ap()], [h_all.ap()], GROUPS, ["h_src"], ["h_all"])
        P.dma("sp", CAR[:, :, :], h_all.ap()[0:128, :].rearrange("p (k c) -> p k c", k=8), ["h_all"], ["car"], sem="hx2")
        ts(CAR[:, :, :], CAR[:, :, :], C["flag"][:, 0:1], None, ALU.mult, ALU.bypass, ["car", "c_flag"], ["car"])

        NG = 4
        wcnt = 0

        def group_norms(g):
            vcopy(HTG[:, :, 0:2], (CAR if g == 0 else CARN)[:, :, :], ["car" if g == 0 else "carn", "htg_x"], ["htg_c"] + [f"htg{t}" for t in range(4)])
            for t in range(4):
                ffn_norm_T(4 * g + t, HTG, 2 + t * 128, f"htg{t}")
            if g + 1 < NG:
                vcopy(CARN[:, :, :], HTG[:, :, 512:514], ["htg3"], ["carn"])
            if g == NG - 1:
                ffn_norm_T(NT, HTS, 0, "hts")

        group_norms(0)
        for g in range(NG):
            has_s = (g == NG - 1)
            pend_f2 = [None]
            for blk in range(NBLK):
                wu = WU[wcnt % 2]; wcnt += 1
                P.dma("sp", wu[:, :, :].rearrange("p k c -> p (k c)"), wup_bf[l].ap()[blk * 128:(blk + 1) * 128, :], [f"wupbf{l}"], [wu.name],
                      sem=wu.name)
                sl = blk % 2
                hkeys = ["htg_c"] + [f"htg{t}" for t in range(4)]
                for a in range(2):
                    pb = 1 + 2 * sl + a
                    for k in range(8):
                        mm(PS[pb][:, :], wu[:, k, a * 128:(a + 1) * 128], HTG[:, k, 2:514], k == 0, k == 7, [wu.name] + hkeys, [f"ps{pb}"])
                    cc = (2 * sl + a) * 2
                    if g == 0:
                        for k in range(8):
                            mm(PS[5][:, cc:cc + 2], wu[:, k, a * 128:(a + 1) * 128], HTG[:, k, 0:2], k == 0, k == 7, [wu.name] + hkeys, ["ps5"])
                    if has_s:
                        sc0 = 8 + (2 * sl + a) * 16
                        for k in range(8):
                            mm(PS[5][:, sc0:sc0 + NS], wu[:, k, a * 128:(a + 1) * 128], HTS[:, k, :], k == 0, k == 7, [wu.name, "hts"], ["ps5"])
                for a in range(2):
                    pb = 1 + 2 * sl + a
                    acopy(UB[a][sl][:, 2:514], PS[pb][:, :], [], [f"ps{pb}", UB[a][sl].name])
                for a in range(2):
                    cc = (2 * sl + a) * 2
                    ub = UB[a][sl]
                    if g == 0:
                        acopy(ub[:, 0:2], PS[5][:, cc:cc + 2], [], ["ps5", ub.name])
                    else:
                        P.op("pool", lambda e, ub=ub, a=a, blk=blk: e.tensor_copy(out=ub[:, 0:2], in_=ULAST[:, :, a * NBLK + blk]),
                             [f"ulast{a}_{blk}"], [ub.name])
                for a in range(2):
                    pb = 1 + 2 * sl + a
                    cw = CONVP[:, a * NBLK + blk, :]
                    act(CBF[a][sl][:, :], PS[pb][:, :], AF.Identity, ["convp"], [f"ps{pb}", CBF[a][sl].name], scale=cw[:, 2:3], bias=cw[:, 3:4])
                for a in range(2):
                    ub, cb = UB[a][sl], CBF[a][sl]
                    cw = CONVP[:, a * NBLK + blk, :]
                    stt(cb[:, :], ub[:, 1:513], cw[:, 1:2], cb[:, :], ALU.mult, ALU.add, [ub.name, "convp", cb.name], [cb.name])
                for a in range(2):
                    ub, cb = UB[a][sl], CBF[a][sl]
                    cw = CONVP[:, a * NBLK + blk, :]
                    stt(cb[:, :], ub[:, 0:512], cw[:, 0:1], cb[:, :], ALU.mult, ALU.add, [ub.name, "convp", cb.name], [cb.name])
                for a in range(2):
                    ub = UB[a][sl]
                    P.op("pool", lambda e, ub=ub, a=a, blk=blk: e.tensor_copy(out=ULAST[:, :, a * NBLK + blk], in_=ub[:, 512:514]),
                         [ub.name], [f"ulast{a}_{blk}"])
                def f2(sl=sl, blk=blk):
                    act(CBF[0][sl][:, :], CBF[0][sl][:, :], AF.Gelu_apprx_tanh, [CBF[0][sl].name], [CBF[0][sl].name])
                    tt(YT[:, blk, 0:512], CBF[0][sl][:, :], CBF[1][sl][:, :], ALU.mult, [CBF[0][sl].name, CBF[1][sl].name], [f"yt{blk}"])
                if pend_f2[0] is not None:
                    pend_f2[0]()
                pend_f2[0] = f2
                if has_s:
                    for a in range(2):
                        sc0 = 8 + (2 * sl + a) * 16
                        ubs, cbs = UBS[a], CBS[a]
                        cw = CONVP[:, a * NBLK + blk, :]
                        vcopy(ubs[:, :, 0:2], CPRE[:, a * NBLK + blk, :].rearrange("p (s r) -> p s r", s=4), ["cpre"], [ubs.name])
                        vcopy(ubs[:, :, 2:6], PS[5][:, sc0:sc0 + NS].rearrange("p (s t) -> p s t", s=4), [], ["ps5", ubs.name])
                        ts(cbs[:, :, :], ubs[:, :, 2:6], cw[:, 2:3], cw[:, 3:4], ALU.mult, ALU.add, [ubs.name, "convp"], [cbs.name])
                        stt(cbs[:, :, :], ubs[:, :, 1:5], cw[:, 1:2], cbs[:, :, :], ALU.mult, ALU.add, [ubs.name, "convp", cbs.name], [cbs.name])
                        stt(cbs[:, :, :], ubs[:, :, 0:4], cw[:, 0:1], cbs[:, :, :], ALU.mult, ALU.add, [ubs.name, "convp", cbs.name], [cbs.name])
                        vcopy(ULS[:, :, a * NBLK + blk].rearrange("p (s r) -> p s r", s=4), ubs[:, :, 4:6], [ubs.name], ["uls"])
                    act(CBS[0][:, :, :], CBS[0][:, :, :], AF.Gelu_apprx_tanh, [CBS[0].name], [CBS[0].name])
                    tt(YT[:, blk, 512:528].rearrange("p (s t) -> p s t", s=4), CBS[0][:, :, :], CBS[1][:, :, :], ALU.mult,
                       [CBS[0].name, CBS[1].name], [f"yts{blk}"])
            pend_f2[0]()
            P.begin_rec()
            tiles = [(4 * g + t, slice(t * 128, (t + 1) * 128)) for t in range(4)] + ([(NT, slice(512, 528))] if has_s else [])
            for ti_, (i, tc) in enumerate(tiles):
                n_ = tile_np(i)
                ykeys = [f"yt{b_}" for b_ in range(NBLK)] if i < NT else [f"yts{b_}" for b_ in range(NBLK)]
                fbs = (6, 7) if ti_ % 2 == 0 else (1, 2)
                for c_ in range(2):
                    fb = fbs[c_]
                    for blk in range(NBLK):
                        mm(PS[fb][0:n_, :], YT[:, blk, tc], WDR[:, blk, c_ * 512:(c_ + 1) * 512], blk == 0, blk == NBLK - 1,
                           ykeys + ["WDR"], [f"ps{fb}"])
                    acopy(TMPF[0:n_, c_ * 512:(c_ + 1) * 512], PS[fb][0:n_, :], [], [f"ps{fb}", f"tmpf{c_}"])
                stt(JKF[0:n_, :], TMPF[0:n_, :], 1.0, TMPF[0:n_, :], ALU.mult, ALU.mult, ["tmpf0", "tmpf1"], ["jkf", "ssf2"],
                    accum_out=SSF[0:n_, 2:3])
                act(SSF[0:n_, 3:4], SSF[0:n_, 2:3], AF.Ln, ["ssf2", "epst"], ["ssf3"], scale=1.0 / D, bias=EPST[0:n_, 0:1])
                act(RMF[0:n_, 0:1], SSF[0:n_, 3:4], AF.Exp, ["ssf3"], ["rmf"], scale=-0.5)
                stt(TMPF[0:n_, :], TMPF[0:n_, :], RMF[0:n_, 0:1], GBF[0:n_, :], ALU.mult, ALU.mult, ["tmpf0", "tmpf1", "rmf", "GBF"],
                    ["tmpf0", "tmpf1"])
                tt(X[0:n_, i, :], X[0:n_, i, :], TMPF[0:n_, :], ALU.add, [f"x{i}", "tmpf0", "tmpf1"], [f"x{i}"])
                if last:
                    if i < NT:
                        P.dma("sp", yp[i * 128:(i + 1) * 128, :], X[:, i, :], [f"x{i}"], [], sem="st_y")
                    else:
                        P.dma("sp", ys, X[0:NS, NT, :], [f"x{i}"], [], sem="st_y")
            p2ops = P.end_rec()
            nops = None
            if g + 1 < NG:
                P.begin_rec()
                group_norms(g + 1)
                nops = P.end_rec()
            P.play(p2ops, nops)
        tr(PS[1][0:88, 0:128], ULAST[:, :, :].rearrange("p r b -> p (r b)"), C["identf"][:, :],
           [f"ulast{a_}_{b_}" for a_ in range(2) for b_ in range(NBLK)] + ["c_identf"], ["ps1"])
        acopy(ULT[0:88, :], PS[1][0:88, 0:128], [], ["ps1", "ult"])
        P.dma("sp", cpo[l].rearrange("r (b p) -> (r b) p", p=128), ULT[0:88, :], ["ult"], [], sem="st_c")
        ulsf = ULS[:, :, :].rearrange("p q b -> p (q b)")
        for q_ in range(3):
            rows = 128 if q_ < 2 else 96
            tr(PS[2][0:rows, 0:128], ulsf[:, q_ * 128:q_ * 128 + rows], C["identf"][:, :], ["uls", "c_identf"], ["ps2"])
            acopy(ULT[0:rows, :], PS[2][0:rows, 0:128], ["ult"], ["ps2", "ult"])
            P.dma("sp", cso[l].rearrange("s r (b p) -> (s r b) p", p=128)[q_ * 128:q_ * 128 + rows, :], ULT[0:rows, :], ["ult"], [], sem="st_c")
        if dbg == "ffn":
            for i in range(NT):
                P.dma("sp", dbg_x[i * 128:(i + 1) * 128, :], X[:, i, :], [f"x{i}"], [], sem="dbg")
            break

    P.barrier(final=True)
    if P.unknown:
        print("WARNING: keys read but never written:", sorted(P.unknown))
    P.emit()
    return nc


_NC_CACHE = {}


def _in_maps(inputs):
    f = lambda a: np.ascontiguousarray(np.asarray(a, dtype=np.float32))
    maps = []
    shared = {k: f(inputs[k]) for k in ("g_mix_pre", "g_mix_post", "g_ffn_pre", "g_ffn_post", "w_in", "w_gate2", "b_gate",
                                         "g_gla", "w_out", "w_up", "conv_w", "conv_b", "w_down")}
    x_prompt, x_sample = f(inputs["x_prompt"]), f(inputs["x_sample"])
    ckw, cvw = f(inputs["cache_k_win"]), f(inputs["cache_v_win"])
    sgl, sfc = f(inputs["state_gla"]), f(inputs["state_ffn_conv"])
    consts = [_consts(0), _consts(1)]
    for c in range(8):
        s, half = c // 2, c % 2
        m = dict(shared)
        m["xp"] = np.ascontiguousarray(x_prompt[s, half * TL:(half + 1) * TL, :])
        m["xs"] = np.ascontiguousarray(x_sample[4 * c:4 * c + 4].reshape(NS, D))
        m["ck"] = np.ascontiguousarray(ckw[:, 4 * c:4 * c + 4].reshape(DEPTH, 4, 2048, 512))
        m["cv"] = np.ascontiguousarray(cvw[:, 4 * c:4 * c + 4].reshape(DEPTH, 4, 2048, 512))
        m["sg"] = np.ascontiguousarray(sgl[:, 4 * c:4 * c + 4])
        m["sc"] = np.ascontiguousarray(sfc[:, 4 * c:4 * c + 4])
        for n, v in consts[half].items():
            m["c_" + n] = v
        maps.append(m)
    return maps


def kernel(**inputs):
    if "nc" not in _NC_CACHE:
        _NC_CACHE["nc"] = build_program()
    nc = _NC_CACHE["nc"]
    res = run_bass_kernel_spmd(nc, _in_maps(inputs), core_ids=list(range(8)))
    R = res.results
    B, T = 4, 4096
    y_prompt = np.zeros((B, T, D), np.float32)
    y_sample = np.zeros((32, 4, D), np.float32)
    nk = np.zeros((DEPTH, B, TL, 8, 64), np.float32); nv = np.zeros_like(nk)
    nsp = np.zeros((DEPTH, B, 4, 64, 128), np.float32)
    ncp = np.zeros((DEPTH, B, 2, 2 * DFF), np.float32)
    ksn = np.zeros((DEPTH, 32, 4, 8, 64), np.float32); vsn = np.zeros_like(ksn)
    ssn = np.zeros((DEPTH, 32, 4, 64, 128), np.float32)
    csn = np.zeros((DEPTH, 32, 2, 2 * DFF), np.float32)
    for c in range(8):
        s, half = c // 2, c % 2
        r = R[c]
        y_prompt[s, half * TL:(half + 1) * TL] = r["yp"]
        y_sample[4 * c:4 * c + 4] = r["ys"].reshape(4, 4, D)
        if half == 1:
            nk[:, s] = r["kp"].reshape(DEPTH, TL, 8, 64); nv[:, s] = r["vp"].reshape(DEPTH, TL, 8, 64)
            nsp[:, s] = r["spo"]; ncp[:, s] = r["cpo"]
        ksn[:, 4 * c:4 * c + 4] = r["kso"].reshape(DEPTH, 4, 4, 8, 64)
        vsn[:, 4 * c:4 * c + 4] = r["vso"].reshape(DEPTH, 4, 4, 8, 64)
        ssn[:, 4 * c:4 * c + 4] = r["sso"]; csn[:, 4 * c:4 * c + 4] = r["cso"]
    return (y_prompt, y_sample, nk, nv, nsp, ncp, ksn, vsn, ssn, csn)
```
